# Optimizing a Trainium2 kernel written in Bass

```python
import math
import jax
import jax.numpy as jnp
from jax import lax
import numpy as np

D_MODEL = 1024
BATCH = 8
SEQ = 2048
DEPTH = 1
DEC_BATCH = 128
DEC_SEQ = 1
PAST_LEN = 16384
PAGE_SIZE = 128

H_A = 8
HD_A = 64
D_A = H_A * HD_A
A_PATTERNS = ((128, 1), (512, 4), (2048, 16))
W_A = 2048
A_BLOCK = 128
H_B = 8
KV_B = 2
HD_B = 64
D_B = H_B * HD_B
W_B = 128
H_C = 4
HD_C = 128
D_C = H_C * HD_C
N_MEM = 256
NUM_BUCKETS = 32
MAX_DISTANCE = 2048
EPS = 1e-6
IN_SIZES = (D_A, D_A, D_A, D_A,
            D_B, KV_B * HD_B, KV_B * HD_B, D_B,
            D_C, D_C,
            D_MODEL, D_MODEL, D_MODEL)
IN_WIDTH = sum(IN_SIZES)

kernel_name = "hybrid_dilated_swa_memory_decoder_step"


def rms_norm(x, g):
    xf = x.astype(jnp.float32)
    y = xf * lax.rsqrt(jnp.mean(xf * xf, axis=-1, keepdims=True) + EPS)
    return (y * g.astype(jnp.float32)).astype(x.dtype)


def t5_bucket(dist):
    n = jnp.maximum(dist, 0)
    max_exact = NUM_BUCKETS // 2
    nf = jnp.maximum(n, 1).astype(jnp.float32)
    large = max_exact + (jnp.log(nf / max_exact) / math.log(MAX_DISTANCE / max_exact)
                         * (NUM_BUCKETS - max_exact)).astype(jnp.int32)
    return jnp.where(n < max_exact, n, jnp.minimum(large, NUM_BUCKETS - 1))


def heads(t, n, hd, g=None):
    t = t.reshape(t.shape[:-1] + (n, hd))
    return t if g is None else rms_norm(t, g)


def masked_softmax_stats(logits, mask):
    logits = jnp.where(mask, logits, -jnp.inf)
    m = jnp.max(logits, axis=-1, keepdims=True)
    p = jnp.exp(logits - m)
    s = jnp.sum(p, axis=-1, keepdims=True)
    return p / s, (m + jnp.log(s))[..., 0]


def sink_softmax(logits, mask, sink):
    logits = jnp.where(mask, logits, -jnp.inf)
    sink = sink.astype(jnp.float32)
    m = jnp.maximum(jnp.max(logits, axis=-1, keepdims=True), sink)
    p = jnp.exp(logits - m)
    return p / (jnp.sum(p, axis=-1, keepdims=True) + jnp.exp(sink - m))


def combine_by_denominator(outs, lses):
    wts = jax.nn.softmax(jnp.stack(lses, 0), axis=0)
    return jnp.einsum('gbsh,gbshe->bshe', wts, jnp.stack(outs, 0))


def dilated_attn_prompt(q, k, v, table_a):
    b, s, h, e = q.shape
    scale = HD_A ** -0.5
    outs, lses = [], []
    for w, d in A_PATTERNS:
        nb = w // d
        L = s // d
        c = math.gcd(L, A_BLOCK)
        nblk = L // c

        def strided(t):
            return t.reshape(b, L, d, h, e).transpose(0, 2, 1, 3, 4)

        qs = strided(q).reshape(b, d, nblk, c, h, e)
        pad = ((0, 0), (0, 0), (nb, 0), (0, 0), (0, 0))
        kp = jnp.pad(strided(k), pad)
        vp = jnp.pad(strided(v), pad)
        idx = jnp.arange(nblk)[:, None] * c + jnp.arange(c + nb)[None, :]
        kb = kp[:, :, idx]
        vb = vp[:, :, idx]
        logits = jnp.einsum('bdnqhe,bdnshe->bdnhqs', qs, kb,
                            preferred_element_type=jnp.float32) * scale
        rel = jnp.arange(c)[:, None] + nb - jnp.arange(c + nb)[None, :]
        bias = table_a[t5_bucket(rel * d)].astype(jnp.float32).transpose(2, 0, 1)
        mask = ((rel >= 0) & (rel <= nb))[None] & (idx >= nb)[:, None, :]
        p, lse = masked_softmax_stats(logits + bias, mask[None, None, :, None])
        o = jnp.einsum('bdnhqs,bdnshe->bdnqhe', p, vb.astype(jnp.float32))
        outs.append(o.reshape(b, d, L, h, e).transpose(0, 2, 1, 3, 4).reshape(b, s, h, e))
        lses.append(lse.transpose(0, 1, 2, 4, 3).reshape(b, d, L, h)
                    .transpose(0, 2, 1, 3).reshape(b, s, h))
    return combine_by_denominator(outs, lses)


def dilated_attn_sample(q, k_all, v_all, table_a):
    b, t, h, e = q.shape
    scale = HD_A ** -0.5
    wb = k_all.shape[1] - t
    qidx = wb + jnp.arange(t)
    outs, lses = [], []
    for w, d in A_PATTERNS:
        nb = w // d
        steps = jnp.arange(nb + 1)
        idx = qidx[:, None] - steps[None, :] * d
        safe = jnp.maximum(idx, 0)
        kg = k_all[:, safe]
        vg = v_all[:, safe]
        logits = jnp.einsum('bqhe,bqshe->bhqs', q, kg,
                            preferred_element_type=jnp.float32) * scale
        bias = table_a[t5_bucket(steps * d)].astype(jnp.float32).T[:, None, :]
        p, lse = masked_softmax_stats(logits + bias, (idx >= 0)[None, None])
        outs.append(jnp.einsum('bhqs,bqshe->bqhe', p, vg.astype(jnp.float32)))
        lses.append(lse.transpose(0, 2, 1))
    return combine_by_denominator(outs, lses)


def swa_sink_prompt(q, k, v, table_b, sinks):
    b, s, h, e = q.shape
    g = h // KV_B
    c = W_B
    nblk = s // c
    qb = q.reshape(b, nblk, c, KV_B, g, e)

    def band(t):
        tp = jnp.pad(t, ((0, 0), (c, 0), (0, 0), (0, 0))).reshape(b, nblk + 1, c, KV_B, e)
        return jnp.concatenate([tp[:, :-1], tp[:, 1:]], axis=2)

    kb, vb = band(k), band(v)
    logits = jnp.einsum('bnqhgd,bnshd->bnhgqs', qb, kb,
                        preferred_element_type=jnp.float32) * (HD_B ** -0.5)
    rel = jnp.arange(c)[:, None] + c - jnp.arange(2 * c)[None, :]
    bias = table_b[t5_bucket(rel)].astype(jnp.float32).transpose(2, 0, 1).reshape(KV_B, g, c, 2 * c)
    kpos_ok = (jnp.arange(nblk)[:, None] * c + jnp.arange(2 * c)[None, :]) >= c
    mask = ((rel >= 0) & (rel < W_B))[None] & kpos_ok[:, None, :]
    p = sink_softmax(logits + bias, mask[None, :, None, None], sinks.reshape(KV_B, g)[:, :, None, None])
    o = jnp.einsum('bnhgqs,bnshd->bnqhgd', p, vb.astype(jnp.float32))
    return o.reshape(b, s, h, e)


def swa_sink_sample(q, k_all, v_all, table_b, sinks):
    b, t, h, e = q.shape
    g = h // KV_B
    wb = k_all.shape[1] - t
    rel = (wb + jnp.arange(t))[:, None] - jnp.arange(wb + t)[None, :]
    mask = (rel >= 0) & (rel < W_B)
    qg = q.reshape(b, t, KV_B, g, e)
    logits = jnp.einsum('bqhgd,bshd->bhgqs', qg, k_all,
                        preferred_element_type=jnp.float32) * (HD_B ** -0.5)
    bias = table_b[t5_bucket(rel)].astype(jnp.float32).transpose(2, 0, 1).reshape(KV_B, g, t, wb + t)
    p = sink_softmax(logits + bias, mask, sinks.reshape(KV_B, g)[:, :, None, None])
    o = jnp.einsum('bhgqs,bshd->bqhgd', p, v_all.astype(jnp.float32))
    return o.reshape(b, t, h, e)


def mem_attn(q, mk, mv):
    logits = jnp.einsum('bqhd,bmhd->bhqm', q, mk,
                        preferred_element_type=jnp.float32) * (HD_C ** -0.5)
    p = jax.nn.softmax(logits, axis=-1)
    return jnp.einsum('bhqm,bmhd->bqhd', p, mv.astype(jnp.float32))


def mem_kv(mem, mem_ln_g, w_mem_kv, gk_c):
    mk, mv = jnp.split(rms_norm(mem, mem_ln_g) @ w_mem_kv, 2, axis=-1)
    return heads(mk, H_C, HD_C, gk_c), heads(mv, H_C, HD_C)


def layer_in(x, ln_g, w_in, gq_a, gk_a, gq_b, gk_b, gq_c):
    h = rms_norm(x, ln_g)
    pts = np.cumsum(IN_SIZES)[:-1].tolist()
    qa, ka, va, za, qb, kb, vb, zb, qc, zc, ga, gb, gc = jnp.split(h @ w_in, pts, axis=-1)
    qa = heads(qa, H_A, HD_A, gq_a)
    ka = heads(ka, H_A, HD_A, gk_a)
    va = heads(va, H_A, HD_A)
    qb = heads(qb, H_B, HD_B, gq_b)
    kb = heads(kb, KV_B, HD_B, gk_b)
    vb = heads(vb, KV_B, HD_B)
    qc = heads(qc, H_C, HD_C, gq_c)
    return qa, ka, va, za, qb, kb, vb, zb, qc, zc, ga, gb, gc


def layer_out(x, oa, ob, oc, za, zb, zc, ga, gb, gc, w_br_a, w_br_b, w_br_c, w_out):
    def branch(o, z, w):
        y = o.reshape(z.shape).astype(jnp.float32) * jax.nn.silu(z.astype(jnp.float32))
        return y.astype(x.dtype) @ w
    m = (jax.nn.sigmoid(ga) * branch(oa, za, w_br_a)
         + jax.nn.sigmoid(gb) * branch(ob, zb, w_br_b)
         + jax.nn.sigmoid(gc) * branch(oc, zc, w_br_c))
    return x + m @ w_out


def setup_inputs(seed: int = 0) -> dict:
    key = jax.random.key(seed)
    ks = jax.random.split(key, 32)
    f32 = jnp.float32

    def nrm(k, shape, scale=1.0):
        return jax.random.normal(k, shape, f32) * scale

    wa_buf = min(W_A, PAST_LEN)
    wb_buf = min(W_B, PAST_LEN)
    return {
        "x_prompt": nrm(ks[0], (BATCH, SEQ, D_MODEL)),
        "x_sample": nrm(ks[1], (DEC_BATCH, DEC_SEQ, D_MODEL)),
        "mem_prompt": nrm(ks[2], (BATCH, N_MEM, D_MODEL)),
        "cache_a_k": nrm(ks[3], (DEPTH, DEC_BATCH, wa_buf, H_A, HD_A)),
        "cache_a_v": nrm(ks[4], (DEPTH, DEC_BATCH, wa_buf, H_A, HD_A)),
        "cache_b_k": nrm(ks[5], (DEPTH, DEC_BATCH, wb_buf, KV_B, HD_B)),
        "cache_b_v": nrm(ks[6], (DEPTH, DEC_BATCH, wb_buf, KV_B, HD_B)),
        "cache_mem_k": nrm(ks[7], (DEPTH, DEC_BATCH, N_MEM, H_C, HD_C)),
        "cache_mem_v": nrm(ks[8], (DEPTH, DEC_BATCH, N_MEM, H_C, HD_C)),
        "rel_bias": nrm(ks[9], (NUM_BUCKETS, H_A + H_B), 0.5),
        "ln_g": 1.0 + nrm(ks[10], (DEPTH, D_MODEL), 0.05),
        "w_in": nrm(ks[11], (DEPTH, D_MODEL, IN_WIDTH), D_MODEL ** -0.5),
        "gq_a": 1.0 + nrm(ks[12], (DEPTH, HD_A), 0.05),
        "gk_a": 1.0 + nrm(ks[13], (DEPTH, HD_A), 0.05),
        "gq_b": 1.0 + nrm(ks[14], (DEPTH, HD_B), 0.05),
        "gk_b": 1.0 + nrm(ks[15], (DEPTH, HD_B), 0.05),
        "gq_c": 1.0 + nrm(ks[16], (DEPTH, HD_C), 0.05),
        "gk_c": 1.0 + nrm(ks[17], (DEPTH, HD_C), 0.05),
        "sinks_b": nrm(ks[18], (DEPTH, H_B), 0.5),
        "mem_ln_g": 1.0 + nrm(ks[19], (DEPTH, D_MODEL), 0.05),
        "w_mem_kv": nrm(ks[20], (DEPTH, D_MODEL, 2 * D_C), D_MODEL ** -0.5),
        "w_br_a": nrm(ks[21], (DEPTH, D_A, D_MODEL), D_A ** -0.5),
        "w_br_b": nrm(ks[22], (DEPTH, D_B, D_MODEL), D_B ** -0.5),
        "w_br_c": nrm(ks[23], (DEPTH, D_C, D_MODEL), D_C ** -0.5),
        "w_out": nrm(ks[24], (DEPTH, D_MODEL, D_MODEL), D_MODEL ** -0.5),
    }


def reference(x_prompt, x_sample, mem_prompt, cache_a_k, cache_a_v, cache_b_k, cache_b_v,
              cache_mem_k, cache_mem_v, rel_bias, ln_g, w_in, gq_a, gk_a, gq_b, gk_b, gq_c, gk_c,
              sinks_b, mem_ln_g, w_mem_kv, w_br_a, w_br_b, w_br_c, w_out):
    table_a = rel_bias[:, :H_A]
    table_b = rel_bias[:, H_A:]
    xp, xs = x_prompt, x_sample
    pa_k, pa_v, pb_k, pb_v, pm_k, pm_v = [], [], [], [], [], []
    sa_k, sa_v, sb_k, sb_v = [], [], [], []
    for l in range(DEPTH):
        qa, ka, va, za, qb, kb, vb, zb, qc, zc, ga, gb, gc = layer_in(
            xp, ln_g[l], w_in[l], gq_a[l], gk_a[l], gq_b[l], gk_b[l], gq_c[l])
        mk, mv = mem_kv(mem_prompt, mem_ln_g[l], w_mem_kv[l], gk_c[l])
        oa = dilated_attn_prompt(qa, ka, va, table_a)
        ob = swa_sink_prompt(qb, kb, vb, table_b, sinks_b[l])
        oc = mem_attn(qc, mk, mv)
        xp = layer_out(xp, oa, ob, oc, za, zb, zc, ga, gb, gc,
                       w_br_a[l], w_br_b[l], w_br_c[l], w_out[l])
        pa_k.append(ka[:, -W_A:])
        pa_v.append(va[:, -W_A:])
        pb_k.append(kb[:, -W_B:])
        pb_v.append(vb[:, -W_B:])
        pm_k.append(mk)
        pm_v.append(mv)
        qa, ka, va, za, qb, kb, vb, zb, qc, zc, ga, gb, gc = layer_in(
            xs, ln_g[l], w_in[l], gq_a[l], gk_a[l], gq_b[l], gk_b[l], gq_c[l])
        ka_all = jnp.concatenate([cache_a_k[l], ka], axis=1)
        va_all = jnp.concatenate([cache_a_v[l], va], axis=1)
        kb_all = jnp.concatenate([cache_b_k[l], kb], axis=1)
        vb_all = jnp.concatenate([cache_b_v[l], vb], axis=1)
        oa = dilated_attn_sample(qa, ka_all, va_all, table_a)
        ob = swa_sink_sample(qb, kb_all, vb_all, table_b, sinks_b[l])
        oc = mem_attn(qc, cache_mem_k[l], cache_mem_v[l])
        xs = layer_out(xs, oa, ob, oc, za, zb, zc, ga, gb, gc,
                       w_br_a[l], w_br_b[l], w_br_c[l], w_out[l])
        sa_k.append(ka)
        sa_v.append(va)
        sb_k.append(kb)
        sb_v.append(vb)
    return (xp, xs,
            jnp.stack(pa_k), jnp.stack(pa_v), jnp.stack(pb_k), jnp.stack(pb_v),
            jnp.stack(pm_k), jnp.stack(pm_v),
            jnp.stack(sa_k), jnp.stack(sa_v), jnp.stack(sb_k), jnp.stack(sb_v))
```

```python
import math
from contextlib import ExitStack

import numpy as np
import concourse.bass as bass
import concourse.mybir as mybir
from concourse.bass_utils import run_bass_kernel_spmd

F32 = mybir.dt.float32
BF16 = mybir.dt.bfloat16
AF = mybir.ActivationFunctionType
ALU = mybir.AluOpType
AX = mybir.AxisListType

NT = 2064
EPS = 1e-6
IN_W = 7424
C_QA, C_KA, C_VA, C_ZA = 0, 512, 1024, 1536
C_QB, C_KB, C_VB, C_ZB = 2048, 2560, 2688, 2816
C_QC, C_ZC = 3328, 3840
C_GA = 4352
A_D = (1, 4, 16)


class Sched:
    ENGS = ("pe", "act", "dve", "pool", "sp")

    def __init__(self, nc, stack, dma_slots=None):
        self.nc = nc
        self.ops = []
        self.last_writer = {}
        self.readers = {}
        self.dma_slots = dma_slots or {"sp": 8, "pool": 6, "act": 4}
        self.sems = {}
        for e in ("pe", "act", "dve", "pool"):
            self.sems[e] = stack.enter_context(nc.semaphore("s_" + e))
        for q, n in self.dma_slots.items():
            for i in range(n):
                self.sems[(q, i)] = stack.enter_context(nc.semaphore(f"d_{q}{i}"))
        self.dma_count = {q: 0 for q in self.dma_slots}
        self.slot_last = {}
        self.emitted = 0
        self.sig_count = {k: 0 for k in self.sems}
        self.clock = {e: {} for e in self.ENGS}
        self.since_barrier_dma = []
        self.last_op_eng = {}
        self.excl = set()
        self.last_access = {}

    def add(self, eng, fn, reads=(), writes=(), dma=False, extra_deps=()):
        idx = len(self.ops)
        deps = set(extra_deps)
        for r in set(reads) | set(writes):
            if r in self.excl:
                d = self.last_access.get(r)
                if d is not None:
                    od = self.ops[d]
                    if dma or od["dma"] or od["eng"] != eng:
                        deps.add(d)
                self.last_access[r] = idx
        for r in reads:
            w = self.last_writer.get(r)
            if w is not None:
                deps.add(w)
        for r in writes:
            for d in [self.last_writer.get(r)] + self.readers.get(r, []):
                if d is None:
                    continue
                od = self.ops[d]
                if (not dma) and (not od["dma"]) and od["eng"] == eng:
                    continue
                deps.add(d)
        op = dict(eng=eng, fn=fn, dma=dma, deps=deps, sig=None, need_sig=dma, idx=idx)
        if dma:
            n = self.dma_count[eng]
            self.dma_count[eng] = n + 1
            slot = (eng, n % self.dma_slots[eng])
            prev = self.slot_last.get(slot)
            if prev is not None:
                deps.add(prev)
            self.slot_last[slot] = idx
            op["slot"] = slot
            self.since_barrier_dma.append(idx)
        for d in deps:
            self.ops[d]["need_sig"] = True
        for r in writes:
            self.last_writer[r] = idx
            self.readers[r] = []
        for r in reads:
            if r not in writes:
                self.readers.setdefault(r, []).append(idx)
        self.ops.append(op)
        if fn is not None and not dma:
            self.last_op_eng[eng] = idx
        return idx

    def barrier(self):
        deps = set(self.since_barrier_dma) | set(self.last_op_eng.values())
        self.since_barrier_dma = []
        for e in self.ENGS:
            self.add(e, None, extra_deps=deps)

    def emit(self):
        nc = self.nc
        lo, hi = self.emitted, len(self.ops)
        self.emitted = hi
        for op in self.ops[lo:hi]:
            if op["fn"] is None:
                continue
            if op["dma"]:
                k = op["slot"]
                self.sig_count[k] += 16
                op["sig"] = (k, self.sig_count[k])
            elif op["need_sig"]:
                k = op["eng"]
                self.sig_count[k] += 1
                op["sig"] = (k, self.sig_count[k])
        by_eng = {e: [] for e in self.ENGS}
        for op in self.ops[lo:hi]:
            e = op["eng"]
            clk = self.clock[e]
            wm = {}
            for d in sorted(op["deps"]):
                od = self.ops[d]
                if od["sig"] is None:
                    continue
                k, v = od["sig"]
                if clk.get(k, 0) < v:
                    wm[k] = max(wm.get(k, 0), v)
                    for kk, vv in od["clk"].items():
                        if clk.get(kk, 0) < vv:
                            clk[kk] = vv
            op["waits"] = list(wm.items())
            oc = dict(clk)
            if op["sig"] is not None:
                k, v = op["sig"]
                oc[k] = v
            op["clk"] = oc
            by_eng[e].append(op)

        def run(engobj, lst):
            for op in lst:
                for k, v in op["waits"]:
                    engobj.wait_ge(self.sems[k], v)
                if op["fn"] is None:
                    continue
                ins = op["fn"](engobj)
                if op["sig"] is not None:
                    ins.then_inc(self.sems[op["sig"][0]], 16 if op["dma"] else 1)

        with nc.Block() as block:
            if by_eng["pe"]:
                @block.tensor
                def _(t):
                    run(t, by_eng["pe"])
            if by_eng["act"]:
                @block.scalar
                def _(t):
                    run(t, by_eng["act"])
            if by_eng["dve"]:
                @block.vector
                def _(t):
                    run(t, by_eng["dve"])
            if by_eng["pool"]:
                @block.gpsimd
                def _(t):
                    run(t, by_eng["pool"])
            if by_eng["sp"]:
                @block.sync
                def _(t):
                    run(t, by_eng["sp"])
        for op in self.ops[lo:hi]:
            op["fn"] = None if op["fn"] is None else True

    def phase_end(self):
        self.barrier()
        self.emit()


class T:
    def __init__(self, h, shape):
        self.h = h
        self.shape = shape
        self.F = int(np.prod(shape[1:]))

    def ap(self, p0, n, col, dims):
        return bass.AP(self.h, p0 * self.F + col, [[self.F, n]] + [list(d) for d in dims])


class Rot:
    def __init__(self, ts, name):
        self.ts = ts
        self.name = name
        self.i = 0

    def next(self):
        j = self.i % len(self.ts)
        self.i += 1
        return self.ts[j], (self.name, j)


class RotL:
    def __init__(self, pairs):
        self.pairs = pairs
        self.i = 0

    def next(self):
        p = self.pairs[self.i % len(self.pairs)]
        self.i += 1
        return p


class Builder:
    def bank_rot(self, names):
        prs = []
        for nm in names:
            r = self.Q[nm]
            if isinstance(r, Rot):
                prs += [(t, (r.name, i)) for i, t in enumerate(r.ts)]
            else:
                prs.append((r, nm))
        return RotL(prs)

    def __init__(self):
        self.nc = bass.Bass("TRN2", target_bir_lowering=False)
        self.uid = 0
        self.only = None
        self.gate_stack = None

    def sb(self, st, shape, dt, name=None):
        self.uid += 1
        return T(st.enter_context(self.nc.sbuf_tensor(f"{name or 't'}_{self.uid}", shape, dt)), shape)

    def ps(self, st, shape, dt, name=None):
        self.uid += 1
        return T(st.enter_context(self.nc.psum_tensor(f"{name or 'p'}_{self.uid}", shape, dt)), shape)

    def rot(self, st, n, shape, dt, name):
        return Rot([self.sb(st, shape, dt, name) for _ in range(n)], name + str(self.uid))

    def dma(self, out, in_, r=(), w=(), q="sp", slow=False):
        if slow:
            self.S.add(q, lambda e, o=out, i=in_: e.dma_start(out=o, in_=i, allow_slow_non_contiguous=True), r, w, dma=True)
        else:
            self.S.add(q, lambda e, o=out, i=in_: e.dma_start(out=o, in_=i), r, w, dma=True)

    def mm(self, out, lhsT, rhs, start, stop, r, w, sgc=False, tp=None):
        if tp is None:
            self.S.add("pe", lambda e, o=out, l=lhsT, rr=rhs, a=start, b=stop, s=sgc:
                       e.matmul(o, l, rr, start=a, stop=b, skip_group_check=s), r, w)
        else:
            self.S.add("pe", lambda e, o=out, l=lhsT, rr=rhs, a=start, b=stop, s=sgc, t=tp:
                       e.matmul(o, l, rr, start=a, stop=b, skip_group_check=s, tile_position=t), r, w)

    def tr(self, out, in_, ident, r, w):
        self.S.add("pe", lambda e, o=out, i=in_, d=ident: e.transpose(out=o, in_=i, identity=d), r, w)

    def act(self, out, in_, func, r, w, scale=1.0, bias=None):
        if bias is None:
            self.S.add("act", lambda e, o=out, i=in_, f=func, s=scale: e.activation(out=o, in_=i, func=f, scale=s), r, w)
        else:
            self.S.add("act", lambda e, o=out, i=in_, f=func, s=scale, b=bias:
                       e.activation(out=o, in_=i, func=f, scale=s, bias=b), r, w)

    def acopy(self, out, in_, r, w):
        self.S.add("act", lambda e, o=out, i=in_: e.copy(out=o, in_=i), r, w)

    def amul(self, out, in_, mul, r, w):
        self.S.add("act", lambda e, o=out, i=in_, m=mul: e.mul(out=o, in_=i, mul=m), r, w)

    def tt(self, eng, out, in0, in1, op, r, w):
        self.S.add(eng, lambda e, o=out, a=in0, b=in1, p=op: e.tensor_tensor(out=o, in0=a, in1=b, op=p), r, w)

    def stt(self, eng, out, in0, scalar, in1, op0, op1, r, w):
        self.S.add(eng, lambda e, o=out, a=in0, s=scalar, b=in1, p0=op0, p1=op1:
                   e.scalar_tensor_tensor(out=o, in0=a, scalar=s, in1=b, op0=p0, op1=p1), r, w)

    def tsmul(self, eng, out, in0, scalar, r, w):
        self.S.add(eng, lambda e, o=out, a=in0, s=scalar: e.tensor_scalar_mul(out=o, in0=a, scalar1=s), r, w)

    def tcopy(self, eng, out, in_, r, w):
        self.S.add(eng, lambda e, o=out, i=in_: e.tensor_copy(out=o, in_=i), r, w)

    def recip(self, out, in_, r, w):
        self.S.add("dve", lambda e, o=out, i=in_: e.reciprocal(out=o, in_=i), r, w)

    def rsum(self, out, in_, r, w):
        self.S.add("dve", lambda e, o=out, i=in_: e.reduce_sum(out=o, in_=i, axis=AX.X), r, w)

    def memset(self, eng, ap, val, w):
        self.S.add(eng, lambda e, a=ap, v=val: e.memset(a, v), (), w)

    def asel(self, out, pattern, base, cm, r, w):
        self.S.add("pool", lambda e, o=out, p=pattern, b=base, c=cm: e.affine_select(
            out=o, in_=o, pattern=p, compare_op=ALU.not_equal, fill=1.0, base=b, channel_multiplier=c), r, w)

    def build(self):
        nc = self.nc

        def din(name, shape):
            return nc.dram_tensor(name, shape, F32, kind="ExternalInput")

        def dout(name, shape):
            return nc.dram_tensor(name, shape, F32, kind="ExternalOutput")

        D = self.D = {}
        for name, shape in [("x", [2048, 1024]), ("xs", [16, 1024]), ("mem", [256, 1024]),
                            ("cak", [16 * 2048, 512]), ("cav", [16 * 2048, 512]),
                            ("cbk", [16 * 128, 128]), ("cbv", [16 * 128, 128]),
                            ("cmk", [16 * 256, 512]), ("cmv", [16 * 256, 512]),
                            ("rel_bias", [32, 16]), ("ln_g", [1, 1024]), ("w_in", [1024, IN_W]),
                            ("gq_a", [1, 64]), ("gk_a", [1, 64]), ("gq_b", [1, 64]), ("gk_b", [1, 64]),
                            ("gq_c", [1, 128]), ("gk_c", [1, 128]), ("sinks", [1, 8]),
                            ("mem_ln_g", [1, 1024]), ("w_mem_kv", [1024, 1024]),
                            ("w_br_a", [512, 1024]), ("w_br_b", [512, 1024]), ("w_br_c", [512, 1024]),
                            ("w_out", [1024, 1024]), ("ohp", [32, 4 * 384]), ("ohs", [32, 4 * 128]), ("masks", [128, 1024])]:
            D[name] = din(name, shape)
        for name, shape in [("y", [2048, 1024]), ("ys", [16, 1024]), ("pak", [2048, 512]), ("pav", [2048, 512]),
                            ("pbk", [128, 128]), ("pbv", [128, 128]), ("pmk", [256, 512]), ("pmv", [256, 512]),
                            ("sak", [16, 512]), ("sav", [16, 512]), ("sbk", [16, 128]), ("sbv", [16, 128])]:
            D[name] = dout(name, shape)
        D["gS"] = nc.dram_tensor("gS", [64, 384], F32)
        D["qS"] = nc.dram_tensor("qS", [48, 512], F32)
        D["rD"] = nc.dram_tensor("rD", [20, 2048], F32)

        with ExitStack() as st:
            self.S = Sched(nc, st)
            P = self.P = {}
            P["hT"] = self.sb(st, [128, 8, NT], BF16, "hT")
            P["yT"] = self.sb(st, [128, 12, NT], BF16, "yT")
            P["identb"] = self.sb(st, [128, 128], BF16, "identb")
            P["Jf"] = self.sb(st, [128, 128], F32, "Jf")
            P["onesb"] = self.sb(st, [128, 128], BF16, "onesb")
            P["eps"] = self.sb(st, [128, 1], F32, "eps")
            P["kscA"] = self.sb(st, [128, 1], F32, "kscA")
            P["kscB"] = self.sb(st, [128, 1], F32, "kscB")
            P["kscC"] = self.sb(st, [128, 1], F32, "kscC")
            P["gkA"] = self.sb(st, [128, 64], F32, "gkA")
            P["gkB"] = self.sb(st, [128, 64], F32, "gkB")
            P["gkC"] = self.sb(st, [128, 128], F32, "gkC")
            P["gqsA"] = self.sb(st, [16, 64], F32, "gqsA")
            P["gqsB"] = self.sb(st, [16, 64], F32, "gqsB")
            P["gqsC"] = self.sb(st, [16, 128], F32, "gqsC")
            P["E"] = self.sb(st, [32, 16], F32, "E")
            P["e0"] = self.sb(st, [16, 16], F32, "e0")
            P["esink"] = self.sb(st, [16, 8], F32, "esink")
            P["sinkL"] = self.sb(st, [1, 8 * 128], F32, "sinkL")
            P["EBs"] = self.sb(st, [128, 4, 16], F32, "EBs")
            P["OHB"] = self.sb(st, [128, 16 * 16], BF16, "OHB")
            P["qsA"] = self.sb(st, [16, 512], F32, "qsA")
            P["qsB"] = self.sb(st, [16, 512], F32, "qsB")
            P["qsC"] = self.sb(st, [16, 512], F32, "qsC")
            P["ksA"] = self.sb(st, [16, 512], F32, "ksA")
            P["vsA"] = self.sb(st, [16, 512], F32, "vsA")
            P["ksB"] = self.sb(st, [16, 128], F32, "ksB")
            P["vsB"] = self.sb(st, [16, 128], F32, "vsB")
            Q = self.Q = {}
            Q["PJ"] = Rot([self.ps(st, [128, 512], F32, "PJ") for _ in range(2)], "PJ")
            Q["TRb"] = self.ps(st, [128, 1024], BF16, "TRb")
            Q["S"] = Rot([self.ps(st, [128, 512], F32, "S") for _ in range(2)], "S")
            Q["ACC"] = Rot([self.ps(st, [128, 512], F32, "ACC") for _ in range(2)], "ACC")
            Q["O3"] = self.ps(st, [128, 512], F32, "O3")
            for nm in ("PJ", "S", "ACC"):
                for i in range(2):
                    self.S.excl.add((Q[nm].name, i))
            self.S.excl |= {"O3", "TRb"}

            phases = [("setup", self.phase_setup), ("norm", self.phase_norm), ("A0", lambda: self.phase_A(0)),
                      ("A1", lambda: self.phase_A(1)), ("B", self.phase_B), ("C", self.phase_C),
                      ("tail", self.phase_tail_all)]
            for nm, fn in phases:
                if self.only is not None and nm not in self.only:
                    continue
                fn()
        return nc

    def phase_setup(self):
        P, D, Q = self.P, self.D, self.Q
        with ExitStack() as st:
            tmpf = self.sb(st, [128, 128], F32, "tmpf")
            rb = self.sb(st, [32, 16], F32, "rb")
            ohp = self.sb(st, [32, 4 * 384], F32, "ohp")
            ohs = self.sb(st, [32, 4 * 128], F32, "ohs")
            gv = self.rot(st, 2, [16, 384], F32, "gv")
            self.memset("pool", tmpf.h[:], 0.0, ["tmpf"])
            self.asel(tmpf.h[:], [[-1, 128]], 0, 1, ["tmpf"], ["tmpf"])
            self.tcopy("pool", P["identb"].h[:], tmpf.h[:], ["tmpf"], ["identb"])
            self.memset("pool", P["Jf"].h[:], 0.0, ["Jf"])
            self.asel(P["Jf"].h[:], [[1, 128]], -127, 1, ["Jf"], ["Jf"])
            self.memset("pool", P["onesb"].h[:], 1.0, ["onesb"])
            self.memset("pool", P["eps"].h[:], EPS, ["eps"])
            ohbf = self.sb(st, [128, 256], F32, "ohbf")
            self.memset("pool", ohbf.h[:], 0.0, ["ohbf"])
            self.asel(ohbf.h[:].rearrange("p (b m) -> p b m", m=16), [[1, 16], [-1, 16]], 0, 0, ["ohbf"], ["ohbf"])
            self.tcopy("pool", P["OHB"].h[:], ohbf.h[:], ["ohbf"], ["OHB"])
            self.S.barrier()
            for nm, src, n in (("kscA", "gq_a", 64), ("kscB", "gq_b", 64)):
                for half in range(2):
                    self.dma(P[nm].h[64 * half:64 * half + 64, :], bass.AP(D[src], 0, [[1, 64], [1, 1]]), (), [nm])
                self.tsmul("dve", P[nm].h[:], P[nm].h[:], 0.125, [nm], [nm])
            self.dma(P["kscC"].h[:, :], bass.AP(D["gq_c"], 0, [[1, 128], [1, 1]]), (), ["kscC"])
            self.tsmul("dve", P["kscC"].h[:], P["kscC"].h[:], 128 ** -0.5, ["kscC"], ["kscC"])
            for nm, src, n, np_ in (("gkA", "gk_a", 64, 128), ("gkB", "gk_b", 64, 128), ("gkC", "gk_c", 128, 128),
                                    ("gqsA", "gq_a", 64, 16), ("gqsB", "gq_b", 64, 16), ("gqsC", "gq_c", 128, 16)):
                self.dma(P[nm].h[:], bass.AP(D[src], 0, [[0, np_], [1, n]]), (), [nm])
            self.tsmul("dve", P["gqsA"].h[:], P["gqsA"].h[:], 0.125, ["gqsA"], ["gqsA"])
            self.tsmul("dve", P["gqsB"].h[:], P["gqsB"].h[:], 0.125, ["gqsB"], ["gqsB"])
            self.tsmul("dve", P["gqsC"].h[:], P["gqsC"].h[:], 128 ** -0.5, ["gqsC"], ["gqsC"])
            self.dma(rb.h[:], D["rel_bias"].ap(), (), ["rb"])
            self.act(P["E"].h[:], rb.h[:], AF.Exp, ["rb"], ["E"])
            self.dma(P["e0"].h[:], bass.AP(D["rel_bias"], 0, [[0, 16], [1, 16]]), (), ["e0"])
            self.act(P["e0"].h[:], P["e0"].h[:], AF.Exp, ["e0"], ["e0"])
            self.dma(P["esink"].h[:], bass.AP(D["sinks"], 0, [[0, 16], [1, 8]]), (), ["esink"])
            self.act(P["esink"].h[:], P["esink"].h[:], AF.Exp, ["esink"], ["esink"])
            self.memset("dve", P["sinkL"].h[:], 0.0, ["sinkL"])
            for h in range(8):
                lo = 64 if h % 2 == 0 else 0
                self.tcopy("dve", P["sinkL"].ap(0, 1, h * 128 + lo, [[1, 64]]), P["esink"].ap(0, 1, h, [[0, 64]]),
                           ["esink", "sinkL"], ["sinkL"])
            self.dma(ohp.h[:], D["ohp"].ap(), (), ["ohp"])
            self.dma(ohs.h[:], D["ohs"].ap(), (), ["ohs"])
            for p in range(4):
                pj, pjt = Q["PJ"].next()
                self.mm(pj.h[0:16, 0:384], P["E"].h[:, :], ohp.h[:, p * 384:(p + 1) * 384], True, True, ["E", "ohp"], [pjt])
                g, gt = gv.next()
                self.acopy(g.h[:], pj.h[0:16, 0:384], [pjt], [gt])
                self.dma(D["gS"].ap()[p * 16:(p + 1) * 16, :], g.h[:], [gt], ["gS"])
                pj, pjt = Q["PJ"].next()
                self.mm(pj.h[:, 0:16], ohs.h[:, p * 128:(p + 1) * 128], P["E"].h[:, :], True, True, ["E", "ohs"], [pjt])
                self.acopy(P["EBs"].h[:, p, :], pj.h[:, 0:16], [pjt], ["EBs"])
            self.S.phase_end()

    def norm_tiles(self, st, jobs, deep=True):
        P, D, Q = self.P, self.D, self.Q
        xt = self.rot(st, 6 if deep else 2, [128, 1024], F32, "xt")
        sq = self.rot(st, 2 if deep else 1, [128, 1024], F32, "sq")
        xb = self.rot(st, 3 if deep else 2, [128, 1024], BF16, "xb")
        s1 = self.rot(st, 4, [128, 4], F32, "s1")

        def stages(job):
            (src, r0, dstT, dst, g, gtok, col0, n) = job
            x_, xtok = xt.next()
            q_, qtok = sq.next()
            b_, btok = xb.next()
            s_, stok = s1.next()

            def A():
                self.dma(x_.h[0:n, :], D[src].ap()[r0:r0 + n, :], (), [xtok])
                self.act(q_.h[0:n, :], x_.h[0:n, :], AF.Square, [xtok], [qtok])
                self.rsum(s_.h[0:n, 0:1], q_.h[0:n, :], [qtok], [stok])

            def B():
                self.act(s_.h[0:n, 1:2], s_.h[0:n, 0:1], AF.Sqrt, [stok], [stok], scale=1.0 / 1024, bias=P["eps"].h[0:n, :])
                self.recip(s_.h[0:n, 2:3], s_.h[0:n, 1:2], [stok], [stok])
                self.stt("dve", b_.h[0:n, :], x_.h[0:n, :], s_.h[0:n, 2:3], g.h[0:n, :], ALU.mult, ALU.mult,
                         [xtok, stok, gtok], [btok])

            def C():
                for kc in range(8):
                    self.tr(Q["TRb"].h[:, kc * 128:kc * 128 + n], b_.h[0:n, kc * 128:(kc + 1) * 128],
                            P["identb"].h[0:n, 0:n], [btok, "identb"], ["TRb"])
                self.acopy(dstT.h[:, :, col0:col0 + n],
                           Q["TRb"].h[:, :].rearrange("p (k t) -> p k t", t=128)[:, :, 0:n], ["TRb"], [(dst, col0)])
            return [A, B, C]

        pipe = []
        for i in range(len(jobs) + 2):
            if i < len(jobs):
                pipe.append(stages(jobs[i]))
            for stg in pipe:
                if stg:
                    stg.pop(0)()
            pipe = [p_ for p_ in pipe if p_]

    def phase_norm(self):
        P, D, Q = self.P, self.D, self.Q
        with ExitStack() as st:
            gln = self.sb(st, [128, 1024], F32, "gln")
            self.dma(gln.h[:], bass.AP(D["ln_g"], 0, [[0, 128], [1, 1024]]), (), ["gln"])
            jobs = [("x", 128 * i, P["hT"], "hT", gln, "gln", 128 * i, 128) for i in range(16)]
            jobs.append(("xs", 0, P["hT"], "hT", gln, "gln", 2048, 16))
            self.norm_tiles(st, jobs)
            self.S.phase_end()

    def load_w(self, W, wtok, src, c0, width, o0, part=0):
        if not hasattr(self, "wparts"):
            self.wparts = {}
        self.wparts.setdefault(wtok, set()).add(part)
        v = self.D[src].ap().rearrange("(kc p) n -> p kc n", p=128)
        for g in range(4):
            self.dma(W.h[:, 2 * g:2 * g + 2, o0:o0 + width], v[:, 2 * g:2 * g + 2, c0:c0 + width], (), [(wtok, part, g)], q="pool")

    def wrd(self, wtok, kc):
        return [(wtok, p, kc // 2) for p in sorted(self.wparts[wtok])]

    def proj_tm(self, pj, pjt, hTname, col0, n, W, wtok, width, tokens=None):
        hT = self.P[hTname]
        for kc in range(8):
            if tokens is None:
                l = hT.h[:, kc, col0:col0 + n]
                rd = [(hTname, col0)] + self.wrd(wtok, kc)
            else:
                start, step = tokens
                l = hT.ap(0, 128, kc * hT.shape[2] + start, [[step, n]])
                rd = [(hTname, 128 * i) for i in range(16)] + self.wrd(wtok, kc)
            self.mm(pj.h[0:n, 0:width], l, W.h[:, kc, 0:width], kc == 0, kc == 7, rd, [pjt])

    def rms_stats_a(self, pj, pjt, n, c0, nh, hd, sq, ss):
        q_, qtok = sq.next()
        s_, stok = ss.next()
        w = nh * hd
        self.act(q_.h[0:n, 0:w], pj.h[0:n, c0:c0 + w], AF.Square, [pjt], [qtok])
        self.rsum(s_.h[0:n, 0:nh], q_.h[0:n, 0:w].rearrange("p (h e) -> p h e", e=hd), [qtok], [stok])
        return s_, stok

    def rms_stats_b(self, s_, stok, n, nh, hd):
        self.act(s_.h[0:n, 8:8 + nh], s_.h[0:n, 0:nh], AF.Sqrt, [stok], [stok], scale=1.0 / hd, bias=self.P["eps"].h[0:n, :])
        self.recip(s_.h[0:n, 16:16 + nh], s_.h[0:n, 8:8 + nh], [stok], [stok])

    def rms_stats(self, pj, pjt, n, c0, nh, hd, sq, ss):
        q_, qtok = sq.next()
        s_, stok = ss.next()
        w = nh * hd
        self.act(q_.h[0:n, 0:w], pj.h[0:n, c0:c0 + w], AF.Square, [pjt], [qtok])
        self.rsum(s_.h[0:n, 0:nh], q_.h[0:n, 0:w].rearrange("p (h e) -> p h e", e=hd), [qtok], [stok])
        self.act(s_.h[0:n, 8:8 + nh], s_.h[0:n, 0:nh], AF.Sqrt, [stok], [stok], scale=1.0 / hd, bias=self.P["eps"].h[0:n, :])
        self.recip(s_.h[0:n, 16:16 + nh], s_.h[0:n, 8:8 + nh], [stok], [stok])
        return s_, stok

    @staticmethod
    def bc_heads(t, n, col, nh, hd):
        return t.ap(0, n, col, [[1, nh], [0, hd]])

    def run_groups(self, groups, Pbuf, Pmbuf, skew=2):
        P = self.P
        Sr = self.bank_rot(["S", "PJ"])
        pendq = []

        def do_pv(item):
            pg, Pm_, pmtok_ = item
            pmtoks = [pmtok_ + (ui,) for ui in range(len(pg["units"]))]
            if pg.get("pre_sink"):
                A_, atok, h = pg["pre_sink"]
                self.mm(A_.h[:, :], P["sinkL"].ap(0, 1, h * 128, [[1, 128]]), P["ones_f"].h[0:1, :], True, False,
                        ["sinkL", "ones_f"], [atok], sgc=True)
            for s_i, t in enumerate(pg["tiles"]):
                self.mm(t["out"], t["v"], Pm_.h[:, 128 * s_i:128 * (s_i + 1)], t["start"], False,
                        pmtoks + t["vrd"], [pg["acctok"]], sgc=True)
            if pg.get("post"):
                pg["post"]()

        for g in groups:
            Sb, stok = Sr.next()
            nt = len(g["tiles"])
            for s_i, t in enumerate(g["tiles"]):
                self.mm(Sb.h[:, 128 * s_i:128 * (s_i + 1)], t["k"], t["q"], True, True, t["rd"], [stok])
            Pt, ptok = Pbuf.next()
            Pm, pmtok = Pmbuf.next()
            self.act(Pt.h[:, 0:128 * nt], Sb.h[:, 0:128 * nt], AF.Exp, [stok], [ptok])
            for ui, (c0, w, eb, ebtok) in enumerate(g["units"]):
                self.mmcnt = getattr(self, "mmcnt", 0) + 1
                self.tt("pool" if self.mmcnt % 4 == 0 else "dve", Pm.h[:, c0:c0 + w], Pt.h[:, c0:c0 + w], eb, ALU.mult,
                        [ptok, ebtok], [pmtok + (ui,)])
            pendq.append((g, Pm, pmtok))
            if len(pendq) > skew:
                do_pv(pendq.pop(0))
        while pendq:
            do_pv(pendq.pop(0))

    def build_EB(self, st, EB, ebname, combos):
        P, D, Q = self.P, self.D, self.Q
        R = self.rot(st, 1, [128, 256], F32, "R")
        for (idx, row) in combos:
            r_, rtok = R.next()
            self.dma(r_.h[:], bass.AP(D["gS"], row * 384, [[1, 128], [1, 256]]), ["gS"], [rtok])
            pj, pjt = Q["PJ"].next()
            self.mm(pj.h[:, 0:256], P["Jf"].h[:, :], r_.h[:, :], True, True, ["Jf", rtok], [pjt])
            self.acopy(EB.h[:, idx, :], pj.h[:, 0:256], [pjt], [(ebname, idx)])

    def phase_A(self, hf):
        P, D, Q = self.P, self.D, self.Q
        with ExitStack() as st:
            Wqk = self.sb(st, [128, 8, 512], BF16, "Wqk")
            Wv = self.sb(st, [128, 8, 256], BF16, "Wv")
            qT = self.sb(st, [128, 2, NT], BF16, "qTA")
            kT = self.sb(st, [128, 2, NT], BF16, "kTA")
            Vst = self.sb(st, [128, 48 * 384], BF16, "Vst")
            EB = self.sb(st, [128, 12, 256], F32, "EBA")
            acc3 = self.rot(st, 1, [128, 2048], F32, "acc3")
            sq = self.rot(st, 2, [128, 512], F32, "sqA")
            ss = self.rot(st, 4, [128, 24], F32, "ssA")
            qb = self.rot(st, 4, [128, 256], BF16, "qbA")
            kn = self.rot(st, 2, [128, 256], F32, "knA")
            ko = self.rot(st, 3, [128, 256], F32, "koA")
            kb = self.rot(st, 3, [128, 256], BF16, "kbA")
            vo = self.rot(st, 2, [128, 256], F32, "voA")
            rsm = self.rot(st, 4, [128, 16], F32, "rsmA")
            dcp = self.rot(st, 1, [128, 512], F32, "dcpA")
            maskD = self.sb(st, [128, 512], F32, "maskD")
            self.dma(maskD.h[:, :], D["masks"].ap()[:, 0:512], (), ["maskD"])

            self.load_w(Wqk, "Wqk", "w_in", C_QA + 256 * hf, 256, 0)
            self.load_w(Wqk, "Wqk", "w_in", C_KA + 256 * hf, 256, 256, part=1)
            self.load_w(Wv, "Wv", "w_in", C_VA + 256 * hf, 256, 0)
            self.memset("pool", Vst.ap(0, 128, 64, [[192, 96], [1, 64]]), 1.0, ["Vones"])

            def vdst(arr, ti):
                return Vst.ap(0, 128, (arr * 16 + ti) * 384, [[192, 2], [128, 2], [1, 64]])

            def vsrc(pj):
                return pj.h[:, 0:256].rearrange("p (a b e) -> p a b e", b=2, e=64)
            self.build_EB(st, EB, "EBA", [(hl * 3 + p, p * 16 + (4 * hf + hl)) for hl in range(4) for p in range(3)])

            import os
            tiles = [(128 * i, 128) for i in range(16)] + [(2048, 16)]
            tiles = tiles[:int(os.environ.get("DBG_TILES", "17"))]
            PJr = self.bank_rot(["PJ", "S", "ACC"])

            def tile_stages(ti, col0, n):
                samp = (ti == 16)
                X = {}

                def SA():
                    pj, pjt = PJr.next()
                    self.proj_tm(pj, pjt, "hT", col0, n, Wqk, "Wqk", 512)
                    X["pj"] = (pj, pjt)
                    X["s"] = self.rms_stats_a(pj, pjt, n, 0, 8, 64, sq, ss)
                    pv, pvt = PJr.next()
                    self.proj_tm(pv, pvt, "hT", col0, n, Wv, "Wv", 256)
                    if samp:
                        self.acopy(P["vsA"].h[0:16, 256 * hf:256 * hf + 256], pv.h[0:16, 0:256], [pvt], [("vsA", hf)])
                        self.dma(D["sav"].ap()[0:16, 256 * hf:256 * hf + 256], P["vsA"].h[0:16, 256 * hf:256 * hf + 256],
                                 [("vsA", hf)], [])
                    else:
                        v_, vtok = vo.next()
                        self.acopy(v_.h[0:n, :], pv.h[0:n, 0:256], [pvt], [vtok])
                        self.dma(D["pav"].ap()[col0:col0 + n, 256 * hf:256 * hf + 256], v_.h[0:n, :], [vtok], [])
                        self.tcopy("dve", vdst(0, ti), vsrc(pv), [pvt], [("V", 0, ti)])

                def SB1():
                    pj, pjt = X["pj"]
                    s_, stok = X["s"]
                    self.rms_stats_b(s_, stok, n, 8, 64)
                    q_, qtok = qb.next()
                    X["q"] = (q_, qtok)
                    self.tt("dve", q_.h[0:n, :].rearrange("p (h e) -> p h e", e=64),
                            pj.h[0:n, 0:256].rearrange("p (h e) -> p h e", e=64),
                            self.bc_heads(s_, n, 16, 4, 64), ALU.mult, [pjt, stok], [qtok])
                    n_, ntok = kn.next()
                    self.tt("dve", n_.h[0:n, :].rearrange("p (h e) -> p h e", e=64),
                            pj.h[0:n, 256:512].rearrange("p (h e) -> p h e", e=64),
                            self.bc_heads(s_, n, 20, 4, 64), ALU.mult, [pjt, stok], [ntok])
                    if samp:
                        self.tt("dve", P["qsA"].h[0:16, 256 * hf:256 * hf + 256].rearrange("p (h e) -> p h e", e=64),
                                pj.h[0:16, 0:256].rearrange("p (h e) -> p h e", e=64),
                                self.bc_heads(s_, 16, 16, 4, 64), ALU.mult, [pjt, stok], [("qsA", hf)])
                        self.tt("pool", P["qsA"].h[0:16, 256 * hf:256 * hf + 256].rearrange("p (h e) -> p h e", e=64),
                                P["qsA"].h[0:16, 256 * hf:256 * hf + 256].rearrange("p (h e) -> p h e", e=64),
                                P["gqsA"].ap(0, 16, 0, [[0, 4], [1, 64]]), ALU.mult, [("qsA", hf), "gqsA"], [("qsA", hf)])
                        otok = ("ksA", hf)
                        oap = P["ksA"].h[0:16, 256 * hf:256 * hf + 256]
                    else:
                        o_, otok = ko.next()
                        oap = o_.h[0:n, :]
                    X["o"] = (oap, otok)
                    self.tt("pool", oap.rearrange("p (h e) -> p h e", e=64), n_.h[0:n, :].rearrange("p (h e) -> p h e", e=64),
                            P["gkA"].ap(0, n, 0, [[0, 4], [1, 64]]), ALU.mult, [ntok, "gkA"], [otok])
                    dst = D["sak"].ap()[0:16, 256 * hf:256 * hf + 256] if samp else D["pak"].ap()[col0:col0 + n, 256 * hf:256 * hf + 256]
                    self.dma(dst, oap, [otok], [])

                def SB2():
                    oap, otok = X["o"]
                    b_, btok = kb.next()
                    X["b"] = (b_, btok)
                    self.acopy(b_.h[0:n, :], oap, [otok], [btok])

                def SC():
                    q_, qtok = X["q"]
                    b_, btok = X["b"]
                    for j in range(2):
                        self.tr(Q["TRb"].h[:, j * 128:j * 128 + n], q_.h[0:n, j * 128:(j + 1) * 128], P["identb"].h[0:n, 0:n],
                                [qtok, "identb"], ["TRb"])
                    for j in range(2):
                        self.tr(Q["TRb"].h[:, (2 + j) * 128:(2 + j) * 128 + n], b_.h[0:n, j * 128:(j + 1) * 128],
                                P["identb"].h[0:n, 0:n], [btok, "identb"], ["TRb"])
                    trv = Q["TRb"].h[:, 0:512].rearrange("p (k t) -> p k t", t=128)
                    self.acopy(qT.h[:, :, col0:col0 + n], trv[:, 0:2, 0:n], ["TRb"], [("qTA", ti)])
                    self.tsmul("dve", kT.h[:, :, col0:col0 + n], trv[:, 2:4, 0:n], P["kscA"].h[:, 0:1], ["TRb", "kscA"], [("kTA", ti)])
                return [SA, SB1, SB2, SC]

            pipe = []

            def step(newtile=None):
                nonlocal pipe
                if newtile is not None:
                    pipe.append(tile_stages(*newtile))
                for stg in pipe:
                    if stg:
                        stg.pop(0)()
                pipe = [p_ for p_ in pipe if p_]
            for ti, (col0, n) in enumerate(tiles):
                step((ti, col0, n))
            defer = []
            for arr in (1, 2):
                for ti in range(int(os.environ.get("DBG_VARR", "16"))):
                    if pipe and ti % 2 == 1:
                        step()
                    if arr == 1:
                        tb, r = ti // 4, ti % 4
                        tokens = (512 * tb + r, 4)
                    else:
                        tokens = (ti, 16)
                    pj, pjt = PJr.next()
                    self.proj_tm(pj, pjt, "hT", 0, 128, Wv, "Wv", 256, tokens=tokens)
                    if ti % 2 == 0:
                        self.acopy(vdst(arr, ti), vsrc(pj), [pjt], [("V", arr, ti)])
                    else:
                        self.tcopy("dve", vdst(arr, ti), vsrc(pj), [pjt], [("V", arr, ti)])

            while pipe:
                step()
            self.S.barrier()
            Pb = RotL([(T(Wqk.h[:, 2 * i:2 * i + 2, :].rearrange("p a b -> p (a b)").bitcast(F32), [128, 512]), ("PbA", i))
                       for i in range(4)])
            Pm = RotL([(T(Wv.h[:, 2 * i:2 * i + 2, :].rearrange("p a b -> p (a b)"), [128, 512]), ("PmA", i)) for i in range(4)])
            allq = [("qTA", i) for i in range(16)]
            allk = [("kTA", i) for i in range(16)]

            def vaug(arr, ti, hl):
                base = (arr * 16 + ti) * 384 + (hl // 2) * 192 + (0 if hl % 2 == 0 else 64)
                return Vst.ap(0, 128, base, [[1, 128]])

            groups = []
            for hl in range(4):
                hp, pr = hl % 2, hl // 2
                p0 = 64 * hp
                a3, a3tok = acc3.next()
                for rg in range(4):
                    tl = []
                    for k in range(4):
                        r = 4 * rg + k
                        tl.append(dict(k=kT.ap(p0, 64, pr * NT + r, [[16, 128]]), q=qT.ap(p0, 64, pr * NT + r, [[16, 128]]),
                                       v=vaug(2, r, hl), out=Q["O3"].h[:, 128 * k:128 * (k + 1)], start=True,
                                       rd=allq + allk, vrd=[("V", 2, r), "Vones"]))
                    ebap = EB.ap(0, 128, (hl * 3 + 2) * 256, [[0, 4], [1, 128]])

                    def post3(rg=rg, a3=a3, a3tok=a3tok):
                        self.acopy(a3.ap(0, 128, 4 * rg, [[1, 4], [16, 128]]),
                                   Q["O3"].h[:, :].rearrange("p (k i) -> p k i", i=128), ["O3"], [a3tok])
                    groups.append(dict(tiles=tl, units=[(0, 512, ebap, ("EBA", hl * 3 + 2))], post=post3, acctok="O3",
                                       ebview=True))
                for tb in range(4):
                    A_, atok = Q["ACC"].next()
                    units = []
                    first = True
                    for n_ in range(4 * tb, 4 * tb + 4):
                        tl = [dict(k=kT.ap(p0, 64, pr * NT + 128 * n_, [[1, 128]]), q=qT.ap(p0, 64, pr * NT + 128 * n_, [[1, 128]]),
                                   v=vaug(0, n_, hl), out=A_.h[:, 128 * (n_ - 4 * tb):128 * (n_ - 4 * tb + 1)], start=first,
                                   rd=[("qTA", n_), ("kTA", n_)], vrd=[("V", 0, n_), "Vones"])]
                        first = False
                        if n_ > 0:
                            tl.append(dict(k=kT.ap(p0, 64, pr * NT + 128 * (n_ - 1), [[1, 128]]), q=tl[0]["q"],
                                           v=vaug(0, n_ - 1, hl), out=tl[0]["out"], start=False,
                                           rd=[("qTA", n_), ("kTA", n_ - 1)], vrd=[("V", 0, n_ - 1), "Vones"]))
                        units.append((tl, (hl * 3 + 0)))
                    for r in range(4):
                        qa = qT.ap(p0, 64, pr * NT + 512 * tb + r, [[4, 128]])
                        oa = A_.ap(0, 128, r, [[4, 128]])
                        tl = [dict(k=kT.ap(p0, 64, pr * NT + 512 * tb + r, [[4, 128]]), q=qa, v=vaug(1, 4 * tb + r, hl),
                                   out=oa, start=False, rd=allq + allk, vrd=[("V", 1, 4 * tb + r), "Vones"])]
                        if tb > 0:
                            tl.append(dict(k=kT.ap(p0, 64, pr * NT + 512 * (tb - 1) + r, [[4, 128]]), q=qa,
                                           v=vaug(1, 4 * (tb - 1) + r, hl), out=oa, start=False, rd=allq + allk,
                                           vrd=[("V", 1, 4 * (tb - 1) + r), "Vones"]))
                        units.append((tl, (hl * 3 + 1)))
                    cur_t, cur_u = [], []
                    packed = []
                    for (tl, ebi) in units:
                        if len(cur_t) + len(tl) > 4:
                            packed.append((cur_t, cur_u))
                            cur_t, cur_u = [], []
                        c0 = 128 * len(cur_t)
                        cur_u.append((c0, 128 * len(tl), EB.ap(0, 128, ebi * 256, [[1, 128 * len(tl)]]), ("EBA", ebi)))
                        cur_t = cur_t + tl
                    packed.append((cur_t, cur_u))

                    def postA(A_=A_, atok=atok, a3=a3, a3tok=a3tok, tb=tb, hp=hp, pr=pr):
                        nr, dr = (0, 64) if hp == 0 else (64, 0)
                        self.tt("dve", A_.h[:, :], A_.h[:, :], a3.h[:, 512 * tb:512 * tb + 512], ALU.add, [atok, a3tok], [atok])
                        self.acopy(P["yT"].ap(nr, 64, (2 * hf + pr) * NT + 512 * tb, [[1, 512]]), A_.h[nr:nr + 64, :], [atok],
                                   [("yT", 2 * hf + pr, tb, hp)])
                        rs_, rstok = rsm.next()
                        dc_, dctok = dcp.next()
                        self.acopy(dc_.h[dr:dr + 64, :], A_.h[dr:dr + 64, :], [atok], [dctok])
                        self.tt("pool", dc_.h[dr:dr + 64, :], dc_.h[dr:dr + 64, :], maskD.h[dr:dr + 64, :], ALU.mult, [dctok, "maskD"], [dctok])
                        self.S.add("dve", lambda e, o=rs_.h[dr:dr + 64, 8:16], i=dc_.h[dr:dr + 64, :].rearrange("p (c j) -> p j c", j=8):
                                   e.reduce_sum(out=o, in_=i, axis=AX.X), [dctok], [rstok])
                        self.recip(rs_.h[dr:dr + 64, 0:8], rs_.h[dr:dr + 64, 8:16], [rstok], [rstok])
                        hg = 4 * hf + 2 * pr + hp
                        self.dma(bass.AP(D["rD"], hg * 2048 + 512 * tb, [[8, 64], [1, 8]]), rs_.h[dr:dr + 64, 0:8], [rstok],
                                 [("rD", hg, tb)])
                    for gi, (tl, ul) in enumerate(packed):
                        groups.append(dict(tiles=tl, units=ul, post=postA if gi == len(packed) - 1 else None, acctok=atok))
            for g in groups:
                if g.get("ebview"):
                    ebi = g["units"][0][3][1]
                    g["units"] = [(128 * k, 128, EB.ap(0, 128, ebi * 256, [[1, 128]]), ("EBA", ebi)) for k in range(4)]
            import os
            lim = os.environ.get("DBG_GROUPS")
            if lim is not None:
                groups = groups[:int(lim)]
            self.run_groups(groups, Pb, Pm)
            self.S.phase_end()

    def phase_B(self):
        P, D, Q = self.P, self.D, self.Q
        with ExitStack() as st:
            Wq = self.sb(st, [128, 8, 512], BF16, "WqB")
            Wkv = self.sb(st, [128, 8, 256], BF16, "WkvB")
            qT = self.sb(st, [128, 4, NT], BF16, "qTB")
            kT = self.sb(st, [128, 2, NT], BF16, "kTB")
            VB = self.sb(st, [128, 16, 320], BF16, "VB")
            EB = self.sb(st, [128, 8, 256], F32, "EBB")
            sq = self.rot(st, 3, [128, 512], F32, "sqB")
            ss = self.rot(st, 6, [128, 24], F32, "ssB")
            qb = self.rot(st, 4, [128, 512], BF16, "qbB")
            kn = self.rot(st, 2, [128, 128], F32, "knB")
            ko = self.rot(st, 3, [128, 128], F32, "koB")
            kb = self.rot(st, 4, [128, 256], BF16, "kbB")
            vo = self.rot(st, 2, [128, 128], F32, "voB")
            rsm = self.rot(st, 4, [128, 16], F32, "rsmB")
            dcp = self.rot(st, 2, [128, 512], F32, "dcpB")
            maskD = self.sb(st, [128, 512], F32, "maskD")
            self.dma(maskD.h[:, :], D["masks"].ap()[:, 0:512], (), ["maskD"])
            P["ones_f"] = self.sb(st, [1, 512], F32, "ones_f")
            self.memset("pool", P["ones_f"].h[:], 1.0, ["ones_f"])
            self.load_w(Wq, "WqB", "w_in", C_QB, 512, 0)
            self.load_w(Wkv, "WkvB", "w_in", C_KB, 256, 0)
            for c in (0, 128, 256):
                self.memset("pool", VB.h[:, :, c:c + 64], 1.0, ["VBones"])
            self.build_EB(st, EB, "EBB", [(h, 3 * 16 + 8 + h) for h in range(8)])
            tiles = [(128 * i, 128) for i in range(16)] + [(2048, 16)]
            PJr = self.bank_rot(["PJ", "S", "ACC"])

            def tile_stages(ti, col0, n):
                samp = (ti == 16)
                X = {}

                def SA():
                    pj, pjt = PJr.next()
                    self.proj_tm(pj, pjt, "hT", col0, n, Wq, "WqB", 512)
                    X["pq"] = (pj, pjt)
                    X["sq"] = self.rms_stats_a(pj, pjt, n, 0, 8, 64, sq, ss)
                    pk, pkt = PJr.next()
                    self.proj_tm(pk, pkt, "hT", col0, n, Wkv, "WkvB", 256)
                    X["pk"] = (pk, pkt)
                    X["sk"] = self.rms_stats_a(pk, pkt, n, 0, 2, 64, sq, ss)
                    if samp:
                        self.acopy(P["vsB"].h[0:16, :], pk.h[0:16, 128:256], [pkt], ["vsB"])
                        self.dma(D["sbv"].ap(), P["vsB"].h[0:16, :], ["vsB"], [])
                    else:
                        if ti == 15:
                            v_, vtok = vo.next()
                            self.acopy(v_.h[:, :], pk.h[:, 128:256], [pkt], [vtok])
                            self.dma(D["pbv"].ap(), v_.h[:, :], [vtok], [])
                        self.tcopy("dve", VB.ap(0, 128, ti * 320 + 64, [[128, 2], [1, 64]]),
                                   pk.h[:, 128:256].rearrange("p (g e) -> p g e", e=64), [pkt], [("VB", ti)])

                def SB1():
                    pj, pjt = X["pq"]
                    s_, stok = X["sq"]
                    self.rms_stats_b(s_, stok, n, 8, 64)
                    q_, qtok = qb.next()
                    X["q"] = (q_, qtok)
                    self.tt("dve", q_.h[0:n, :].rearrange("p (h e) -> p h e", e=64),
                            pj.h[0:n, 0:512].rearrange("p (h e) -> p h e", e=64),
                            self.bc_heads(s_, n, 16, 8, 64), ALU.mult, [pjt, stok], [qtok])
                    if samp:
                        self.tt("dve", P["qsB"].h[0:16, :].rearrange("p (h e) -> p h e", e=64),
                                pj.h[0:16, 0:512].rearrange("p (h e) -> p h e", e=64),
                                self.bc_heads(s_, 16, 16, 8, 64), ALU.mult, [pjt, stok], ["qsB"])
                        self.tt("pool", P["qsB"].h[0:16, :].rearrange("p (h e) -> p h e", e=64),
                                P["qsB"].h[0:16, :].rearrange("p (h e) -> p h e", e=64),
                                P["gqsB"].ap(0, 16, 0, [[0, 8], [1, 64]]), ALU.mult, ["qsB", "gqsB"], ["qsB"])
                    pk, pkt = X["pk"]
                    s2, s2tok = X["sk"]
                    self.rms_stats_b(s2, s2tok, n, 2, 64)
                    n_, ntok = kn.next()
                    self.tt("dve", n_.h[0:n, :].rearrange("p (h e) -> p h e", e=64),
                            pk.h[0:n, 0:128].rearrange("p (h e) -> p h e", e=64),
                            self.bc_heads(s2, n, 16, 2, 64), ALU.mult, [pkt, s2tok], [ntok])
                    if samp:
                        o_, otok = P["ksB"], "ksB"
                    else:
                        o_, otok = ko.next()
                    X["o"] = (o_, otok)
                    self.tt("pool", o_.h[0:n, :].rearrange("p (h e) -> p h e", e=64), n_.h[0:n, :].rearrange("p (h e) -> p h e", e=64),
                            P["gkB"].ap(0, n, 0, [[0, 2], [1, 64]]), ALU.mult, [ntok, "gkB"], [otok])
                    if samp:
                        self.dma(D["sbk"].ap(), o_.h[0:16, :], [otok], [])
                    elif ti == 15:
                        self.dma(D["pbk"].ap(), o_.h[:, :], [otok], [])

                def SB2():
                    o_, otok = X["o"]
                    b_, btok = kb.next()
                    X["b"] = (b_, btok)
                    self.acopy(b_.h[0:n, :].rearrange("p (g r e) -> p g r e", r=2, e=64),
                               o_.ap(0, n, 0, [[64, 2], [0, 2], [1, 64]]), [otok], [btok])

                def SC():
                    q_, qtok = X["q"]
                    b_, btok = X["b"]
                    for j in range(4):
                        self.tr(Q["TRb"].h[:, j * 128:j * 128 + n], q_.h[0:n, j * 128:(j + 1) * 128], P["identb"].h[0:n, 0:n],
                                [qtok, "identb"], ["TRb"])
                    for j in range(2):
                        self.tr(Q["TRb"].h[:, (4 + j) * 128:(4 + j) * 128 + n], b_.h[0:n, j * 128:(j + 1) * 128],
                                P["identb"].h[0:n, 0:n], [btok, "identb"], ["TRb"])
                    trv = Q["TRb"].h[:, 0:512].rearrange("p (k t) -> p k t", t=128)
                    self.acopy(qT.h[:, :, col0:col0 + n], trv[:, 0:4, 0:n], ["TRb"], [("qTB", ti)])
                    trk = Q["TRb"].h[:, 512:768].rearrange("p (k t) -> p k t", t=128)
                    self.tsmul("dve", kT.h[:, :, col0:col0 + n], trk[:, 0:2, 0:n], P["kscB"].h[:, 0:1], ["TRb", "kscB"], [("kTB", ti)])
                return [SA, SB1, SB2, SC]

            pipe = []
            for it in range(len(tiles) + 3):
                if it < len(tiles):
                    pipe.append(tile_stages(it, *tiles[it]))
                for stg in pipe:
                    if stg:
                        stg.pop(0)()
                pipe = [p_ for p_ in pipe if p_]
            self.S.barrier()
            Pb = RotL([(T(Wq.h[:, 2 * i:2 * i + 2, :].rearrange("p a b -> p (a b)").bitcast(F32), [128, 512]), ("PbB", i))
                       for i in range(4)])
            Pm = RotL([(T(Wkv.h[:, 2 * i:2 * i + 2, :].rearrange("p a b -> p (a b)"), [128, 512]), ("PmB", i)) for i in range(4)])
            groups = []
            for h in range(8):
                g_, hp, pr = h // 4, h % 2, h // 2
                p0 = 64 * hp
                vcol = (64 + 128 * g_) if hp == 0 else (128 * g_)
                for tb in range(4):
                    A_, atok = Q["ACC"].next()
                    units = []
                    for n_ in range(4 * tb, 4 * tb + 4):
                        oa = A_.h[:, 128 * (n_ - 4 * tb):128 * (n_ - 4 * tb + 1)]
                        qa = qT.ap(p0, 64, pr * NT + 128 * n_, [[1, 128]])
                        tl = [dict(k=kT.ap(p0, 64, g_ * NT + 128 * n_, [[1, 128]]), q=qa,
                                   v=VB.ap(0, 128, n_ * 320 + vcol, [[1, 128]]), out=oa, start=False,
                                   rd=[("qTB", n_), ("kTB", n_)], vrd=[("VB", n_), "VBones"])]
                        if n_ > 0:
                            tl.append(dict(k=kT.ap(p0, 64, g_ * NT + 128 * (n_ - 1), [[1, 128]]), q=qa,
                                           v=VB.ap(0, 128, (n_ - 1) * 320 + vcol, [[1, 128]]), out=oa, start=False,
                                           rd=[("qTB", n_), ("kTB", n_ - 1)], vrd=[("VB", n_ - 1), "VBones"]))
                        units.append(tl)
                    packed = [(units[0] + units[1]), (units[2] + units[3])]

                    def postB(A_=A_, atok=atok, tb=tb, hp=hp, pr=pr):
                        nr, dr = (0, 64) if hp == 0 else (64, 0)
                        self.acopy(P["yT"].ap(nr, 64, (4 + pr) * NT + 512 * tb, [[1, 512]]), A_.h[nr:nr + 64, :], [atok],
                                   [("yT", 4 + pr, tb, hp)])
                        rs_, rstok = rsm.next()
                        dc_, dctok = dcp.next()
                        self.acopy(dc_.h[dr:dr + 64, :], A_.h[dr:dr + 64, :], [atok], [dctok])
                        self.tt("pool", dc_.h[dr:dr + 64, :], dc_.h[dr:dr + 64, :], maskD.h[dr:dr + 64, :], ALU.mult, [dctok, "maskD"], [dctok])
                        self.S.add("dve", lambda e, o=rs_.h[dr:dr + 64, 8:16], i=dc_.h[dr:dr + 64, :].rearrange("p (c j) -> p j c", j=8):
                                   e.reduce_sum(out=o, in_=i, axis=AX.X), [dctok], [rstok])
                        self.recip(rs_.h[dr:dr + 64, 0:8], rs_.h[dr:dr + 64, 8:16], [rstok], [rstok])
                        hg = 8 + 2 * pr + hp
                        self.dma(bass.AP(D["rD"], hg * 2048 + 512 * tb, [[8, 64], [1, 8]]), rs_.h[dr:dr + 64, 0:8], [rstok],
                                 [("rD", hg, tb)])
                    for gi, tl in enumerate(packed):
                        ul = []
                        c = 0
                        i = 0
                        while i < len(tl):
                            w = 2 if (i + 1 < len(tl) and tl[i + 1]["out"] is tl[i]["out"]) else 1
                            ul.append((128 * i, 128 * w, EB.ap(0, 128, h * 256, [[1, 128 * w]]), ("EBB", h)))
                            i += w
                        gd = dict(tiles=tl, units=ul, post=postB if gi == 1 else None, acctok=atok)
                        if gi == 0:
                            gd["pre_sink"] = (A_, atok, h)
                        groups.append(gd)
            self.run_groups(groups, Pb, Pm)
            self.S.phase_end()

    def phase_C(self):
        P, D, Q = self.P, self.D, self.Q
        with ExitStack() as st:
            Wq = self.sb(st, [128, 8, 512], BF16, "WqC")
            Wm = self.sb(st, [128, 8, 1024], BF16, "Wm")
            qT = self.sb(st, [128, 4, NT], BF16, "qTC")
            mkT = self.sb(st, [128, 4, 256], BF16, "mkT")
            mvb = self.sb(st, [128, 2, 512], BF16, "mvb")
            sq = self.rot(st, 2, [128, 512], F32, "sqC")
            ss = self.rot(st, 2, [128, 24], F32, "ssC")
            qb = self.rot(st, 4, [128, 512], BF16, "qbC")
            kn = self.rot(st, 2, [128, 512], F32, "knC")
            ko = self.rot(st, 2, [128, 512], F32, "koC")
            kb = self.rot(st, 2, [128, 512], BF16, "kbC")
            vo = self.rot(st, 2, [128, 512], F32, "voC")
            Pm = self.rot(st, 2, [128, 512], BF16, "PmC")
            rsm = self.rot(st, 4, [128, 16], F32, "rsmC")
            dcp = self.rot(st, 2, [128, 512], F32, "dcpC")
            maskD = self.sb(st, [128, 512], F32, "maskD")
            self.dma(maskD.h[:, :], D["masks"].ap()[:, 512:1024], (), ["maskD"])
            self.load_w(Wq, "WqC", "w_in", C_QC, 512, 0)
            self.load_w(Wm, "Wm", "w_mem_kv", 0, 1024, 0)
            P["memT"] = self.sb(st, [128, 8, 256], BF16, "memT")
            with ExitStack() as st3:
                gmem = self.sb(st3, [128, 1024], F32, "gmem")
                self.dma(gmem.h[:], bass.AP(D["mem_ln_g"], 0, [[0, 128], [1, 1024]]), (), ["gmem"])
                self.norm_tiles(st3, [("mem", 128 * i, P["memT"], "memT", gmem, "gmem", 128 * i, 128) for i in range(2)], deep=False)
                self.S.phase_end()
            for mt in range(2):
                pj, pjt = Q["PJ"].next()
                for kc in range(8):
                    self.mm(pj.h[:, :], P["memT"].h[:, kc, 128 * mt:128 * mt + 128], Wm.h[:, kc, 0:512], kc == 0, kc == 7,
                            [("memT", 128 * mt)] + self.wrd("Wm", kc), [pjt])
                s_, stok = self.rms_stats(pj, pjt, 128, 0, 4, 128, sq, ss)
                n_, ntok = kn.next()
                self.tt("dve", n_.h[:, :].rearrange("p (h e) -> p h e", e=128), pj.h[:, :].rearrange("p (h e) -> p h e", e=128),
                        self.bc_heads(s_, 128, 16, 4, 128), ALU.mult, [pjt, stok], [ntok])
                o_, otok = ko.next()
                self.tt("pool", o_.h[:, :].rearrange("p (h e) -> p h e", e=128), n_.h[:, :].rearrange("p (h e) -> p h e", e=128),
                        P["gkC"].ap(0, 128, 0, [[0, 4], [1, 128]]), ALU.mult, [ntok, "gkC"], [otok])
                self.dma(D["pmk"].ap()[128 * mt:128 * mt + 128, :], o_.h[:, :], [otok], [])
                b_, btok = kb.next()
                self.acopy(b_.h[:, :], o_.h[:, :], [otok], [btok])
                for j in range(4):
                    self.tr(Q["TRb"].h[:, j * 128:(j + 1) * 128], b_.h[:, j * 128:(j + 1) * 128], P["identb"].h[:, :],
                            [btok, "identb"], ["TRb"])
                trv = Q["TRb"].h[:, 0:512].rearrange("p (k t) -> p k t", t=128)
                self.tsmul("dve", mkT.h[:, :, 128 * mt:128 * mt + 128], trv, P["kscC"].h[:, 0:1], ["TRb", "kscC"], [("mkT", mt)])
                pj, pjt = Q["PJ"].next()
                for kc in range(8):
                    self.mm(pj.h[:, :], P["memT"].h[:, kc, 128 * mt:128 * mt + 128], Wm.h[:, kc, 512:1024], kc == 0, kc == 7,
                            [("memT", 128 * mt)] + self.wrd("Wm", kc), [pjt])
                v_, vtok = vo.next()
                self.acopy(v_.h[:, :], pj.h[:, :], [pjt], [vtok])
                self.dma(D["pmv"].ap()[128 * mt:128 * mt + 128, :], v_.h[:, :], [vtok], [])
                self.tcopy("dve", mvb.h[:, mt, :], pj.h[:, :], [pjt], [("mvb", mt)])
            tiles = [(128 * i, 128) for i in range(16)] + [(2048, 16)]
            deferC = []
            PJr = self.bank_rot(["PJ", "S", "ACC"])
            for ti, (col0, n) in enumerate(tiles):
                pj, pjt = PJr.next()
                self.proj_tm(pj, pjt, "hT", col0, n, Wq, "WqC", 512)
                s_, stok = self.rms_stats(pj, pjt, n, 0, 4, 128, sq, ss)
                if ti == 16:
                    self.tt("dve", P["qsC"].h[0:16, :].rearrange("p (h e) -> p h e", e=128),
                            pj.h[0:16, 0:512].rearrange("p (h e) -> p h e", e=128),
                            self.bc_heads(s_, 16, 16, 4, 128), ALU.mult, [pjt, stok], ["qsC"])
                    self.tt("pool", P["qsC"].h[0:16, :].rearrange("p (h e) -> p h e", e=128),
                            P["qsC"].h[0:16, :].rearrange("p (h e) -> p h e", e=128),
                            P["gqsC"].ap(0, 16, 0, [[0, 4], [1, 128]]), ALU.mult, ["qsC", "gqsC"], ["qsC"])
                    continue
                q_, qtok = qb.next()
                self.tt("dve", q_.h[0:n, :].rearrange("p (h e) -> p h e", e=128),
                        pj.h[0:n, 0:512].rearrange("p (h e) -> p h e", e=128),
                        self.bc_heads(s_, n, 16, 4, 128), ALU.mult, [pjt, stok], [qtok])
                def st2c(q_=q_, qtok=qtok, col0=col0, n=n, ti=ti):
                    for j in range(4):
                        self.tr(Q["TRb"].h[:, j * 128:j * 128 + n], q_.h[0:n, j * 128:(j + 1) * 128], P["identb"].h[0:n, 0:n],
                                [qtok, "identb"], ["TRb"])
                    trv = Q["TRb"].h[:, 0:512].rearrange("p (k t) -> p k t", t=128)
                    self.acopy(qT.h[:, :, col0:col0 + n], trv[:, 0:4, 0:n], ["TRb"], [("qTC", ti)])
                deferC.append(st2c)
                while len(deferC) > 2:
                    deferC.pop(0)()
            while deferC:
                deferC.pop(0)()
            self.dma(D["qS"].ap()[0:16, :], P["qsA"].h[:, :], [("qsA", 0), ("qsA", 1)], ["qS"])
            self.dma(D["qS"].ap()[16:32, :], P["qsB"].h[:, :], ["qsB"], ["qS"])
            self.dma(D["qS"].ap()[32:48, :], P["qsC"].h[:, :], ["qsC"], ["qS"])
            jobs = [(h, tb, mt) for h in range(4) for tb in range(4) for mt in range(2)]
            accs = {}
            pend = None
            for job in jobs + [None]:
                cur = None
                if job is not None:
                    h, tb, mt = job
                    Sb, stok = Q["S"].next()
                    self.mm(Sb.h[:, :], mkT.h[:, h, 128 * mt:128 * mt + 128], qT.h[:, h, 512 * tb:512 * tb + 512], True, True,
                            [("mkT", mt)] + [("qTC", 4 * tb + i) for i in range(4)], [stok])
                    Pm_, pmtok = Pm.next()
                    self.act(Pm_.h[:, :], Sb.h[:, :], AF.Exp, [stok], [pmtok])
                    cur = (job, Pm_, pmtok)
                if pend is not None:
                    (h, tb, mt), Pm_, pmtok = pend
                    if mt == 0:
                        accs[(h, tb)] = Q["ACC"].next()
                    A_, atok = accs[(h, tb)]
                    self.mm(A_.h[:, :], mvb.h[:, mt, 128 * h:128 * h + 128], Pm_.h[:, :], mt == 0, mt == 1,
                            [pmtok, ("mvb", mt)], [atok])
                    self.mm(Q["O3"].h[:, :], P["onesb"].h[:, :], Pm_.h[:, :], mt == 0, mt == 1, [pmtok, "onesb"], ["O3"])
                    if mt == 1:
                        self.acopy(P["yT"].h[:, 8 + h, 512 * tb:512 * tb + 512], A_.h[:, :], [atok], [("yT", 8 + h, tb, 0)])
                        rs_, rstok = rsm.next()
                        dc_, dctok = dcp.next()
                        self.acopy(dc_.h[:, :], Q["O3"].h[:, :], ["O3"], [dctok])
                        self.tt("pool", dc_.h[:, :], dc_.h[:, :], maskD.h[:, :], ALU.mult, [dctok, "maskD"], [dctok])
                        self.S.add("dve", lambda e, o=rs_.h[:, 8:12], i=dc_.h[:, :].rearrange("p (c j) -> p j c", j=4):
                                   e.reduce_sum(out=o, in_=i, axis=AX.X), [dctok], [rstok])
                        self.recip(rs_.h[:, 0:4], rs_.h[:, 8:12], [rstok], [rstok])
                        self.dma(bass.AP(D["rD"], (16 + h) * 2048 + 512 * tb, [[4, 128], [1, 4]]), rs_.h[:, 0:4], [rstok],
                                 [("rD", 16 + h, tb)])
                pend = cur
            self.S.phase_end()

    def sample_alloc(self, st, stM):
        B = self.SB = {}
        B["szS"] = self.sb(stM, [128, 12, 16], F32, "szS")
        B["sgS"] = self.sb(stM, [128, 24, 16], F32, "sgS")
        B["Kt"] = self.rot(st, 4, [128, 512], F32, "Kt")
        B["Vt"] = self.rot(st, 5, [128, 512], F32, "Vt")
        B["qbc"] = self.rot(st, 2, [128, 512], F32, "qbc")
        B["prod"] = self.rot(st, 2, [128, 512], F32, "prod")
        B["Wb"] = self.rot(st, 3, [128, 512], BF16, "Wb")
        B["sm"] = self.rot(st, 5, [128, 24], F32, "sm")
        B["pmb"] = self.rot(st, 5, [128, 8], BF16, "pmb")
        B["fin"] = self.sb(st, [16, 64], F32, "fin")
        B["fb"] = self.sb(st, [16, 64], F32, "fb")
        B["fc"] = self.sb(st, [16, 8], F32, "fc")
        B["t1"] = self.sb(st, [16, 512], F32, "t1")
        B["t2"] = self.sb(st, [16, 512], F32, "t2")
        B["ob"] = self.sb(st, [16, 1536], BF16, "ob")

    def sample_tiles(self):
        P, D, Q, B = self.P, self.D, self.Q, self.SB
        NUMB = Q["ACC"].ts[0]
        DENB = Q["ACC"].ts[1]
        p0 = {"A": 0, "B": 32, "C": 64}
        dcol = {"A": 0, "B": 8, "C": 16}
        cfg = {"A": dict(nh=8, hd=64, mi=0), "B": dict(nh=8, hd=64, mi=1), "C": dict(nh=4, hd=128, mi=2)}
        total = {"A": 48, "B": 16, "C": 32}
        count = {"A": 0, "B": 0, "C": 0}
        tiles = []
        for b in range(16):
            for mix in ("A", "B", "C"):
                for t in range({"A": 3, "B": 1, "C": 2}[mix]):
                    tiles.append((b, mix, t))
        loaded = {}

        def load(i):
            b, mix, t = tiles[i]
            k_, ktok = B["Kt"].next()
            v_, vtok = B["Vt"].next()
            if mix == "A":
                d = A_D[t]
                base = (b * 2048 + 2048 - 128 * d) * 512
                self.dma(k_.h[:, :], bass.AP(D["cak"], base, [[d * 512, 128], [1, 512]]), (), [ktok])
                self.dma(v_.h[:, :], bass.AP(D["cav"], base, [[d * 512, 128], [1, 512]]), (), [vtok], q="act")
            elif mix == "B":
                self.dma(k_.h[:, 0:128], D["cbk"].ap()[128 * b:128 * b + 128, :], (), [ktok])
                self.dma(v_.h[:, 0:128], D["cbv"].ap()[128 * b:128 * b + 128, :], (), [vtok], q="act")
            else:
                r0 = 256 * b + 128 * t
                self.dma(k_.h[:, :], D["cmk"].ap()[r0:r0 + 128, :], (), [ktok])
                self.dma(v_.h[:, :], D["cmv"].ap()[r0:r0 + 128, :], (), [vtok], q="act")
            loaded[i] = (k_, ktok, v_, vtok)

        PF = 2
        for i in range(min(PF, len(tiles))):
            load(i)
        qcur = {}
        state = {"dfirst": True}

        def make_stages(i, b, mix, t):
            c = cfg[mix]
            nh, hd = c["nh"], c["hd"]
            if t == 0:
                qb_, qbt = B["qbc"].next()
                self.dma(qb_.h[:, :], bass.AP(D["qS"], (c["mi"] * 16 + b) * 512, [[0, 128], [1, 512]]), ["qS"], [qbt])
                qcur[mix] = (qb_, qbt)
            qb_, qbt = qcur[mix]
            k_, ktok, v_, vtok = loaded.pop(i)
            if mix == "A":
                kin = k_.h[:, :].rearrange("p (h e) -> p h e", e=64)
                vin = v_.h[:, :].rearrange("p (h e) -> p h e", e=64)
                ebap = P["EBs"].h[:, t, 0:8]
            elif mix == "B":
                kin = k_.ap(0, 128, 0, [[64, 2], [0, 4], [1, 64]])
                vin = v_.ap(0, 128, 0, [[64, 2], [0, 4], [1, 64]])
                ebap = P["EBs"].h[:, 3, 8:16]
            else:
                kin = k_.h[:, :].rearrange("p (h e) -> p h e", e=128)
                vin = v_.h[:, :].rearrange("p (h e) -> p h e", e=128)
                ebap = None
            pr_, prtok = B["prod"].next()
            s_, stok = B["sm"].next()
            pm_, pmtok = B["pmb"].next()
            w_, wtok = B["Wb"].next()
            count[mix] += 1
            cnt = count[mix]

            def stA():
                if mix == "B":
                    pview = pr_.h[:, :].rearrange("p (g r e) -> p g r e", r=4, e=64)
                    qview = qb_.h[:, :].rearrange("p (g r e) -> p g r e", r=4, e=64)
                else:
                    pview = pr_.h[:, :].rearrange("p (h e) -> p h e", e=hd)
                    qview = qb_.h[:, :].rearrange("p (h e) -> p h e", e=hd)
                self.tt("dve", pview, kin, qview, ALU.mult, [ktok, qbt], [prtok])
                self.rsum(s_.h[:, 0:nh], pr_.h[:, :].rearrange("p (h e) -> p h e", e=hd), [prtok], [stok])

            def stB():
                if ebap is None:
                    self.act(pm_.h[:, 0:nh], s_.h[:, 0:nh], AF.Exp, [stok], [pmtok])
                else:
                    self.act(s_.h[:, 8:8 + nh], s_.h[:, 0:nh], AF.Exp, [stok], [stok])

            def stC():
                if ebap is not None:
                    self.tt("dve", pm_.h[:, 0:nh], s_.h[:, 8:8 + nh], ebap, ALU.mult, [stok, "EBs"], [pmtok])
                if mix == "B":
                    wview = w_.h[:, :].rearrange("p (g r e) -> p g r e", r=4, e=64)
                    pbc = pm_.ap(0, 128, 0, [[4, 2], [1, 4], [0, 64]])
                else:
                    wview = w_.h[:, :].rearrange("p (h e) -> p h e", e=hd)
                    pbc = pm_.ap(0, 128, 0, [[1, nh], [0, hd]])
                self.tt("pool", wview, vin, pbc, ALU.mult, [vtok, pmtok], [wtok])

            def stD():
                q0 = p0[mix]
                self.mm(NUMB.h[q0:q0 + 16, :], P["OHB"].h[:, 16 * b:16 * b + 16], w_.h[:, :], cnt == 1, cnt == total[mix],
                        ["OHB", wtok], ["num" + mix], sgc=True, tp=(0, q0))
                self.mm(DENB.h[0:16, dcol[mix]:dcol[mix] + nh], P["OHB"].h[:, 16 * b:16 * b + 16], pm_.h[:, 0:nh],
                        state["dfirst"], False, ["OHB", pmtok], ["den"], sgc=True)
                state["dfirst"] = False
            return [stA, stB, stC, stD]

        pipe = []
        for i in range(len(tiles) + 3):
            if i < len(tiles):
                if i + PF < len(tiles):
                    load(i + PF)
                pipe.append(make_stages(i, *tiles[i]))
            for stg in pipe:
                if stg:
                    stg.pop(0)()
            pipe = [p_ for p_ in pipe if p_]
            yield

    def pump(self, gen, k=1):
        if gen is None:
            return
        for _ in range(k):
            try:
                next(gen)
            except StopIteration:
                return

    def sample_finalize(self):
        P, D, Q, B = self.P, self.D, self.Q, self.SB
        NUMB = Q["ACC"].ts[0]
        DENB = Q["ACC"].ts[1]
        fin, fb, fc, t1, t2, ob = B["fin"], B["fb"], B["fc"], B["t1"], B["t2"], B["ob"]
        self.tt("dve", t1.h[:, :], P["qsA"].h[:, :], P["ksA"].h[:, :], ALU.mult,
                [("qsA", 0), ("qsA", 1), ("ksA", 0), ("ksA", 1)], ["t1"])
        self.rsum(fin.h[:, 0:8], t1.h[:, :].rearrange("p (h e) -> p h e", e=64), ["t1"], ["finA"])
        self.act(fin.h[:, 8:16], fin.h[:, 0:8], AF.Exp, ["finA"], ["finA"])
        self.stt("dve", fin.h[:, 16:24], fin.h[:, 8:16], 3.0, P["e0"].h[:, 0:8], ALU.mult, ALU.mult, ["finA", "e0"], ["finA"])
        self.tt("dve", t1.h[:, :].rearrange("p (h e) -> p h e", e=64), P["vsA"].h[:, :].rearrange("p (h e) -> p h e", e=64),
                fin.ap(0, 16, 16, [[1, 8], [0, 64]]), ALU.mult, ["finA", ("vsA", 0), ("vsA", 1), "t1"], ["t1"])
        self.tt("dve", t1.h[:, :], t1.h[:, :], NUMB.h[0:16, :], ALU.add, ["t1", "numA", "numB", "numC", "den"], ["t1"])
        self.tt("dve", fin.h[:, 24:32], fin.h[:, 16:24], DENB.h[0:16, 0:8], ALU.add, ["finA", "den"], ["finA"])
        self.recip(fin.h[:, 32:40], fin.h[:, 24:32], ["finA"], ["finA"])
        self.tt("dve", ob.h[:, 0:512].rearrange("p (h e) -> p h e", e=64), t1.h[:, :].rearrange("p (h e) -> p h e", e=64),
                fin.ap(0, 16, 32, [[1, 8], [0, 64]]), ALU.mult, ["t1", "finA"], ["obA"])
        self.tt("dve", t2.h[:, :].rearrange("p (g r e) -> p g r e", r=4, e=64),
                P["qsB"].h[:, :].rearrange("p (g r e) -> p g r e", r=4, e=64),
                P["ksB"].ap(0, 16, 0, [[64, 2], [0, 4], [1, 64]]), ALU.mult, ["qsB", "ksB"], ["t2"])
        self.rsum(fb.h[:, 0:8], t2.h[:, :].rearrange("p (h e) -> p h e", e=64), ["t2"], ["finB"])
        self.act(fb.h[:, 8:16], fb.h[:, 0:8], AF.Exp, ["finB"], ["finB"])
        self.tt("dve", fb.h[:, 16:24], fb.h[:, 8:16], P["e0"].h[:, 8:16], ALU.mult, ["finB", "e0"], ["finB"])
        self.tt("dve", t2.h[:, :].rearrange("p (g r e) -> p g r e", r=4, e=64),
                P["vsB"].ap(0, 16, 0, [[64, 2], [0, 4], [1, 64]]),
                fb.ap(0, 16, 16, [[4, 2], [1, 4], [0, 64]]), ALU.mult, ["finB", "vsB", "t2"], ["t2"])
        self.tt("dve", t2.h[:, :], t2.h[:, :], NUMB.h[32:48, :], ALU.add, ["t2", "numA", "numB", "numC", "den"], ["t2"])
        self.tt("dve", fb.h[:, 24:32], fb.h[:, 16:24], DENB.h[0:16, 8:16], ALU.add, ["finB", "den"], ["finB"])
        self.tt("dve", fb.h[:, 32:40], fb.h[:, 24:32], P["esink"].h[:, :], ALU.add, ["finB", "esink"], ["finB"])
        self.recip(fb.h[:, 40:48], fb.h[:, 32:40], ["finB"], ["finB"])
        self.tt("dve", ob.h[:, 512:1024].rearrange("p (h e) -> p h e", e=64), t2.h[:, :].rearrange("p (h e) -> p h e", e=64),
                fb.ap(0, 16, 40, [[1, 8], [0, 64]]), ALU.mult, ["t2", "finB"], ["obB"])
        self.recip(fc.h[:, 0:4], DENB.h[0:16, 16:20], ["den"], ["finC"])
        self.tt("dve", ob.h[:, 1024:1536].rearrange("p (h e) -> p h e", e=128), fc.ap(0, 16, 0, [[1, 4], [0, 128]]),
                NUMB.h[64:80, :].rearrange("p (h e) -> p h e", e=128), ALU.mult, ["numA", "numB", "numC", "den", "finC"], ["obC"])
        for j in range(12):
            self.tr(Q["TRb"].h[:, 16 * j:16 * j + 16], ob.h[0:16, 128 * j:128 * j + 128], P["identb"].h[0:16, 0:16],
                    ["obA", "obB", "obC", "identb"], ["TRb"])
        self.acopy(P["yT"].h[:, :, 2048:2064], Q["TRb"].h[:, 0:192].rearrange("p (j t) -> p j t", t=16), ["TRb"],
                   [("yT", "samp")])

    def yT_rd(self, j, blk):
        return [("yT", j, blk, 0), ("yT", j, blk, 1)]

    def phase_tail_all(self):
        with ExitStack() as stM:
            self.mT = self.sb(stM, [128, 8, NT], BF16, "mT")
            with ExitStack() as stS:
                self.sample_alloc(stS, stM)
                gen = self.sample_tiles()
                self.phase_z(gen)
                self.phase_gate(gen)
            self.phase_tail_out()

    def z_stages(self, W, wtok, mi, c, j, bi, col0, n, Zr, sz, uz, rb, tz):
        P, D, B = self.P, self.D, self.SB
        X = {}

        def SA():
            pj, pjt = Zr.next()
            for kc in range(8):
                self.mm(pj.h[:, 0:n], W.h[:, kc, 128 * c:128 * c + 128], P["hT"].h[:, kc, col0:col0 + n],
                        kc == 0, kc == 7, self.wrd(wtok, kc) + [("hT", col0 + 128 * i) for i in range(max(1, n // 128))], [pjt])
            s_, stok = sz.next()
            self.act(s_.h[:, 0:n], pj.h[:, 0:n], AF.Tanh, [pjt], [stok], scale=0.5)
            if bi == 4:
                self.stt("dve", B["szS"].h[:, j, :], s_.h[:, 0:16], 1.0, pj.h[:, 0:16], ALU.add, ALU.mult, [stok, pjt], [("szS", j)])
                return
            u_, utok = uz.next()
            X["u"] = (u_, utok)
            self.stt("dve", u_.h[:, 0:n], s_.h[:, 0:n], 1.0, pj.h[:, 0:n], ALU.add, ALU.mult, [stok, pjt], [utok])
            r_, rtok = rb.next()
            if mi < 2:
                for hp in range(2):
                    hg = 2 * j + hp
                    self.dma(r_.h[64 * hp:64 * hp + 64, :], bass.AP(D["rD"], hg * 2048 + col0, [[0, 64], [1, 512]]),
                             [("rD", hg, bi)], [rtok + (hp,)])
                X["r"] = (r_, [rtok + (0,), rtok + (1,)])
            else:
                hg = 16 + c
                self.dma(r_.h[:, :], bass.AP(D["rD"], hg * 2048 + col0, [[0, 128], [1, 512]]), [("rD", hg, bi)], [rtok + (0,)])
                X["r"] = (r_, [rtok + (0,)])

        def SB():
            r_, rtoks = X["r"]
            t_, ttok = tz.next()
            X["t"] = (t_, ttok)
            self.tt("pool", t_.h[:, :], P["yT"].h[:, j, col0:col0 + n], r_.h[:, :], ALU.mult, rtoks + self.yT_rd(j, bi), [ttok])

        def SC():
            u_, utok = X["u"]
            t_, ttok = X["t"]
            self.stt("dve", P["yT"].h[:, j, col0:col0 + n], t_.h[:, :], 0.5, u_.h[:, 0:n], ALU.mult, ALU.mult,
                     [utok, ttok] + self.yT_rd(j, bi), [("y", j, bi)])
        return [SA] if bi == 4 else [SA, SB, SC]

    def phase_z(self, gen):
        P, D, Q, B = self.P, self.D, self.Q, self.SB
        blocks = [(512 * i, 512) for i in range(4)] + [(2048, 16)]
        with ExitStack() as st:
            Wz = self.rot(st, 2, [128, 8, 512], BF16, "Wz")
            sz = self.rot(st, 2, [128, 512], F32, "sz")
            uz = self.rot(st, 3, [128, 512], F32, "uz")
            rb = self.rot(st, 3, [128, 512], F32, "rbz")
            tz = self.rot(st, 2, [128, 512], F32, "tz")
            it = 0
            zc = (C_ZA, C_ZB, C_ZC)
            Zr = self.bank_rot(["PJ", "S", "O3"])

            def zload(mi):
                W, wtok = Wz.next()
                self.load_w(W, wtok, "w_in", zc[mi], 512, 0)
                return W, wtok
            znxt = zload(0)
            zpipe = []
            for mi, c0 in enumerate(zc):
                W, wtok = znxt
                if mi + 1 < 3:
                    znxt = zload(mi + 1)
                for c in range(4):
                    j = 4 * mi + c
                    for bi, (col0, n) in enumerate(blocks):
                        zpipe.append(self.z_stages(W, wtok, mi, c, j, bi, col0, n, Zr, sz, uz, rb, tz))
                        for stg in zpipe:
                            if stg:
                                stg.pop(0)()
                        zpipe[:] = [p_ for p_ in zpipe if p_]
                        it += 1
                        if it % 2 == 0:
                            self.pump(gen)
            while zpipe:
                for stg in zpipe:
                    if stg:
                        stg.pop(0)()
                zpipe[:] = [p_ for p_ in zpipe if p_]
            self.S.phase_end()

    def phase_gate(self, gen):
        P, D, Q, B = self.P, self.D, self.Q, self.SB
        blocks = [(512 * i, 512) for i in range(4)] + [(2048, 16)]
        with ExitStack() as st2:
            Wg = self.rot(st2, 2, [128, 8, 384], BF16, "Wg")
            Wb = self.rot(st2, 2, [128, 3, 4, 128], BF16, "WbS")
            sg = self.rot(st2, 3, [128, 512], F32, "sg")
            tmp = self.rot(st2, 3, [128, 512], F32, "tmpg")
            mc = self.rot(st2, 2, [128, 512], F32, "mc")
            it = 0
            def gload(c):
                W, wtok = Wg.next()
                Wr, wrtok = Wb.next()
                for i, nm in enumerate(("w_br_a", "w_br_b", "w_br_c")):
                    self.load_w(W, wtok, "w_in", C_GA + 1024 * i + 128 * c, 128, 128 * i, part=i)
                    self.dma(Wr.h[:, i, :, :], D[nm].ap().rearrange("(kc p) n -> p kc n", p=128)[:, :, 128 * c:128 * c + 128],
                             (), [(wrtok, i)], q="pool")
                return W, wtok, Wr, wrtok
            nxt = gload(0)
            Gr = self.bank_rot(["PJ", "O3"])
            for c in range(8):
                W, wtok, Wr, wrtok = nxt
                if c + 1 < 8:
                    nxt = gload(c + 1)
                for bi, (col0, n) in enumerate(blocks):
                    if bi < 4:
                        m_, mtok = mc.next()
                    for i in range(3):
                        pg, pgt = Gr.next()
                        for kc in range(8):
                            self.mm(pg.h[:, 0:n], W.h[:, kc, 128 * i:128 * i + 128], P["hT"].h[:, kc, col0:col0 + n],
                                    kc == 0, kc == 7, [(wtok, i, kc // 2)] + [("hT", col0 + 128 * k) for k in range(max(1, n // 128))], [pgt])
                        if bi == 4:
                            self.act(B["sgS"].h[:, 3 * c + i, :], pg.h[:, 0:16], AF.Tanh, [pgt], [("sgS", c)], scale=0.5)
                            continue
                        s_, stok = sg.next()
                        self.act(s_.h[:, 0:n], pg.h[:, 0:n], AF.Tanh, [pgt], [stok], scale=0.5)
                        pb, pbt = Q["S"].next()
                        for kc in range(4):
                            self.mm(pb.h[:, 0:n], Wr.h[:, i, kc, :], P["yT"].h[:, 4 * i + kc, col0:col0 + n],
                                    kc == 0, kc == 3, [(wrtok, i), ("y", 4 * i + kc, bi)], [pbt])
                        if i == 0:
                            self.stt("dve", m_.h[:, 0:n], s_.h[:, 0:n], 1.0, pb.h[:, 0:n], ALU.add, ALU.mult, [pbt, stok], [mtok])
                        else:
                            t_, ttok = tmp.next()
                            self.stt("dve", t_.h[:, 0:n], s_.h[:, 0:n], 1.0, pb.h[:, 0:n], ALU.add, ALU.mult, [pbt, stok], [ttok])
                            if i == 1:
                                self.tt("pool", m_.h[:, 0:n], m_.h[:, 0:n], t_.h[:, 0:n], ALU.add, [mtok, ttok], [mtok])
                            else:
                                self.tt("pool", self.mT.h[:, c, col0:col0 + n], m_.h[:, 0:n], t_.h[:, 0:n], ALU.add,
                                        [mtok, ttok], [("mT", c, bi)])
                        it += 1
                        if it % 2 == 0:
                            self.pump(gen)
            self.pump(gen, 1000)
            self.sample_finalize()
            self.S.phase_end()

    def phase_tail_out(self):
        P, D, Q, B = self.P, self.D, self.Q, self.SB
        with ExitStack() as st:
            Wo = self.sb(st, [128, 8, 1024], BF16, "Wo")
            Wbr = [self.sb(st, [128, 4, 1024], BF16, f"WbrF{i}") for i in range(3)]
            prod = self.sb(st, [128, 24, 16], F32, "prodS")
            xt = self.rot(st, 3, [128, 1024], F32, "xto")
            yo = self.rot(st, 4, [128, 512], F32, "yo")
            wv = D["w_out"].ap().rearrange("(kc p) n -> p kc n", p=128)
            for g in range(4):
                self.dma(Wo.h[:, 2 * g:2 * g + 2, :], wv[:, 2 * g:2 * g + 2, :], (), [("Wo", g)], q="pool")
                self.amul(Wo.h[:, 2 * g:2 * g + 2, :], Wo.h[:, 2 * g:2 * g + 2, :], 0.5, [("Wo", g)], [("Wo", g)])
            for i, nm in enumerate(("w_br_a", "w_br_b", "w_br_c")):
                self.dma(Wbr[i].h[:, :, :], D[nm].ap().rearrange("(kc p) n -> p kc n", p=128), (), [("WbrF", i)], q="pool")

            def out_tile(ti, col0, n, src, dst, r0):
                x_, xtok = xt.next()
                self.dma(x_.h[0:n, :], D[src].ap()[r0:r0 + n, :], (), [xtok])
                bi = 4 if ti == 16 else ti // 4
                for half in range(2):
                    pj, pjt = PJr.next()
                    for kc in range(8):
                        self.mm(pj.h[0:n, :], self.mT.h[:, kc, col0:col0 + n], Wo.h[:, kc, 512 * half:512 * half + 512],
                                kc == 0, kc == 7, [("Wo", kc // 2), ("mT", kc, bi)], [pjt])
                    y_, ytok = yo.next()
                    self.tt("dve", y_.h[0:n, :], pj.h[0:n, :], x_.h[0:n, 512 * half:512 * half + 512], ALU.add, [pjt, xtok], [ytok])
                    self.dma(D[dst].ap()[r0:r0 + n, 512 * half:512 * half + 512], y_.h[0:n, :], [ytok], [], q="act")

            PJr = self.bank_rot(["PJ", "ACC", "O3"])
            for ti in range(16):
                out_tile(ti, 128 * ti, 128, "x", "y", 128 * ti)
            self.stt("dve", P["yT"].h[:, :, 2048:2064], P["yT"].h[:, :, 2048:2064], 0.5, B["szS"].h[:, :, :], ALU.mult, ALU.mult,
                     [("yT", "samp")] + [("szS", j) for j in range(12)], [("y", j, 4) for j in range(12)])
            pb, pbt = Q["S"].next()
            first = True
            for c in range(8):
                for i in range(3):
                    col = (3 * c + i) * 16
                    for kc in range(4):
                        self.mm(pb.h[:, col:col + 16], Wbr[i].h[:, kc, 128 * c:128 * c + 128], P["yT"].h[:, 4 * i + kc, 2048:2064],
                                first, False, [("WbrF", i), ("y", 4 * i + kc, 4)], [pbt], sgc=True)
                        first = False
            self.stt("dve", prod.h[:, :, :], B["sgS"].h[:, :, :], 1.0, pb.h[:, 0:384].rearrange("p (a t) -> p a t", t=16),
                     ALU.add, ALU.mult, [pbt] + [("sgS", c) for c in range(8)], ["prodS"])
            pv = prod.h[:, :, :].rearrange("p (c i) t -> p c i t", i=3)
            self.tt("dve", pv[:, :, 0, :], pv[:, :, 0, :], pv[:, :, 1, :], ALU.add, ["prodS"], ["prodS"])
            self.tt("dve", self.mT.h[:, :, 2048:2064], pv[:, :, 0, :], pv[:, :, 2, :], ALU.add, ["prodS"],
                    [("mT", c, 4) for c in range(8)])
            out_tile(16, 2048, 16, "xs", "ys", 0)
            self.S.phase_end()


def _bucket_np(dist):
    n = np.maximum(dist, 0)
    nf = np.maximum(n, 1).astype(np.float32)
    v = np.log(nf / np.float32(16)) / np.float32(math.log(2048 / 16)) * np.float32(16)
    large = 16 + v.astype(np.int32)
    return np.where(n < 16, n, np.minimum(large, 31))


def _static_tables():
    pats = [(1, 128), (4, 128), (16, 128), (1, 127)]
    ohp = np.zeros((32, 4, 384), np.float32)
    ohs = np.zeros((32, 4, 128), np.float32)
    for p, (d, mx) in enumerate(pats):
        for x in range(383):
            delta = x - 127
            if 0 <= delta <= mx:
                ohp[_bucket_np(np.array(delta * d))[()], p, x] = 1.0
        for rho in range(128):
            steps = 128 - rho
            if steps <= mx:
                ohs[_bucket_np(np.array(steps * d))[()], p, rho] = 1.0
    masks = np.zeros((128, 1024), np.float32)
    for p in range(128):
        masks[p, 8 * (p % 64):8 * (p % 64) + 8] = 1.0
        masks[p, 512 + 4 * p:512 + 4 * p + 4] = 1.0
    return ohp.reshape(32, 4 * 384), ohs.reshape(32, 4 * 128), masks


_NC_CACHE = {}


def kernel(x_prompt, x_sample, mem_prompt, cache_a_k, cache_a_v, cache_b_k, cache_b_v, cache_mem_k, cache_mem_v,
           rel_bias, ln_g, w_in, gq_a, gk_a, gq_b, gk_b, gq_c, gk_c, sinks_b, mem_ln_g, w_mem_kv,
           w_br_a, w_br_b, w_br_c, w_out):
    f = lambda a: np.ascontiguousarray(np.asarray(a, dtype=np.float32))
    if "nc" not in _NC_CACHE:
        _NC_CACHE["nc"] = Builder().build()
    nc = _NC_CACHE["nc"]
    ohp, ohs, masks = _static_tables()
    shared = {"masks": masks, "rel_bias": f(rel_bias), "ln_g": f(ln_g), "w_in": f(w_in)[0], "gq_a": f(gq_a), "gk_a": f(gk_a),
              "gq_b": f(gq_b), "gk_b": f(gk_b), "gq_c": f(gq_c), "gk_c": f(gk_c), "sinks": f(sinks_b),
              "mem_ln_g": f(mem_ln_g), "w_mem_kv": f(w_mem_kv)[0], "w_br_a": f(w_br_a)[0], "w_br_b": f(w_br_b)[0],
              "w_br_c": f(w_br_c)[0], "w_out": f(w_out)[0], "ohp": ohp, "ohs": ohs}
    xp, xs, mp = f(x_prompt), f(x_sample), f(mem_prompt)
    cak, cav, cbk, cbv, cmk, cmv = (f(a)[0] for a in (cache_a_k, cache_a_v, cache_b_k, cache_b_v, cache_mem_k, cache_mem_v))
    in_maps = []
    for c in range(8):
        sl = slice(16 * c, 16 * c + 16)
        m = dict(shared)
        m.update({"x": xp[c], "xs": xs[sl, 0], "mem": mp[c],
                  "cak": cak[sl].reshape(16 * 2048, 512), "cav": cav[sl].reshape(16 * 2048, 512),
                  "cbk": cbk[sl].reshape(16 * 128, 128), "cbv": cbv[sl].reshape(16 * 128, 128),
                  "cmk": cmk[sl].reshape(16 * 256, 512), "cmv": cmv[sl].reshape(16 * 256, 512)})
        in_maps.append(m)
    res = run_bass_kernel_spmd(nc, in_maps, core_ids=list(range(8)))
    R = res.results
    cat = lambda k: np.stack([np.asarray(R[c][k], dtype=np.float32) for c in range(8)], 0)
    y = cat("y")
    ys = cat("ys").reshape(128, 1, 1024)
    pak = cat("pak").reshape(1, 8, 2048, 8, 64)
    pav = cat("pav").reshape(1, 8, 2048, 8, 64)
    pbk = cat("pbk").reshape(1, 8, 128, 2, 64)
    pbv = cat("pbv").reshape(1, 8, 128, 2, 64)
    pmk = cat("pmk").reshape(1, 8, 256, 4, 128)
    pmv = cat("pmv").reshape(1, 8, 256, 4, 128)
    sak = cat("sak").reshape(1, 128, 1, 8, 64)
    sav = cat("sav").reshape(1, 128, 1, 8, 64)
    sbk = cat("sbk").reshape(1, 128, 1, 2, 64)
    sbv = cat("sbv").reshape(1, 128, 1, 2, 64)
    return (y, ys, pak, pav, pbk, pbv, pmk, pmv, sak, sav, sbk, sbv)
```

```python
import math
from contextlib import ExitStack

import numpy as np
import concourse.bass as bass
import concourse.mybir as mybir
from concourse.bass_utils import run_bass_kernel_spmd

F32 = mybir.dt.float32
BF16 = mybir.dt.bfloat16
AF = mybir.ActivationFunctionType
ALU = mybir.AluOpType
AX = mybir.AxisListType

NT = 2064
EPS = 1e-6
IN_W = 7424
C_QA, C_KA, C_VA, C_ZA = 0, 512, 1024, 1536
C_QB, C_KB, C_VB, C_ZB = 2048, 2560, 2688, 2816
C_QC, C_ZC = 3328, 3840
C_GA = 4352
A_D = (1, 4, 16)


class Sched:
    ENGS = ("pe", "act", "dve", "pool", "sp")

    def __init__(self, nc, stack, dma_slots=None):
        self.nc = nc
        self.ops = []
        self.last_writer = {}
        self.readers = {}
        self.dma_slots = dma_slots or {"sp": 8, "pool": 6, "act": 4}
        self.sems = {}
        for e in ("pe", "act", "dve", "pool"):
            self.sems[e] = stack.enter_context(nc.semaphore("s_" + e))
        for q, n in self.dma_slots.items():
            for i in range(n):
                self.sems[(q, i)] = stack.enter_context(nc.semaphore(f"d_{q}{i}"))
        self.dma_count = {q: 0 for q in self.dma_slots}
        self.slot_last = {}
        self.emitted = 0
        self.sig_count = {k: 0 for k in self.sems}
        self.clock = {e: {} for e in self.ENGS}
        self.since_barrier_dma = []
        self.last_op_eng = {}
        self.excl = set()
        self.last_access = {}

    def add(self, eng, fn, reads=(), writes=(), dma=False, extra_deps=()):
        idx = len(self.ops)
        deps = set(extra_deps)
        for r in set(reads) | set(writes):
            if r in self.excl:
                d = self.last_access.get(r)
                if d is not None:
                    od = self.ops[d]
                    if dma or od["dma"] or od["eng"] != eng:
                        deps.add(d)
                self.last_access[r] = idx
        for r in reads:
            w = self.last_writer.get(r)
            if w is not None:
                deps.add(w)
        for r in writes:
            for d in [self.last_writer.get(r)] + self.readers.get(r, []):
                if d is None:
                    continue
                od = self.ops[d]
                if (not dma) and (not od["dma"]) and od["eng"] == eng:
                    continue
                deps.add(d)
        op = dict(eng=eng, fn=fn, dma=dma, deps=deps, sig=None, need_sig=dma, idx=idx)
        if dma:
            n = self.dma_count[eng]
            self.dma_count[eng] = n + 1
            slot = (eng, n % self.dma_slots[eng])
            prev = self.slot_last.get(slot)
            if prev is not None:
                deps.add(prev)
            self.slot_last[slot] = idx
            op["slot"] = slot
            self.since_barrier_dma.append(idx)
        for d in deps:
            self.ops[d]["need_sig"] = True
        for r in writes:
            self.last_writer[r] = idx
            self.readers[r] = []
        for r in reads:
            if r not in writes:
                self.readers.setdefault(r, []).append(idx)
        self.ops.append(op)
        if fn is not None and not dma:
            self.last_op_eng[eng] = idx
        return idx

    def barrier(self):
        deps = set(self.since_barrier_dma) | set(self.last_op_eng.values())
        self.since_barrier_dma = []
        for e in self.ENGS:
            self.add(e, None, extra_deps=deps)

    def emit(self):
        nc = self.nc
        lo, hi = self.emitted, len(self.ops)
        self.emitted = hi
        for op in self.ops[lo:hi]:
            if op["fn"] is None:
                continue
            if op["dma"]:
                k = op["slot"]
                self.sig_count[k] += 16
                op["sig"] = (k, self.sig_count[k])
            elif op["need_sig"]:
                k = op["eng"]
                self.sig_count[k] += 1
                op["sig"] = (k, self.sig_count[k])
        by_eng = {e: [] for e in self.ENGS}
        for op in self.ops[lo:hi]:
            e = op["eng"]
            clk = self.clock[e]
            wm = {}
            for d in sorted(op["deps"]):
                od = self.ops[d]
                if od["sig"] is None:
                    continue
                k, v = od["sig"]
                if clk.get(k, 0) < v:
                    wm[k] = max(wm.get(k, 0), v)
                    for kk, vv in od["clk"].items():
                        if clk.get(kk, 0) < vv:
                            clk[kk] = vv
            op["waits"] = list(wm.items())
            oc = dict(clk)
            if op["sig"] is not None:
                k, v = op["sig"]
                oc[k] = v
            op["clk"] = oc
            by_eng[e].append(op)

        def run(engobj, lst):
            for op in lst:
                for k, v in op["waits"]:
                    engobj.wait_ge(self.sems[k], v)
                if op["fn"] is None:
                    continue
                ins = op["fn"](engobj)
                if op["sig"] is not None:
                    ins.then_inc(self.sems[op["sig"][0]], 16 if op["dma"] else 1)

        with nc.Block() as block:
            if by_eng["pe"]:
                @block.tensor
                def _(t):
                    run(t, by_eng["pe"])
            if by_eng["act"]:
                @block.scalar
                def _(t):
                    run(t, by_eng["act"])
            if by_eng["dve"]:
                @block.vector
                def _(t):
                    run(t, by_eng["dve"])
            if by_eng["pool"]:
                @block.gpsimd
                def _(t):
                    run(t, by_eng["pool"])
            if by_eng["sp"]:
                @block.sync
                def _(t):
                    run(t, by_eng["sp"])
        for op in self.ops[lo:hi]:
            op["fn"] = None if op["fn"] is None else True

    def phase_end(self):
        self.barrier()
        self.emit()


class T:
    def __init__(self, h, shape):
        self.h = h
        self.shape = shape
        self.F = int(np.prod(shape[1:]))

    def ap(self, p0, n, col, dims):
        return bass.AP(self.h, p0 * self.F + col, [[self.F, n]] + [list(d) for d in dims])


class Rot:
    def __init__(self, ts, name):
        self.ts = ts
        self.name = name
        self.i = 0

    def next(self):
        j = self.i % len(self.ts)
        self.i += 1
        return self.ts[j], (self.name, j)


class RotL:
    def __init__(self, pairs):
        self.pairs = pairs
        self.i = 0

    def next(self):
        p = self.pairs[self.i % len(self.pairs)]
        self.i += 1
        return p


class Builder:
    def bank_rot(self, names):
        prs = []
        for nm in names:
            r = self.Q[nm]
            if isinstance(r, Rot):
                prs += [(t, (r.name, i)) for i, t in enumerate(r.ts)]
            else:
                prs.append((r, nm))
        return RotL(prs)

    def __init__(self):
        self.nc = bass.Bass("TRN2", target_bir_lowering=False)
        self.uid = 0
        self.only = None
        self.gate_stack = None

    def sb(self, st, shape, dt, name=None):
        self.uid += 1
        return T(st.enter_context(self.nc.sbuf_tensor(f"{name or 't'}_{self.uid}", shape, dt)), shape)

    def ps(self, st, shape, dt, name=None):
        self.uid += 1
        return T(st.enter_context(self.nc.psum_tensor(f"{name or 'p'}_{self.uid}", shape, dt)), shape)

    def rot(self, st, n, shape, dt, name):
        return Rot([self.sb(st, shape, dt, name) for _ in range(n)], name + str(self.uid))

    def dma(self, out, in_, r=(), w=(), q="sp", slow=False):
        if slow:
            self.S.add(q, lambda e, o=out, i=in_: e.dma_start(out=o, in_=i, allow_slow_non_contiguous=True), r, w, dma=True)
        else:
            self.S.add(q, lambda e, o=out, i=in_: e.dma_start(out=o, in_=i), r, w, dma=True)

    def mm(self, out, lhsT, rhs, start, stop, r, w, sgc=False, tp=None):
        if tp is None:
            self.S.add("pe", lambda e, o=out, l=lhsT, rr=rhs, a=start, b=stop, s=sgc:
                       e.matmul(o, l, rr, start=a, stop=b, skip_group_check=s), r, w)
        else:
            self.S.add("pe", lambda e, o=out, l=lhsT, rr=rhs, a=start, b=stop, s=sgc, t=tp:
                       e.matmul(o, l, rr, start=a, stop=b, skip_group_check=s, tile_position=t), r, w)

    def tr(self, out, in_, ident, r, w):
        self.S.add("pe", lambda e, o=out, i=in_, d=ident: e.transpose(out=o, in_=i, identity=d), r, w)

    def act(self, out, in_, func, r, w, scale=1.0, bias=None):
        if bias is None:
            self.S.add("act", lambda e, o=out, i=in_, f=func, s=scale: e.activation(out=o, in_=i, func=f, scale=s), r, w)
        else:
            self.S.add("act", lambda e, o=out, i=in_, f=func, s=scale, b=bias:
                       e.activation(out=o, in_=i, func=f, scale=s, bias=b), r, w)

    def acopy(self, out, in_, r, w):
        self.S.add("act", lambda e, o=out, i=in_: e.copy(out=o, in_=i), r, w)

    def amul(self, out, in_, mul, r, w):
        self.S.add("act", lambda e, o=out, i=in_, m=mul: e.mul(out=o, in_=i, mul=m), r, w)

    def tt(self, eng, out, in0, in1, op, r, w):
        self.S.add(eng, lambda e, o=out, a=in0, b=in1, p=op: e.tensor_tensor(out=o, in0=a, in1=b, op=p), r, w)

    def stt(self, eng, out, in0, scalar, in1, op0, op1, r, w):
        self.S.add(eng, lambda e, o=out, a=in0, s=scalar, b=in1, p0=op0, p1=op1:
                   e.scalar_tensor_tensor(out=o, in0=a, scalar=s, in1=b, op0=p0, op1=p1), r, w)

    def tsmul(self, eng, out, in0, scalar, r, w):
        self.S.add(eng, lambda e, o=out, a=in0, s=scalar: e.tensor_scalar_mul(out=o, in0=a, scalar1=s), r, w)

    def tcopy(self, eng, out, in_, r, w):
        self.S.add(eng, lambda e, o=out, i=in_: e.tensor_copy(out=o, in_=i), r, w)

    def recip(self, out, in_, r, w):
        self.S.add("dve", lambda e, o=out, i=in_: e.reciprocal(out=o, in_=i), r, w)

    def rsum(self, out, in_, r, w):
        self.S.add("dve", lambda e, o=out, i=in_: e.reduce_sum(out=o, in_=i, axis=AX.X), r, w)

    def memset(self, eng, ap, val, w):
        self.S.add(eng, lambda e, a=ap, v=val: e.memset(a, v), (), w)

    def asel(self, out, pattern, base, cm, r, w):
        self.S.add("pool", lambda e, o=out, p=pattern, b=base, c=cm: e.affine_select(
            out=o, in_=o, pattern=p, compare_op=ALU.not_equal, fill=1.0, base=b, channel_multiplier=c), r, w)

    def build(self):
        nc = self.nc

        def din(name, shape):
            return nc.dram_tensor(name, shape, F32, kind="ExternalInput")

        def dout(name, shape):
            return nc.dram_tensor(name, shape, F32, kind="ExternalOutput")

        D = self.D = {}
        for name, shape in [("x", [2048, 1024]), ("xs", [16, 1024]), ("mem", [256, 1024]),
                            ("cak", [16 * 2048, 512]), ("cav", [16 * 2048, 512]),
                            ("cbk", [16 * 128, 128]), ("cbv", [16 * 128, 128]),
                            ("cmk", [16 * 256, 512]), ("cmv", [16 * 256, 512]),
                            ("rel_bias", [32, 16]), ("ln_g", [1, 1024]), ("w_in", [1024, IN_W]),
                            ("gq_a", [1, 64]), ("gk_a", [1, 64]), ("gq_b", [1, 64]), ("gk_b", [1, 64]),
                            ("gq_c", [1, 128]), ("gk_c", [1, 128]), ("sinks", [1, 8]),
                            ("mem_ln_g", [1, 1024]), ("w_mem_kv", [1024, 1024]),
                            ("w_br_a", [512, 1024]), ("w_br_b", [512, 1024]), ("w_br_c", [512, 1024]),
                            ("w_out", [1024, 1024]), ("ohp", [32, 4 * 384]), ("ohs", [32, 4 * 128]), ("masks", [128, 1024])]:
            D[name] = din(name, shape)
        for name, shape in [("y", [2048, 1024]), ("ys", [16, 1024]), ("pak", [2048, 512]), ("pav", [2048, 512]),
                            ("pbk", [128, 128]), ("pbv", [128, 128]), ("pmk", [256, 512]), ("pmv", [256, 512]),
                            ("sak", [16, 512]), ("sav", [16, 512]), ("sbk", [16, 128]), ("sbv", [16, 128])]:
            D[name] = dout(name, shape)
        D["gS"] = nc.dram_tensor("gS", [64, 384], F32)
        D["qS"] = nc.dram_tensor("qS", [48, 512], F32)
        D["rD"] = nc.dram_tensor("rD", [20, 2048], F32)

        with ExitStack() as st:
            self.S = Sched(nc, st)
            P = self.P = {}
            P["hT"] = self.sb(st, [128, 8, NT], BF16, "hT")
            P["yT"] = self.sb(st, [128, 12, NT], BF16, "yT")
            P["identb"] = self.sb(st, [128, 128], BF16, "identb")
            P["Jf"] = self.sb(st, [128, 128], F32, "Jf")
            P["onesb"] = self.sb(st, [128, 128], BF16, "onesb")
            P["eps"] = self.sb(st, [128, 1], F32, "eps")
            P["kscA"] = self.sb(st, [128, 1], F32, "kscA")
            P["kscB"] = self.sb(st, [128, 1], F32, "kscB")
            P["kscC"] = self.sb(st, [128, 1], F32, "kscC")
            P["gkA"] = self.sb(st, [128, 64], F32, "gkA")
            P["gkB"] = self.sb(st, [128, 64], F32, "gkB")
            P["gkC"] = self.sb(st, [128, 128], F32, "gkC")
            P["gqsA"] = self.sb(st, [16, 64], F32, "gqsA")
            P["gqsB"] = self.sb(st, [16, 64], F32, "gqsB")
            P["gqsC"] = self.sb(st, [16, 128], F32, "gqsC")
            P["E"] = self.sb(st, [32, 16], F32, "E")
            P["e0"] = self.sb(st, [16, 16], F32, "e0")
            P["esink"] = self.sb(st, [16, 8], F32, "esink")
            P["sinkL"] = self.sb(st, [1, 8 * 128], F32, "sinkL")
            P["EBs"] = self.sb(st, [128, 4, 16], F32, "EBs")
            P["OHB"] = self.sb(st, [128, 16 * 16], BF16, "OHB")
            P["qsA"] = self.sb(st, [16, 512], F32, "qsA")
            P["qsB"] = self.sb(st, [16, 512], F32, "qsB")
            P["qsC"] = self.sb(st, [16, 512], F32, "qsC")
            P["ksA"] = self.sb(st, [16, 512], F32, "ksA")
            P["vsA"] = self.sb(st, [16, 512], F32, "vsA")
            P["ksB"] = self.sb(st, [16, 128], F32, "ksB")
            P["vsB"] = self.sb(st, [16, 128], F32, "vsB")
            Q = self.Q = {}
            Q["PJ"] = Rot([self.ps(st, [128, 512], F32, "PJ") for _ in range(2)], "PJ")
            Q["TRb"] = self.ps(st, [128, 1024], BF16, "TRb")
            Q["S"] = Rot([self.ps(st, [128, 512], F32, "S") for _ in range(2)], "S")
            Q["ACC"] = Rot([self.ps(st, [128, 512], F32, "ACC") for _ in range(2)], "ACC")
            Q["O3"] = self.ps(st, [128, 512], F32, "O3")
            for nm in ("PJ", "S", "ACC"):
                for i in range(2):
                    self.S.excl.add((Q[nm].name, i))
            self.S.excl |= {"O3", "TRb"}

            phases = [("setup", self.phase_setup), ("norm", self.phase_norm), ("A0", lambda: self.phase_A(0)),
                      ("A1", lambda: self.phase_A(1)), ("B", self.phase_B), ("C", self.phase_C),
                      ("tail", self.phase_tail_all)]
            for nm, fn in phases:
                if self.only is not None and nm not in self.only:
                    continue
                fn()
        return nc

    def phase_setup(self):
        P, D, Q = self.P, self.D, self.Q
        with ExitStack() as st:
            tmpf = self.sb(st, [128, 128], F32, "tmpf")
            rb = self.sb(st, [32, 16], F32, "rb")
            ohp = self.sb(st, [32, 4 * 384], F32, "ohp")
            ohs = self.sb(st, [32, 4 * 128], F32, "ohs")
            gv = self.rot(st, 2, [16, 384], F32, "gv")
            self.memset("pool", tmpf.h[:], 0.0, ["tmpf"])
            self.asel(tmpf.h[:], [[-1, 128]], 0, 1, ["tmpf"], ["tmpf"])
            self.tcopy("pool", P["identb"].h[:], tmpf.h[:], ["tmpf"], ["identb"])
            self.memset("pool", P["Jf"].h[:], 0.0, ["Jf"])
            self.asel(P["Jf"].h[:], [[1, 128]], -127, 1, ["Jf"], ["Jf"])
            self.memset("pool", P["onesb"].h[:], 1.0, ["onesb"])
            self.memset("pool", P["eps"].h[:], EPS, ["eps"])
            ohbf = self.sb(st, [128, 256], F32, "ohbf")
            self.memset("pool", ohbf.h[:], 0.0, ["ohbf"])
            self.asel(ohbf.h[:].rearrange("p (b m) -> p b m", m=16), [[1, 16], [-1, 16]], 0, 0, ["ohbf"], ["ohbf"])
            self.tcopy("pool", P["OHB"].h[:], ohbf.h[:], ["ohbf"], ["OHB"])
            self.S.barrier()
            for nm, src, n in (("kscA", "gq_a", 64), ("kscB", "gq_b", 64)):
                for half in range(2):
                    self.dma(P[nm].h[64 * half:64 * half + 64, :], bass.AP(D[src], 0, [[1, 64], [1, 1]]), (), [nm])
                self.tsmul("dve", P[nm].h[:], P[nm].h[:], 0.125, [nm], [nm])
            self.dma(P["kscC"].h[:, :], bass.AP(D["gq_c"], 0, [[1, 128], [1, 1]]), (), ["kscC"])
            self.tsmul("dve", P["kscC"].h[:], P["kscC"].h[:], 128 ** -0.5, ["kscC"], ["kscC"])
            for nm, src, n, np_ in (("gkA", "gk_a", 64, 128), ("gkB", "gk_b", 64, 128), ("gkC", "gk_c", 128, 128),
                                    ("gqsA", "gq_a", 64, 16), ("gqsB", "gq_b", 64, 16), ("gqsC", "gq_c", 128, 16)):
                self.dma(P[nm].h[:], bass.AP(D[src], 0, [[0, np_], [1, n]]), (), [nm])
            self.tsmul("dve", P["gqsA"].h[:], P["gqsA"].h[:], 0.125, ["gqsA"], ["gqsA"])
            self.tsmul("dve", P["gqsB"].h[:], P["gqsB"].h[:], 0.125, ["gqsB"], ["gqsB"])
            self.tsmul("dve", P["gqsC"].h[:], P["gqsC"].h[:], 128 ** -0.5, ["gqsC"], ["gqsC"])
            self.dma(rb.h[:], D["rel_bias"].ap(), (), ["rb"])
            self.act(P["E"].h[:], rb.h[:], AF.Exp, ["rb"], ["E"])
            self.dma(P["e0"].h[:], bass.AP(D["rel_bias"], 0, [[0, 16], [1, 16]]), (), ["e0"])
            self.act(P["e0"].h[:], P["e0"].h[:], AF.Exp, ["e0"], ["e0"])
            self.dma(P["esink"].h[:], bass.AP(D["sinks"], 0, [[0, 16], [1, 8]]), (), ["esink"])
            self.act(P["esink"].h[:], P["esink"].h[:], AF.Exp, ["esink"], ["esink"])
            self.memset("dve", P["sinkL"].h[:], 0.0, ["sinkL"])
            for h in range(8):
                lo = 64 if h % 2 == 0 else 0
                self.tcopy("dve", P["sinkL"].ap(0, 1, h * 128 + lo, [[1, 64]]), P["esink"].ap(0, 1, h, [[0, 64]]),
                           ["esink", "sinkL"], ["sinkL"])
            self.dma(ohp.h[:], D["ohp"].ap(), (), ["ohp"])
            self.dma(ohs.h[:], D["ohs"].ap(), (), ["ohs"])
            for p in range(4):
                pj, pjt = Q["PJ"].next()
                self.mm(pj.h[0:16, 0:384], P["E"].h[:, :], ohp.h[:, p * 384:(p + 1) * 384], True, True, ["E", "ohp"], [pjt])
                g, gt = gv.next()
                self.acopy(g.h[:], pj.h[0:16, 0:384], [pjt], [gt])
                self.dma(D["gS"].ap()[p * 16:(p + 1) * 16, :], g.h[:], [gt], ["gS"])
                pj, pjt = Q["PJ"].next()
                self.mm(pj.h[:, 0:16], ohs.h[:, p * 128:(p + 1) * 128], P["E"].h[:, :], True, True, ["E", "ohs"], [pjt])
                self.acopy(P["EBs"].h[:, p, :], pj.h[:, 0:16], [pjt], ["EBs"])
            self.S.phase_end()

    def norm_tiles(self, st, jobs, deep=True):
        P, D, Q = self.P, self.D, self.Q
        xt = self.rot(st, 3 if deep else 2, [128, 1024], F32, "xt")
        sq = self.rot(st, 2 if deep else 1, [128, 1024], F32, "sq")
        xb = self.rot(st, 3 if deep else 2, [128, 1024], BF16, "xb")
        s1 = self.rot(st, 4, [128, 4], F32, "s1")

        def stages(job):
            (src, r0, dstT, dst, g, gtok, col0, n) = job
            x_, xtok = xt.next()
            q_, qtok = sq.next()
            b_, btok = xb.next()
            s_, stok = s1.next()

            def A():
                self.dma(x_.h[0:n, :], D[src].ap()[r0:r0 + n, :], (), [xtok])
                self.act(q_.h[0:n, :], x_.h[0:n, :], AF.Square, [xtok], [qtok])
                self.rsum(s_.h[0:n, 0:1], q_.h[0:n, :], [qtok], [stok])

            def B():
                self.act(s_.h[0:n, 1:2], s_.h[0:n, 0:1], AF.Sqrt, [stok], [stok], scale=1.0 / 1024, bias=P["eps"].h[0:n, :])
                self.recip(s_.h[0:n, 2:3], s_.h[0:n, 1:2], [stok], [stok])
                self.stt("dve", b_.h[0:n, :], x_.h[0:n, :], s_.h[0:n, 2:3], g.h[0:n, :], ALU.mult, ALU.mult,
                         [xtok, stok, gtok], [btok])

            def C():
                for kc in range(8):
                    self.tr(Q["TRb"].h[:, kc * 128:kc * 128 + n], b_.h[0:n, kc * 128:(kc + 1) * 128],
                            P["identb"].h[0:n, 0:n], [btok, "identb"], ["TRb"])
                self.acopy(dstT.h[:, :, col0:col0 + n],
                           Q["TRb"].h[:, :].rearrange("p (k t) -> p k t", t=128)[:, :, 0:n], ["TRb"], [(dst, col0)])
            return [A, B, C]

        pipe = []
        for i in range(len(jobs) + 2):
            if i < len(jobs):
                pipe.append(stages(jobs[i]))
            for stg in pipe:
                if stg:
                    stg.pop(0)()
            pipe = [p_ for p_ in pipe if p_]

    def phase_norm(self):
        P, D, Q = self.P, self.D, self.Q
        with ExitStack() as st:
            gln = self.sb(st, [128, 1024], F32, "gln")
            self.dma(gln.h[:], bass.AP(D["ln_g"], 0, [[0, 128], [1, 1024]]), (), ["gln"])
            jobs = [("x", 128 * i, P["hT"], "hT", gln, "gln", 128 * i, 128) for i in range(16)]
            jobs.append(("xs", 0, P["hT"], "hT", gln, "gln", 2048, 16))
            self.norm_tiles(st, jobs)
            self.S.phase_end()

    def load_w(self, W, wtok, src, c0, width, o0, part=0):
        if not hasattr(self, "wparts"):
            self.wparts = {}
        self.wparts.setdefault(wtok, set()).add(part)
        v = self.D[src].ap().rearrange("(kc p) n -> p kc n", p=128)
        for g in range(4):
            self.dma(W.h[:, 2 * g:2 * g + 2, o0:o0 + width], v[:, 2 * g:2 * g + 2, c0:c0 + width], (), [(wtok, part, g)], q="pool")

    def wrd(self, wtok, kc):
        return [(wtok, p, kc // 2) for p in sorted(self.wparts[wtok])]

    def proj_tm(self, pj, pjt, hTname, col0, n, W, wtok, width, tokens=None):
        hT = self.P[hTname]
        for kc in range(8):
            if tokens is None:
                l = hT.h[:, kc, col0:col0 + n]
                rd = [(hTname, col0)] + self.wrd(wtok, kc)
            else:
                start, step = tokens
                l = hT.ap(0, 128, kc * hT.shape[2] + start, [[step, n]])
                rd = [(hTname, 128 * i) for i in range(16)] + self.wrd(wtok, kc)
            self.mm(pj.h[0:n, 0:width], l, W.h[:, kc, 0:width], kc == 0, kc == 7, rd, [pjt])

    def rms_stats_a(self, pj, pjt, n, c0, nh, hd, sq, ss):
        q_, qtok = sq.next()
        s_, stok = ss.next()
        w = nh * hd
        self.act(q_.h[0:n, 0:w], pj.h[0:n, c0:c0 + w], AF.Square, [pjt], [qtok])
        self.rsum(s_.h[0:n, 0:nh], q_.h[0:n, 0:w].rearrange("p (h e) -> p h e", e=hd), [qtok], [stok])
        return s_, stok

    def rms_stats_b(self, s_, stok, n, nh, hd):
        self.act(s_.h[0:n, 8:8 + nh], s_.h[0:n, 0:nh], AF.Sqrt, [stok], [stok], scale=1.0 / hd, bias=self.P["eps"].h[0:n, :])
        self.recip(s_.h[0:n, 16:16 + nh], s_.h[0:n, 8:8 + nh], [stok], [stok])

    def rms_stats(self, pj, pjt, n, c0, nh, hd, sq, ss):
        q_, qtok = sq.next()
        s_, stok = ss.next()
        w = nh * hd
        self.act(q_.h[0:n, 0:w], pj.h[0:n, c0:c0 + w], AF.Square, [pjt], [qtok])
        self.rsum(s_.h[0:n, 0:nh], q_.h[0:n, 0:w].rearrange("p (h e) -> p h e", e=hd), [qtok], [stok])
        self.act(s_.h[0:n, 8:8 + nh], s_.h[0:n, 0:nh], AF.Sqrt, [stok], [stok], scale=1.0 / hd, bias=self.P["eps"].h[0:n, :])
        self.recip(s_.h[0:n, 16:16 + nh], s_.h[0:n, 8:8 + nh], [stok], [stok])
        return s_, stok

    @staticmethod
    def bc_heads(t, n, col, nh, hd):
        return t.ap(0, n, col, [[1, nh], [0, hd]])

    def run_groups(self, groups, Pbuf, Pmbuf, skew=2):
        P = self.P
        Sr = self.bank_rot(["S", "PJ"])
        pendq = []

        def do_pv(item):
            pg, Pm_, pmtok_ = item
            pmtoks = [pmtok_ + (ui,) for ui in range(len(pg["units"]))]
            if pg.get("pre_sink"):
                A_, atok, h = pg["pre_sink"]
                self.mm(A_.h[:, :], P["sinkL"].ap(0, 1, h * 128, [[1, 128]]), P["ones_f"].h[0:1, :], True, False,
                        ["sinkL", "ones_f"], [atok], sgc=True)
            for s_i, t in enumerate(pg["tiles"]):
                self.mm(t["out"], t["v"], Pm_.h[:, 128 * s_i:128 * (s_i + 1)], t["start"], False,
                        pmtoks + t["vrd"], [pg["acctok"]], sgc=True)
            if pg.get("post"):
                pg["post"]()

        for g in groups:
            Sb, stok = Sr.next()
            nt = len(g["tiles"])
            for s_i, t in enumerate(g["tiles"]):
                self.mm(Sb.h[:, 128 * s_i:128 * (s_i + 1)], t["k"], t["q"], True, True, t["rd"], [stok])
            Pt, ptok = Pbuf.next()
            Pm, pmtok = Pmbuf.next()
            self.act(Pt.h[:, 0:128 * nt], Sb.h[:, 0:128 * nt], AF.Exp, [stok], [ptok])
            for ui, (c0, w, eb, ebtok) in enumerate(g["units"]):
                self.mmcnt = getattr(self, "mmcnt", 0) + 1
                self.tt("pool" if self.mmcnt % 4 == 0 else "dve", Pm.h[:, c0:c0 + w], Pt.h[:, c0:c0 + w], eb, ALU.mult,
                        [ptok, ebtok], [pmtok + (ui,)])
            pendq.append((g, Pm, pmtok))
            if len(pendq) > skew:
                do_pv(pendq.pop(0))
        while pendq:
            do_pv(pendq.pop(0))

    def build_EB(self, st, EB, ebname, combos):
        P, D, Q = self.P, self.D, self.Q
        R = self.rot(st, 1, [128, 256], F32, "R")
        for (idx, row) in combos:
            r_, rtok = R.next()
            self.dma(r_.h[:], bass.AP(D["gS"], row * 384, [[1, 128], [1, 256]]), ["gS"], [rtok])
            pj, pjt = Q["PJ"].next()
            self.mm(pj.h[:, 0:256], P["Jf"].h[:, :], r_.h[:, :], True, True, ["Jf", rtok], [pjt])
            self.acopy(EB.h[:, idx, :], pj.h[:, 0:256], [pjt], [(ebname, idx)])

    def phase_A(self, hf):
        P, D, Q = self.P, self.D, self.Q
        with ExitStack() as st:
            Wqk = self.sb(st, [128, 8, 512], BF16, "Wqk")
            Wv = self.sb(st, [128, 8, 256], BF16, "Wv")
            qT = self.sb(st, [128, 2, NT], BF16, "qTA")
            kT = self.sb(st, [128, 2, NT], BF16, "kTA")
            Vst = self.sb(st, [128, 48 * 384], BF16, "Vst")
            EB = self.sb(st, [128, 12, 256], F32, "EBA")
            acc3 = self.rot(st, 1, [128, 2048], F32, "acc3")
            sq = self.rot(st, 2, [128, 512], F32, "sqA")
            ss = self.rot(st, 4, [128, 24], F32, "ssA")
            qb = self.rot(st, 4, [128, 256], BF16, "qbA")
            kn = self.rot(st, 2, [128, 256], F32, "knA")
            ko = self.rot(st, 3, [128, 256], F32, "koA")
            kb = self.rot(st, 3, [128, 256], BF16, "kbA")
            vo = self.rot(st, 2, [128, 256], F32, "voA")
            rsm = self.rot(st, 4, [128, 16], F32, "rsmA")
            dcp = self.rot(st, 1, [128, 512], F32, "dcpA")
            maskD = self.sb(st, [128, 512], F32, "maskD")
            self.dma(maskD.h[:, :], D["masks"].ap()[:, 0:512], (), ["maskD"])

            self.load_w(Wqk, "Wqk", "w_in", C_QA + 256 * hf, 256, 0)
            self.load_w(Wqk, "Wqk", "w_in", C_KA + 256 * hf, 256, 256, part=1)
            self.load_w(Wv, "Wv", "w_in", C_VA + 256 * hf, 256, 0)
            self.memset("pool", Vst.ap(0, 128, 64, [[192, 96], [1, 64]]), 1.0, ["Vones"])

            def vdst(arr, ti):
                return Vst.ap(0, 128, (arr * 16 + ti) * 384, [[192, 2], [128, 2], [1, 64]])

            def vsrc(pj):
                return pj.h[:, 0:256].rearrange("p (a b e) -> p a b e", b=2, e=64)
            self.build_EB(st, EB, "EBA", [(hl * 3 + p, p * 16 + (4 * hf + hl)) for hl in range(4) for p in range(3)])

            import os
            tiles = [(128 * i, 128) for i in range(16)] + [(2048, 16)]
            tiles = tiles[:int(os.environ.get("DBG_TILES", "17"))]
            PJr = self.bank_rot(["PJ", "S", "ACC"])

            def tile_stages(ti, col0, n):
                samp = (ti == 16)
                X = {}

                def SA():
                    pj, pjt = PJr.next()
                    self.proj_tm(pj, pjt, "hT", col0, n, Wqk, "Wqk", 512)
                    X["pj"] = (pj, pjt)
                    X["s"] = self.rms_stats_a(pj, pjt, n, 0, 8, 64, sq, ss)
                    pv, pvt = PJr.next()
                    self.proj_tm(pv, pvt, "hT", col0, n, Wv, "Wv", 256)
                    if samp:
                        self.acopy(P["vsA"].h[0:16, 256 * hf:256 * hf + 256], pv.h[0:16, 0:256], [pvt], [("vsA", hf)])
                        self.dma(D["sav"].ap()[0:16, 256 * hf:256 * hf + 256], P["vsA"].h[0:16, 256 * hf:256 * hf + 256],
                                 [("vsA", hf)], [])
                    else:
                        v_, vtok = vo.next()
                        self.acopy(v_.h[0:n, :], pv.h[0:n, 0:256], [pvt], [vtok])
                        self.dma(D["pav"].ap()[col0:col0 + n, 256 * hf:256 * hf + 256], v_.h[0:n, :], [vtok], [])
                        self.tcopy("dve", vdst(0, ti), vsrc(pv), [pvt], [("V", 0, ti)])

                def SB1():
                    pj, pjt = X["pj"]
                    s_, stok = X["s"]
                    self.rms_stats_b(s_, stok, n, 8, 64)
                    q_, qtok = qb.next()
                    X["q"] = (q_, qtok)
                    self.tt("dve", q_.h[0:n, :].rearrange("p (h e) -> p h e", e=64),
                            pj.h[0:n, 0:256].rearrange("p (h e) -> p h e", e=64),
                            self.bc_heads(s_, n, 16, 4, 64), ALU.mult, [pjt, stok], [qtok])
                    n_, ntok = kn.next()
                    self.tt("dve", n_.h[0:n, :].rearrange("p (h e) -> p h e", e=64),
                            pj.h[0:n, 256:512].rearrange("p (h e) -> p h e", e=64),
                            self.bc_heads(s_, n, 20, 4, 64), ALU.mult, [pjt, stok], [ntok])
                    if samp:
                        self.tt("dve", P["qsA"].h[0:16, 256 * hf:256 * hf + 256].rearrange("p (h e) -> p h e", e=64),
                                pj.h[0:16, 0:256].rearrange("p (h e) -> p h e", e=64),
                                self.bc_heads(s_, 16, 16, 4, 64), ALU.mult, [pjt, stok], [("qsA", hf)])
                        self.tt("pool", P["qsA"].h[0:16, 256 * hf:256 * hf + 256].rearrange("p (h e) -> p h e", e=64),
                                P["qsA"].h[0:16, 256 * hf:256 * hf + 256].rearrange("p (h e) -> p h e", e=64),
                                P["gqsA"].ap(0, 16, 0, [[0, 4], [1, 64]]), ALU.mult, [("qsA", hf), "gqsA"], [("qsA", hf)])
                        otok = ("ksA", hf)
                        oap = P["ksA"].h[0:16, 256 * hf:256 * hf + 256]
                    else:
                        o_, otok = ko.next()
                        oap = o_.h[0:n, :]
                    X["o"] = (oap, otok)
                    self.tt("pool", oap.rearrange("p (h e) -> p h e", e=64), n_.h[0:n, :].rearrange("p (h e) -> p h e", e=64),
                            P["gkA"].ap(0, n, 0, [[0, 4], [1, 64]]), ALU.mult, [ntok, "gkA"], [otok])
                    dst = D["sak"].ap()[0:16, 256 * hf:256 * hf + 256] if samp else D["pak"].ap()[col0:col0 + n, 256 * hf:256 * hf + 256]
                    self.dma(dst, oap, [otok], [])

                def SB2():
                    oap, otok = X["o"]
                    b_, btok = kb.next()
                    X["b"] = (b_, btok)
                    self.acopy(b_.h[0:n, :], oap, [otok], [btok])

                def SC():
                    q_, qtok = X["q"]
                    b_, btok = X["b"]
                    for j in range(2):
                        self.tr(Q["TRb"].h[:, j * 128:j * 128 + n], q_.h[0:n, j * 128:(j + 1) * 128], P["identb"].h[0:n, 0:n],
                                [qtok, "identb"], ["TRb"])
                    for j in range(2):
                        self.tr(Q["TRb"].h[:, (2 + j) * 128:(2 + j) * 128 + n], b_.h[0:n, j * 128:(j + 1) * 128],
                                P["identb"].h[0:n, 0:n], [btok, "identb"], ["TRb"])
                    trv = Q["TRb"].h[:, 0:512].rearrange("p (k t) -> p k t", t=128)
                    self.acopy(qT.h[:, :, col0:col0 + n], trv[:, 0:2, 0:n], ["TRb"], [("qTA", ti)])
                    self.tsmul("dve", kT.h[:, :, col0:col0 + n], trv[:, 2:4, 0:n], P["kscA"].h[:, 0:1], ["TRb", "kscA"], [("kTA", ti)])
                return [SA, SB1, SB2, SC]

            pipe = []

            def step(newtile=None):
                nonlocal pipe
                if newtile is not None:
                    pipe.append(tile_stages(*newtile))
                for stg in pipe:
                    if stg:
                        stg.pop(0)()
                pipe = [p_ for p_ in pipe if p_]
            for ti, (col0, n) in enumerate(tiles):
                step((ti, col0, n))
            defer = []
            for arr in (1, 2):
                for ti in range(int(os.environ.get("DBG_VARR", "16"))):
                    if pipe and ti % 2 == 1:
                        step()
                    if arr == 1:
                        tb, r = ti // 4, ti % 4
                        tokens = (512 * tb + r, 4)
                    else:
                        tokens = (ti, 16)
                    pj, pjt = PJr.next()
                    self.proj_tm(pj, pjt, "hT", 0, 128, Wv, "Wv", 256, tokens=tokens)
                    if ti % 2 == 0:
                        self.acopy(vdst(arr, ti), vsrc(pj), [pjt], [("V", arr, ti)])
                    else:
                        self.tcopy("dve", vdst(arr, ti), vsrc(pj), [pjt], [("V", arr, ti)])

            while pipe:
                step()
            self.S.barrier()
            Pb = RotL([(T(Wqk.h[:, 2 * i:2 * i + 2, :].rearrange("p a b -> p (a b)").bitcast(F32), [128, 512]), ("PbA", i))
                       for i in range(4)])
            Pm = RotL([(T(Wv.h[:, 2 * i:2 * i + 2, :].rearrange("p a b -> p (a b)"), [128, 512]), ("PmA", i)) for i in range(4)])
            allq = [("qTA", i) for i in range(16)]
            allk = [("kTA", i) for i in range(16)]

            def vaug(arr, ti, hl):
                base = (arr * 16 + ti) * 384 + (hl // 2) * 192 + (0 if hl % 2 == 0 else 64)
                return Vst.ap(0, 128, base, [[1, 128]])

            groups = []
            for hl in range(4):
                hp, pr = hl % 2, hl // 2
                p0 = 64 * hp
                a3, a3tok = acc3.next()
                for rg in range(4):
                    tl = []
                    for k in range(4):
                        r = 4 * rg + k
                        tl.append(dict(k=kT.ap(p0, 64, pr * NT + r, [[16, 128]]), q=qT.ap(p0, 64, pr * NT + r, [[16, 128]]),
                                       v=vaug(2, r, hl), out=Q["O3"].h[:, 128 * k:128 * (k + 1)], start=True,
                                       rd=allq + allk, vrd=[("V", 2, r), "Vones"]))
                    ebap = EB.ap(0, 128, (hl * 3 + 2) * 256, [[0, 4], [1, 128]])

                    def post3(rg=rg, a3=a3, a3tok=a3tok):
                        self.acopy(a3.ap(0, 128, 4 * rg, [[1, 4], [16, 128]]),
                                   Q["O3"].h[:, :].rearrange("p (k i) -> p k i", i=128), ["O3"], [a3tok])
                    groups.append(dict(tiles=tl, units=[(0, 512, ebap, ("EBA", hl * 3 + 2))], post=post3, acctok="O3",
                                       ebview=True))
                for tb in range(4):
                    A_, atok = Q["ACC"].next()
                    units = []
                    first = True
                    for n_ in range(4 * tb, 4 * tb + 4):
                        tl = [dict(k=kT.ap(p0, 64, pr * NT + 128 * n_, [[1, 128]]), q=qT.ap(p0, 64, pr * NT + 128 * n_, [[1, 128]]),
                                   v=vaug(0, n_, hl), out=A_.h[:, 128 * (n_ - 4 * tb):128 * (n_ - 4 * tb + 1)], start=first,
                                   rd=[("qTA", n_), ("kTA", n_)], vrd=[("V", 0, n_), "Vones"])]
                        first = False
                        if n_ > 0:
                            tl.append(dict(k=kT.ap(p0, 64, pr * NT + 128 * (n_ - 1), [[1, 128]]), q=tl[0]["q"],
                                           v=vaug(0, n_ - 1, hl), out=tl[0]["out"], start=False,
                                           rd=[("qTA", n_), ("kTA", n_ - 1)], vrd=[("V", 0, n_ - 1), "Vones"]))
                        units.append((tl, (hl * 3 + 0)))
                    for r in range(4):
                        qa = qT.ap(p0, 64, pr * NT + 512 * tb + r, [[4, 128]])
                        oa = A_.ap(0, 128, r, [[4, 128]])
                        tl = [dict(k=kT.ap(p0, 64, pr * NT + 512 * tb + r, [[4, 128]]), q=qa, v=vaug(1, 4 * tb + r, hl),
                                   out=oa, start=False, rd=allq + allk, vrd=[("V", 1, 4 * tb + r), "Vones"])]
                        if tb > 0:
                            tl.append(dict(k=kT.ap(p0, 64, pr * NT + 512 * (tb - 1) + r, [[4, 128]]), q=qa,
                                           v=vaug(1, 4 * (tb - 1) + r, hl), out=oa, start=False, rd=allq + allk,
                                           vrd=[("V", 1, 4 * (tb - 1) + r), "Vones"]))
                        units.append((tl, (hl * 3 + 1)))
                    cur_t, cur_u = [], []
                    packed = []
                    for (tl, ebi) in units:
                        if len(cur_t) + len(tl) > 4:
                            packed.append((cur_t, cur_u))
                            cur_t, cur_u = [], []
                        c0 = 128 * len(cur_t)
                        cur_u.append((c0, 128 * len(tl), EB.ap(0, 128, ebi * 256, [[1, 128 * len(tl)]]), ("EBA", ebi)))
                        cur_t = cur_t + tl
                    packed.append((cur_t, cur_u))

                    def postA(A_=A_, atok=atok, a3=a3, a3tok=a3tok, tb=tb, hp=hp, pr=pr):
                        nr, dr = (0, 64) if hp == 0 else (64, 0)
                        self.tt("dve", A_.h[:, :], A_.h[:, :], a3.h[:, 512 * tb:512 * tb + 512], ALU.add, [atok, a3tok], [atok])
                        self.acopy(P["yT"].ap(nr, 64, (2 * hf + pr) * NT + 512 * tb, [[1, 512]]), A_.h[nr:nr + 64, :], [atok],
                                   [("yT", 2 * hf + pr, tb, hp)])
                        rs_, rstok = rsm.next()
                        dc_, dctok = dcp.next()
                        self.acopy(dc_.h[dr:dr + 64, :], A_.h[dr:dr + 64, :], [atok], [dctok])
                        self.tt("pool", dc_.h[dr:dr + 64, :], dc_.h[dr:dr + 64, :], maskD.h[dr:dr + 64, :], ALU.mult, [dctok, "maskD"], [dctok])
                        self.S.add("dve", lambda e, o=rs_.h[dr:dr + 64, 8:16], i=dc_.h[dr:dr + 64, :].rearrange("p (c j) -> p j c", j=8):
                                   e.reduce_sum(out=o, in_=i, axis=AX.X), [dctok], [rstok])
                        self.recip(rs_.h[dr:dr + 64, 0:8], rs_.h[dr:dr + 64, 8:16], [rstok], [rstok])
                        hg = 4 * hf + 2 * pr + hp
                        self.dma(bass.AP(D["rD"], hg * 2048 + 512 * tb, [[8, 64], [1, 8]]), rs_.h[dr:dr + 64, 0:8], [rstok],
                                 [("rD", hg, tb)])
                    for gi, (tl, ul) in enumerate(packed):
                        groups.append(dict(tiles=tl, units=ul, post=postA if gi == len(packed) - 1 else None, acctok=atok))
            for g in groups:
                if g.get("ebview"):
                    ebi = g["units"][0][3][1]
                    g["units"] = [(128 * k, 128, EB.ap(0, 128, ebi * 256, [[1, 128]]), ("EBA", ebi)) for k in range(4)]
            import os
            lim = os.environ.get("DBG_GROUPS")
            if lim is not None:
                groups = groups[:int(lim)]
            self.run_groups(groups, Pb, Pm)
            self.S.phase_end()

    def phase_B(self):
        P, D, Q = self.P, self.D, self.Q
        with ExitStack() as st:
            Wq = self.sb(st, [128, 8, 512], BF16, "WqB")
            Wkv = self.sb(st, [128, 8, 256], BF16, "WkvB")
            qT = self.sb(st, [128, 4, NT], BF16, "qTB")
            kT = self.sb(st, [128, 2, NT], BF16, "kTB")
            VB = self.sb(st, [128, 16, 320], BF16, "VB")
            EB = self.sb(st, [128, 8, 256], F32, "EBB")
            sq = self.rot(st, 3, [128, 512], F32, "sqB")
            ss = self.rot(st, 6, [128, 24], F32, "ssB")
            qb = self.rot(st, 4, [128, 512], BF16, "qbB")
            kn = self.rot(st, 2, [128, 128], F32, "knB")
            ko = self.rot(st, 3, [128, 128], F32, "koB")
            kb = self.rot(st, 4, [128, 256], BF16, "kbB")
            vo = self.rot(st, 2, [128, 128], F32, "voB")
            rsm = self.rot(st, 4, [128, 16], F32, "rsmB")
            dcp = self.rot(st, 2, [128, 512], F32, "dcpB")
            maskD = self.sb(st, [128, 512], F32, "maskD")
            self.dma(maskD.h[:, :], D["masks"].ap()[:, 0:512], (), ["maskD"])
            P["ones_f"] = self.sb(st, [1, 512], F32, "ones_f")
            self.memset("pool", P["ones_f"].h[:], 1.0, ["ones_f"])
            self.load_w(Wq, "WqB", "w_in", C_QB, 512, 0)
            self.load_w(Wkv, "WkvB", "w_in", C_KB, 256, 0)
            for c in (0, 128, 256):
                self.memset("pool", VB.h[:, :, c:c + 64], 1.0, ["VBones"])
            self.build_EB(st, EB, "EBB", [(h, 3 * 16 + 8 + h) for h in range(8)])
            tiles = [(128 * i, 128) for i in range(16)] + [(2048, 16)]
            PJr = self.bank_rot(["PJ", "S", "ACC"])

            def tile_stages(ti, col0, n):
                samp = (ti == 16)
                X = {}

                def SA():
                    pj, pjt = PJr.next()
                    self.proj_tm(pj, pjt, "hT", col0, n, Wq, "WqB", 512)
                    X["pq"] = (pj, pjt)
                    X["sq"] = self.rms_stats_a(pj, pjt, n, 0, 8, 64, sq, ss)
                    pk, pkt = PJr.next()
                    self.proj_tm(pk, pkt, "hT", col0, n, Wkv, "WkvB", 256)
                    X["pk"] = (pk, pkt)
                    X["sk"] = self.rms_stats_a(pk, pkt, n, 0, 2, 64, sq, ss)
                    if samp:
                        self.acopy(P["vsB"].h[0:16, :], pk.h[0:16, 128:256], [pkt], ["vsB"])
                        self.dma(D["sbv"].ap(), P["vsB"].h[0:16, :], ["vsB"], [])
                    else:
                        if ti == 15:
                            v_, vtok = vo.next()
                            self.acopy(v_.h[:, :], pk.h[:, 128:256], [pkt], [vtok])
                            self.dma(D["pbv"].ap(), v_.h[:, :], [vtok], [])
                        self.tcopy("dve", VB.ap(0, 128, ti * 320 + 64, [[128, 2], [1, 64]]),
                                   pk.h[:, 128:256].rearrange("p (g e) -> p g e", e=64), [pkt], [("VB", ti)])

                def SB1():
                    pj, pjt = X["pq"]
                    s_, stok = X["sq"]
                    self.rms_stats_b(s_, stok, n, 8, 64)
                    q_, qtok = qb.next()
                    X["q"] = (q_, qtok)
                    self.tt("dve", q_.h[0:n, :].rearrange("p (h e) -> p h e", e=64),
                            pj.h[0:n, 0:512].rearrange("p (h e) -> p h e", e=64),
                            self.bc_heads(s_, n, 16, 8, 64), ALU.mult, [pjt, stok], [qtok])
                    if samp:
                        self.tt("dve", P["qsB"].h[0:16, :].rearrange("p (h e) -> p h e", e=64),
                                pj.h[0:16, 0:512].rearrange("p (h e) -> p h e", e=64),
                                self.bc_heads(s_, 16, 16, 8, 64), ALU.mult, [pjt, stok], ["qsB"])
                        self.tt("pool", P["qsB"].h[0:16, :].rearrange("p (h e) -> p h e", e=64),
                                P["qsB"].h[0:16, :].rearrange("p (h e) -> p h e", e=64),
                                P["gqsB"].ap(0, 16, 0, [[0, 8], [1, 64]]), ALU.mult, ["qsB", "gqsB"], ["qsB"])
                    pk, pkt = X["pk"]
                    s2, s2tok = X["sk"]
                    self.rms_stats_b(s2, s2tok, n, 2, 64)
                    n_, ntok = kn.next()
                    self.tt("dve", n_.h[0:n, :].rearrange("p (h e) -> p h e", e=64),
                            pk.h[0:n, 0:128].rearrange("p (h e) -> p h e", e=64),
                            self.bc_heads(s2, n, 16, 2, 64), ALU.mult, [pkt, s2tok], [ntok])
                    if samp:
                        o_, otok = P["ksB"], "ksB"
                    else:
                        o_, otok = ko.next()
                    X["o"] = (o_, otok)
                    self.tt("pool", o_.h[0:n, :].rearrange("p (h e) -> p h e", e=64), n_.h[0:n, :].rearrange("p (h e) -> p h e", e=64),
                            P["gkB"].ap(0, n, 0, [[0, 2], [1, 64]]), ALU.mult, [ntok, "gkB"], [otok])
                    if samp:
                        self.dma(D["sbk"].ap(), o_.h[0:16, :], [otok], [])
                    elif ti == 15:
                        self.dma(D["pbk"].ap(), o_.h[:, :], [otok], [])

                def SB2():
                    o_, otok = X["o"]
                    b_, btok = kb.next()
                    X["b"] = (b_, btok)
                    self.acopy(b_.h[0:n, :].rearrange("p (g r e) -> p g r e", r=2, e=64),
                               o_.ap(0, n, 0, [[64, 2], [0, 2], [1, 64]]), [otok], [btok])

                def SC():
                    q_, qtok = X["q"]
                    b_, btok = X["b"]
                    for j in range(4):
                        self.tr(Q["TRb"].h[:, j * 128:j * 128 + n], q_.h[0:n, j * 128:(j + 1) * 128], P["identb"].h[0:n, 0:n],
                                [qtok, "identb"], ["TRb"])
                    for j in range(2):
                        self.tr(Q["TRb"].h[:, (4 + j) * 128:(4 + j) * 128 + n], b_.h[0:n, j * 128:(j + 1) * 128],
                                P["identb"].h[0:n, 0:n], [btok, "identb"], ["TRb"])
                    trv = Q["TRb"].h[:, 0:512].rearrange("p (k t) -> p k t", t=128)
                    self.acopy(qT.h[:, :, col0:col0 + n], trv[:, 0:4, 0:n], ["TRb"], [("qTB", ti)])
                    trk = Q["TRb"].h[:, 512:768].rearrange("p (k t) -> p k t", t=128)
                    self.tsmul("dve", kT.h[:, :, col0:col0 + n], trk[:, 0:2, 0:n], P["kscB"].h[:, 0:1], ["TRb", "kscB"], [("kTB", ti)])
                return [SA, SB1, SB2, SC]

            pipe = []
            for it in range(len(tiles) + 3):
                if it < len(tiles):
                    pipe.append(tile_stages(it, *tiles[it]))
                for stg in pipe:
                    if stg:
                        stg.pop(0)()
                pipe = [p_ for p_ in pipe if p_]
            self.S.barrier()
            Pb = RotL([(T(Wq.h[:, 2 * i:2 * i + 2, :].rearrange("p a b -> p (a b)").bitcast(F32), [128, 512]), ("PbB", i))
                       for i in range(4)])
            Pm = RotL([(T(Wkv.h[:, 2 * i:2 * i + 2, :].rearrange("p a b -> p (a b)"), [128, 512]), ("PmB", i)) for i in range(4)])
            groups = []
            for h in range(8):
                g_, hp, pr = h // 4, h % 2, h // 2
                p0 = 64 * hp
                vcol = (64 + 128 * g_) if hp == 0 else (128 * g_)
                for tb in range(4):
                    A_, atok = Q["ACC"].next()
                    units = []
                    for n_ in range(4 * tb, 4 * tb + 4):
                        oa = A_.h[:, 128 * (n_ - 4 * tb):128 * (n_ - 4 * tb + 1)]
                        qa = qT.ap(p0, 64, pr * NT + 128 * n_, [[1, 128]])
                        tl = [dict(k=kT.ap(p0, 64, g_ * NT + 128 * n_, [[1, 128]]), q=qa,
                                   v=VB.ap(0, 128, n_ * 320 + vcol, [[1, 128]]), out=oa, start=False,
                                   rd=[("qTB", n_), ("kTB", n_)], vrd=[("VB", n_), "VBones"])]
                        if n_ > 0:
                            tl.append(dict(k=kT.ap(p0, 64, g_ * NT + 128 * (n_ - 1), [[1, 128]]), q=qa,
                                           v=VB.ap(0, 128, (n_ - 1) * 320 + vcol, [[1, 128]]), out=oa, start=False,
                                           rd=[("qTB", n_), ("kTB", n_ - 1)], vrd=[("VB", n_ - 1), "VBones"]))
                        units.append(tl)
                    packed = [(units[0] + units[1]), (units[2] + units[3])]

                    def postB(A_=A_, atok=atok, tb=tb, hp=hp, pr=pr):
                        nr, dr = (0, 64) if hp == 0 else (64, 0)
                        self.acopy(P["yT"].ap(nr, 64, (4 + pr) * NT + 512 * tb, [[1, 512]]), A_.h[nr:nr + 64, :], [atok],
                                   [("yT", 4 + pr, tb, hp)])
                        rs_, rstok = rsm.next()
                        dc_, dctok = dcp.next()
                        self.acopy(dc_.h[dr:dr + 64, :], A_.h[dr:dr + 64, :], [atok], [dctok])
                        self.tt("pool", dc_.h[dr:dr + 64, :], dc_.h[dr:dr + 64, :], maskD.h[dr:dr + 64, :], ALU.mult, [dctok, "maskD"], [dctok])
                        self.S.add("dve", lambda e, o=rs_.h[dr:dr + 64, 8:16], i=dc_.h[dr:dr + 64, :].rearrange("p (c j) -> p j c", j=8):
                                   e.reduce_sum(out=o, in_=i, axis=AX.X), [dctok], [rstok])
                        self.recip(rs_.h[dr:dr + 64, 0:8], rs_.h[dr:dr + 64, 8:16], [rstok], [rstok])
                        hg = 8 + 2 * pr + hp
                        self.dma(bass.AP(D["rD"], hg * 2048 + 512 * tb, [[8, 64], [1, 8]]), rs_.h[dr:dr + 64, 0:8], [rstok],
                                 [("rD", hg, tb)])
                    for gi, tl in enumerate(packed):
                        ul = []
                        c = 0
                        i = 0
                        while i < len(tl):
                            w = 2 if (i + 1 < len(tl) and tl[i + 1]["out"] is tl[i]["out"]) else 1
                            ul.append((128 * i, 128 * w, EB.ap(0, 128, h * 256, [[1, 128 * w]]), ("EBB", h)))
                            i += w
                        gd = dict(tiles=tl, units=ul, post=postB if gi == 1 else None, acctok=atok)
                        if gi == 0:
                            gd["pre_sink"] = (A_, atok, h)
                        groups.append(gd)
            self.run_groups(groups, Pb, Pm)
            self.S.phase_end()

    def phase_C(self):
        P, D, Q = self.P, self.D, self.Q
        with ExitStack() as st:
            Wq = self.sb(st, [128, 8, 512], BF16, "WqC")
            Wm = self.sb(st, [128, 8, 1024], BF16, "Wm")
            qT = self.sb(st, [128, 4, NT], BF16, "qTC")
            mkT = self.sb(st, [128, 4, 256], BF16, "mkT")
            mvb = self.sb(st, [128, 2, 512], BF16, "mvb")
            sq = self.rot(st, 2, [128, 512], F32, "sqC")
            ss = self.rot(st, 2, [128, 24], F32, "ssC")
            qb = self.rot(st, 4, [128, 512], BF16, "qbC")
            kn = self.rot(st, 2, [128, 512], F32, "knC")
            ko = self.rot(st, 2, [128, 512], F32, "koC")
            kb = self.rot(st, 2, [128, 512], BF16, "kbC")
            vo = self.rot(st, 2, [128, 512], F32, "voC")
            Pm = self.rot(st, 2, [128, 512], BF16, "PmC")
            rsm = self.rot(st, 4, [128, 16], F32, "rsmC")
            dcp = self.rot(st, 2, [128, 512], F32, "dcpC")
            maskD = self.sb(st, [128, 512], F32, "maskD")
            self.dma(maskD.h[:, :], D["masks"].ap()[:, 512:1024], (), ["maskD"])
            self.load_w(Wq, "WqC", "w_in", C_QC, 512, 0)
            self.load_w(Wm, "Wm", "w_mem_kv", 0, 1024, 0)
            P["memT"] = self.sb(st, [128, 8, 256], BF16, "memT")
            with ExitStack() as st3:
                gmem = self.sb(st3, [128, 1024], F32, "gmem")
                self.dma(gmem.h[:], bass.AP(D["mem_ln_g"], 0, [[0, 128], [1, 1024]]), (), ["gmem"])
                self.norm_tiles(st3, [("mem", 128 * i, P["memT"], "memT", gmem, "gmem", 128 * i, 128) for i in range(2)], deep=False)
                self.S.phase_end()
            for mt in range(2):
                pj, pjt = Q["PJ"].next()
                for kc in range(8):
                    self.mm(pj.h[:, :], P["memT"].h[:, kc, 128 * mt:128 * mt + 128], Wm.h[:, kc, 0:512], kc == 0, kc == 7,
                            [("memT", 128 * mt)] + self.wrd("Wm", kc), [pjt])
                s_, stok = self.rms_stats(pj, pjt, 128, 0, 4, 128, sq, ss)
                n_, ntok = kn.next()
                self.tt("dve", n_.h[:, :].rearrange("p (h e) -> p h e", e=128), pj.h[:, :].rearrange("p (h e) -> p h e", e=128),
                        self.bc_heads(s_, 128, 16, 4, 128), ALU.mult, [pjt, stok], [ntok])
                o_, otok = ko.next()
                self.tt("pool", o_.h[:, :].rearrange("p (h e) -> p h e", e=128), n_.h[:, :].rearrange("p (h e) -> p h e", e=128),
                        P["gkC"].ap(0, 128, 0, [[0, 4], [1, 128]]), ALU.mult, [ntok, "gkC"], [otok])
                self.dma(D["pmk"].ap()[128 * mt:128 * mt + 128, :], o_.h[:, :], [otok], [])
                b_, btok = kb.next()
                self.acopy(b_.h[:, :], o_.h[:, :], [otok], [btok])
                for j in range(4):
                    self.tr(Q["TRb"].h[:, j * 128:(j + 1) * 128], b_.h[:, j * 128:(j + 1) * 128], P["identb"].h[:, :],
                            [btok, "identb"], ["TRb"])
                trv = Q["TRb"].h[:, 0:512].rearrange("p (k t) -> p k t", t=128)
                self.tsmul("dve", mkT.h[:, :, 128 * mt:128 * mt + 128], trv, P["kscC"].h[:, 0:1], ["TRb", "kscC"], [("mkT", mt)])
                pj, pjt = Q["PJ"].next()
                for kc in range(8):
                    self.mm(pj.h[:, :], P["memT"].h[:, kc, 128 * mt:128 * mt + 128], Wm.h[:, kc, 512:1024], kc == 0, kc == 7,
                            [("memT", 128 * mt)] + self.wrd("Wm", kc), [pjt])
                v_, vtok = vo.next()
                self.acopy(v_.h[:, :], pj.h[:, :], [pjt], [vtok])
                self.dma(D["pmv"].ap()[128 * mt:128 * mt + 128, :], v_.h[:, :], [vtok], [])
                self.tcopy("dve", mvb.h[:, mt, :], pj.h[:, :], [pjt], [("mvb", mt)])
            tiles = [(128 * i, 128) for i in range(16)] + [(2048, 16)]
            deferC = []
            PJr = self.bank_rot(["PJ", "S", "ACC"])
            for ti, (col0, n) in enumerate(tiles):
                pj, pjt = PJr.next()
                self.proj_tm(pj, pjt, "hT", col0, n, Wq, "WqC", 512)
                s_, stok = self.rms_stats(pj, pjt, n, 0, 4, 128, sq, ss)
                if ti == 16:
                    self.tt("dve", P["qsC"].h[0:16, :].rearrange("p (h e) -> p h e", e=128),
                            pj.h[0:16, 0:512].rearrange("p (h e) -> p h e", e=128),
                            self.bc_heads(s_, 16, 16, 4, 128), ALU.mult, [pjt, stok], ["qsC"])
                    self.tt("pool", P["qsC"].h[0:16, :].rearrange("p (h e) -> p h e", e=128),
                            P["qsC"].h[0:16, :].rearrange("p (h e) -> p h e", e=128),
                            P["gqsC"].ap(0, 16, 0, [[0, 4], [1, 128]]), ALU.mult, ["qsC", "gqsC"], ["qsC"])
                    continue
                q_, qtok = qb.next()
                self.tt("dve", q_.h[0:n, :].rearrange("p (h e) -> p h e", e=128),
                        pj.h[0:n, 0:512].rearrange("p (h e) -> p h e", e=128),
                        self.bc_heads(s_, n, 16, 4, 128), ALU.mult, [pjt, stok], [qtok])
                def st2c(q_=q_, qtok=qtok, col0=col0, n=n, ti=ti):
                    for j in range(4):
                        self.tr(Q["TRb"].h[:, j * 128:j * 128 + n], q_.h[0:n, j * 128:(j + 1) * 128], P["identb"].h[0:n, 0:n],
                                [qtok, "identb"], ["TRb"])
                    trv = Q["TRb"].h[:, 0:512].rearrange("p (k t) -> p k t", t=128)
                    self.acopy(qT.h[:, :, col0:col0 + n], trv[:, 0:4, 0:n], ["TRb"], [("qTC", ti)])
                deferC.append(st2c)
                while len(deferC) > 2:
                    deferC.pop(0)()
            while deferC:
                deferC.pop(0)()
            self.dma(D["qS"].ap()[0:16, :], P["qsA"].h[:, :], [("qsA", 0), ("qsA", 1)], ["qS"])
            self.dma(D["qS"].ap()[16:32, :], P["qsB"].h[:, :], ["qsB"], ["qS"])
            self.dma(D["qS"].ap()[32:48, :], P["qsC"].h[:, :], ["qsC"], ["qS"])
            jobs = [(h, tb, mt) for h in range(4) for tb in range(4) for mt in range(2)]
            accs = {}
            pend = None
            for job in jobs + [None]:
                cur = None
                if job is not None:
                    h, tb, mt = job
                    Sb, stok = Q["S"].next()
                    self.mm(Sb.h[:, :], mkT.h[:, h, 128 * mt:128 * mt + 128], qT.h[:, h, 512 * tb:512 * tb + 512], True, True,
                            [("mkT", mt)] + [("qTC", 4 * tb + i) for i in range(4)], [stok])
                    Pm_, pmtok = Pm.next()
                    self.act(Pm_.h[:, :], Sb.h[:, :], AF.Exp, [stok], [pmtok])
                    cur = (job, Pm_, pmtok)
                if pend is not None:
                    (h, tb, mt), Pm_, pmtok = pend
                    if mt == 0:
                        accs[(h, tb)] = Q["ACC"].next()
                    A_, atok = accs[(h, tb)]
                    self.mm(A_.h[:, :], mvb.h[:, mt, 128 * h:128 * h + 128], Pm_.h[:, :], mt == 0, mt == 1,
                            [pmtok, ("mvb", mt)], [atok])
                    self.mm(Q["O3"].h[:, :], P["onesb"].h[:, :], Pm_.h[:, :], mt == 0, mt == 1, [pmtok, "onesb"], ["O3"])
                    if mt == 1:
                        self.acopy(P["yT"].h[:, 8 + h, 512 * tb:512 * tb + 512], A_.h[:, :], [atok], [("yT", 8 + h, tb, 0)])
                        rs_, rstok = rsm.next()
                        dc_, dctok = dcp.next()
                        self.acopy(dc_.h[:, :], Q["O3"].h[:, :], ["O3"], [dctok])
                        self.tt("pool", dc_.h[:, :], dc_.h[:, :], maskD.h[:, :], ALU.mult, [dctok, "maskD"], [dctok])
                        self.S.add("dve", lambda e, o=rs_.h[:, 8:12], i=dc_.h[:, :].rearrange("p (c j) -> p j c", j=4):
                                   e.reduce_sum(out=o, in_=i, axis=AX.X), [dctok], [rstok])
                        self.recip(rs_.h[:, 0:4], rs_.h[:, 8:12], [rstok], [rstok])
                        self.dma(bass.AP(D["rD"], (16 + h) * 2048 + 512 * tb, [[4, 128], [1, 4]]), rs_.h[:, 0:4], [rstok],
                                 [("rD", 16 + h, tb)])
                pend = cur
            self.S.phase_end()

    def sample_alloc(self, st, stM):
        B = self.SB = {}
        B["szS"] = self.sb(stM, [128, 12, 16], F32, "szS")
        B["sgS"] = self.sb(stM, [128, 24, 16], F32, "sgS")
        B["Kt"] = self.rot(st, 4, [128, 512], F32, "Kt")
        B["Vt"] = self.rot(st, 5, [128, 512], F32, "Vt")
        B["qbc"] = self.rot(st, 2, [128, 512], F32, "qbc")
        B["prod"] = self.rot(st, 2, [128, 512], F32, "prod")
        B["Wb"] = self.rot(st, 3, [128, 512], BF16, "Wb")
        B["sm"] = self.rot(st, 5, [128, 24], F32, "sm")
        B["pmb"] = self.rot(st, 5, [128, 8], BF16, "pmb")
        B["fin"] = self.sb(st, [16, 64], F32, "fin")
        B["fb"] = self.sb(st, [16, 64], F32, "fb")
        B["fc"] = self.sb(st, [16, 8], F32, "fc")
        B["t1"] = self.sb(st, [16, 512], F32, "t1")
        B["t2"] = self.sb(st, [16, 512], F32, "t2")
        B["ob"] = self.sb(st, [16, 1536], BF16, "ob")

    def sample_tiles(self):
        P, D, Q, B = self.P, self.D, self.Q, self.SB
        NUMB = Q["ACC"].ts[0]
        DENB = Q["ACC"].ts[1]
        p0 = {"A": 0, "B": 32, "C": 64}
        dcol = {"A": 0, "B": 8, "C": 16}
        cfg = {"A": dict(nh=8, hd=64, mi=0), "B": dict(nh=8, hd=64, mi=1), "C": dict(nh=4, hd=128, mi=2)}
        total = {"A": 48, "B": 16, "C": 32}
        count = {"A": 0, "B": 0, "C": 0}
        tiles = []
        for b in range(16):
            for mix in ("A", "B", "C"):
                for t in range({"A": 3, "B": 1, "C": 2}[mix]):
                    tiles.append((b, mix, t))
        loaded = {}

        def load(i):
            b, mix, t = tiles[i]
            k_, ktok = B["Kt"].next()
            v_, vtok = B["Vt"].next()
            if mix == "A":
                d = A_D[t]
                base = (b * 2048 + 2048 - 128 * d) * 512
                self.dma(k_.h[:, :], bass.AP(D["cak"], base, [[d * 512, 128], [1, 512]]), (), [ktok])
                self.dma(v_.h[:, :], bass.AP(D["cav"], base, [[d * 512, 128], [1, 512]]), (), [vtok], q="act")
            elif mix == "B":
                self.dma(k_.h[:, 0:128], D["cbk"].ap()[128 * b:128 * b + 128, :], (), [ktok])
                self.dma(v_.h[:, 0:128], D["cbv"].ap()[128 * b:128 * b + 128, :], (), [vtok], q="act")
            else:
                r0 = 256 * b + 128 * t
                self.dma(k_.h[:, :], D["cmk"].ap()[r0:r0 + 128, :], (), [ktok])
                self.dma(v_.h[:, :], D["cmv"].ap()[r0:r0 + 128, :], (), [vtok], q="act")
            loaded[i] = (k_, ktok, v_, vtok)

        PF = 2
        for i in range(min(PF, len(tiles))):
            load(i)
        qcur = {}
        state = {"dfirst": True}

        def make_stages(i, b, mix, t):
            c = cfg[mix]
            nh, hd = c["nh"], c["hd"]
            if t == 0:
                qb_, qbt = B["qbc"].next()
                self.dma(qb_.h[:, :], bass.AP(D["qS"], (c["mi"] * 16 + b) * 512, [[0, 128], [1, 512]]), ["qS"], [qbt])
                qcur[mix] = (qb_, qbt)
            qb_, qbt = qcur[mix]
            k_, ktok, v_, vtok = loaded.pop(i)
            if mix == "A":
                kin = k_.h[:, :].rearrange("p (h e) -> p h e", e=64)
                vin = v_.h[:, :].rearrange("p (h e) -> p h e", e=64)
                ebap = P["EBs"].h[:, t, 0:8]
            elif mix == "B":
                kin = k_.ap(0, 128, 0, [[64, 2], [0, 4], [1, 64]])
                vin = v_.ap(0, 128, 0, [[64, 2], [0, 4], [1, 64]])
                ebap = P["EBs"].h[:, 3, 8:16]
            else:
                kin = k_.h[:, :].rearrange("p (h e) -> p h e", e=128)
                vin = v_.h[:, :].rearrange("p (h e) -> p h e", e=128)
                ebap = None
            pr_, prtok = B["prod"].next()
            s_, stok = B["sm"].next()
            pm_, pmtok = B["pmb"].next()
            w_, wtok = B["Wb"].next()
            count[mix] += 1
            cnt = count[mix]

            def stA():
                if mix == "B":
                    pview = pr_.h[:, :].rearrange("p (g r e) -> p g r e", r=4, e=64)
                    qview = qb_.h[:, :].rearrange("p (g r e) -> p g r e", r=4, e=64)
                else:
                    pview = pr_.h[:, :].rearrange("p (h e) -> p h e", e=hd)
                    qview = qb_.h[:, :].rearrange("p (h e) -> p h e", e=hd)
                self.tt("dve", pview, kin, qview, ALU.mult, [ktok, qbt], [prtok])
                self.rsum(s_.h[:, 0:nh], pr_.h[:, :].rearrange("p (h e) -> p h e", e=hd), [prtok], [stok])

            def stB():
                if ebap is None:
                    self.act(pm_.h[:, 0:nh], s_.h[:, 0:nh], AF.Exp, [stok], [pmtok])
                else:
                    self.act(s_.h[:, 8:8 + nh], s_.h[:, 0:nh], AF.Exp, [stok], [stok])

            def stC():
                if ebap is not None:
                    self.tt("dve", pm_.h[:, 0:nh], s_.h[:, 8:8 + nh], ebap, ALU.mult, [stok, "EBs"], [pmtok])
                if mix == "B":
                    wview = w_.h[:, :].rearrange("p (g r e) -> p g r e", r=4, e=64)
                    pbc = pm_.ap(0, 128, 0, [[4, 2], [1, 4], [0, 64]])
                else:
                    wview = w_.h[:, :].rearrange("p (h e) -> p h e", e=hd)
                    pbc = pm_.ap(0, 128, 0, [[1, nh], [0, hd]])
                self.tt("pool", wview, vin, pbc, ALU.mult, [vtok, pmtok], [wtok])

            def stD():
                q0 = p0[mix]
                self.mm(NUMB.h[q0:q0 + 16, :], P["OHB"].h[:, 16 * b:16 * b + 16], w_.h[:, :], cnt == 1, cnt == total[mix],
                        ["OHB", wtok], ["num" + mix], sgc=True, tp=(0, q0))
                self.mm(DENB.h[0:16, dcol[mix]:dcol[mix] + nh], P["OHB"].h[:, 16 * b:16 * b + 16], pm_.h[:, 0:nh],
                        state["dfirst"], False, ["OHB", pmtok], ["den"], sgc=True)
                state["dfirst"] = False
            return [stA, stB, stC, stD]

        pipe = []
        for i in range(len(tiles) + 3):
            if i < len(tiles):
                if i + PF < len(tiles):
                    load(i + PF)
                pipe.append(make_stages(i, *tiles[i]))
            for stg in pipe:
                if stg:
                    stg.pop(0)()
            pipe = [p_ for p_ in pipe if p_]
            yield

    def pump(self, gen, k=1):
        if gen is None:
            return
        for _ in range(k):
            try:
                next(gen)
            except StopIteration:
                return

    def sample_finalize(self):
        P, D, Q, B = self.P, self.D, self.Q, self.SB
        NUMB = Q["ACC"].ts[0]
        DENB = Q["ACC"].ts[1]
        fin, fb, fc, t1, t2, ob = B["fin"], B["fb"], B["fc"], B["t1"], B["t2"], B["ob"]
        self.tt("dve", t1.h[:, :], P["qsA"].h[:, :], P["ksA"].h[:, :], ALU.mult,
                [("qsA", 0), ("qsA", 1), ("ksA", 0), ("ksA", 1)], ["t1"])
        self.rsum(fin.h[:, 0:8], t1.h[:, :].rearrange("p (h e) -> p h e", e=64), ["t1"], ["finA"])
        self.act(fin.h[:, 8:16], fin.h[:, 0:8], AF.Exp, ["finA"], ["finA"])
        self.stt("dve", fin.h[:, 16:24], fin.h[:, 8:16], 3.0, P["e0"].h[:, 0:8], ALU.mult, ALU.mult, ["finA", "e0"], ["finA"])
        self.tt("dve", t1.h[:, :].rearrange("p (h e) -> p h e", e=64), P["vsA"].h[:, :].rearrange("p (h e) -> p h e", e=64),
                fin.ap(0, 16, 16, [[1, 8], [0, 64]]), ALU.mult, ["finA", ("vsA", 0), ("vsA", 1), "t1"], ["t1"])
        self.tt("dve", t1.h[:, :], t1.h[:, :], NUMB.h[0:16, :], ALU.add, ["t1", "numA", "numB", "numC", "den"], ["t1"])
        self.tt("dve", fin.h[:, 24:32], fin.h[:, 16:24], DENB.h[0:16, 0:8], ALU.add, ["finA", "den"], ["finA"])
        self.recip(fin.h[:, 32:40], fin.h[:, 24:32], ["finA"], ["finA"])
        self.tt("dve", ob.h[:, 0:512].rearrange("p (h e) -> p h e", e=64), t1.h[:, :].rearrange("p (h e) -> p h e", e=64),
                fin.ap(0, 16, 32, [[1, 8], [0, 64]]), ALU.mult, ["t1", "finA"], ["obA"])
        self.tt("dve", t2.h[:, :].rearrange("p (g r e) -> p g r e", r=4, e=64),
                P["qsB"].h[:, :].rearrange("p (g r e) -> p g r e", r=4, e=64),
                P["ksB"].ap(0, 16, 0, [[64, 2], [0, 4], [1, 64]]), ALU.mult, ["qsB", "ksB"], ["t2"])
        self.rsum(fb.h[:, 0:8], t2.h[:, :].rearrange("p (h e) -> p h e", e=64), ["t2"], ["finB"])
        self.act(fb.h[:, 8:16], fb.h[:, 0:8], AF.Exp, ["finB"], ["finB"])
        self.tt("dve", fb.h[:, 16:24], fb.h[:, 8:16], P["e0"].h[:, 8:16], ALU.mult, ["finB", "e0"], ["finB"])
        self.tt("dve", t2.h[:, :].rearrange("p (g r e) -> p g r e", r=4, e=64),
                P["vsB"].ap(0, 16, 0, [[64, 2], [0, 4], [1, 64]]),
                fb.ap(0, 16, 16, [[4, 2], [1, 4], [0, 64]]), ALU.mult, ["finB", "vsB", "t2"], ["t2"])
        self.tt("dve", t2.h[:, :], t2.h[:, :], NUMB.h[32:48, :], ALU.add, ["t2", "numA", "numB", "numC", "den"], ["t2"])
        self.tt("dve", fb.h[:, 24:32], fb.h[:, 16:24], DENB.h[0:16, 8:16], ALU.add, ["finB", "den"], ["finB"])
        self.tt("dve", fb.h[:, 32:40], fb.h[:, 24:32], P["esink"].h[:, :], ALU.add, ["finB", "esink"], ["finB"])
        self.recip(fb.h[:, 40:48], fb.h[:, 32:40], ["finB"], ["finB"])
        self.tt("dve", ob.h[:, 512:1024].rearrange("p (h e) -> p h e", e=64), t2.h[:, :].rearrange("p (h e) -> p h e", e=64),
                fb.ap(0, 16, 40, [[1, 8], [0, 64]]), ALU.mult, ["t2", "finB"], ["obB"])
        self.recip(fc.h[:, 0:4], DENB.h[0:16, 16:20], ["den"], ["finC"])
        self.tt("dve", ob.h[:, 1024:1536].rearrange("p (h e) -> p h e", e=128), fc.ap(0, 16, 0, [[1, 4], [0, 128]]),
                NUMB.h[64:80, :].rearrange("p (h e) -> p h e", e=128), ALU.mult, ["numA", "numB", "numC", "den", "finC"], ["obC"])
        for j in range(12):
            self.tr(Q["TRb"].h[:, 16 * j:16 * j + 16], ob.h[0:16, 128 * j:128 * j + 128], P["identb"].h[0:16, 0:16],
                    ["obA", "obB", "obC", "identb"], ["TRb"])
        self.acopy(P["yT"].h[:, :, 2048:2064], Q["TRb"].h[:, 0:192].rearrange("p (j t) -> p j t", t=16), ["TRb"],
                   [("yT", "samp")])

    def yT_rd(self, j, blk):
        return [("yT", j, blk, 0), ("yT", j, blk, 1)]

    def phase_tail_all(self):
        with ExitStack() as stM:
            self.mT = self.sb(stM, [128, 8, NT], BF16, "mT")
            with ExitStack() as stS:
                self.sample_alloc(stS, stM)
                gen = self.sample_tiles()
                self.phase_z(gen)
                self.phase_gate(gen)
            self.phase_tail_out()

    def z_stages(self, W, wtok, mi, c, j, bi, col0, n, Zr, sz, uz, rb, tz):
        P, D, B = self.P, self.D, self.SB
        X = {}

        def SA():
            pj, pjt = Zr.next()
            for kc in range(8):
                self.mm(pj.h[:, 0:n], W.h[:, kc, 128 * c:128 * c + 128], P["hT"].h[:, kc, col0:col0 + n],
                        kc == 0, kc == 7, self.wrd(wtok, kc) + [("hT", col0 + 128 * i) for i in range(max(1, n // 128))], [pjt])
            s_, stok = sz.next()
            self.act(s_.h[:, 0:n], pj.h[:, 0:n], AF.Tanh, [pjt], [stok], scale=0.5)
            if bi == 4:
                self.stt("dve", B["szS"].h[:, j, :], s_.h[:, 0:16], 1.0, pj.h[:, 0:16], ALU.add, ALU.mult, [stok, pjt], [("szS", j)])
                return
            u_, utok = uz.next()
            X["u"] = (u_, utok)
            self.stt("dve", u_.h[:, 0:n], s_.h[:, 0:n], 1.0, pj.h[:, 0:n], ALU.add, ALU.mult, [stok, pjt], [utok])
            r_, rtok = rb.next()
            if mi < 2:
                for hp in range(2):
                    hg = 2 * j + hp
                    self.dma(r_.h[64 * hp:64 * hp + 64, :], bass.AP(D["rD"], hg * 2048 + col0, [[0, 64], [1, 512]]),
                             [("rD", hg, bi)], [rtok + (hp,)], q="act")
                X["r"] = (r_, [rtok + (0,), rtok + (1,)])
            else:
                hg = 16 + c
                self.dma(r_.h[:, :], bass.AP(D["rD"], hg * 2048 + col0, [[0, 128], [1, 512]]), [("rD", hg, bi)], [rtok + (0,)], q="act")
                X["r"] = (r_, [rtok + (0,)])

        def SB():
            r_, rtoks = X["r"]
            t_, ttok = tz.next()
            X["t"] = (t_, ttok)
            self.tt("pool", t_.h[:, :], P["yT"].h[:, j, col0:col0 + n], r_.h[:, :], ALU.mult, rtoks + self.yT_rd(j, bi), [ttok])

        def SC():
            u_, utok = X["u"]
            t_, ttok = X["t"]
            self.stt("dve", P["yT"].h[:, j, col0:col0 + n], t_.h[:, :], 0.5, u_.h[:, 0:n], ALU.mult, ALU.mult,
                     [utok, ttok] + self.yT_rd(j, bi), [("y", j, bi)])
        return [SA] if bi == 4 else [SA, SB, SC]

    def phase_z(self, gen):
        P, D, Q, B = self.P, self.D, self.Q, self.SB
        blocks = [(512 * i, 512) for i in range(4)] + [(2048, 16)]
        with ExitStack() as st:
            Wz = self.rot(st, 2, [128, 8, 512], BF16, "Wz")
            sz = self.rot(st, 2, [128, 512], F32, "sz")
            uz = self.rot(st, 3, [128, 512], F32, "uz")
            rb = self.rot(st, 3, [128, 512], F32, "rbz")
            tz = self.rot(st, 2, [128, 512], F32, "tz")
            it = 0
            zc = (C_ZA, C_ZB, C_ZC)
            Zr = self.bank_rot(["PJ", "S", "O3"])

            def zload(mi):
                W, wtok = Wz.next()
                self.load_w(W, wtok, "w_in", zc[mi], 512, 0)
                return W, wtok
            znxt = zload(0)
            zpipe = []
            for mi, c0 in enumerate(zc):
                W, wtok = znxt
                if mi + 1 < 3:
                    znxt = zload(mi + 1)
                for c in range(4):
                    j = 4 * mi + c
                    for bi, (col0, n) in enumerate(blocks):
                        zpipe.append(self.z_stages(W, wtok, mi, c, j, bi, col0, n, Zr, sz, uz, rb, tz))
                        for stg in zpipe:
                            if stg:
                                stg.pop(0)()
                        zpipe[:] = [p_ for p_ in zpipe if p_]
                        it += 1
                        if it % 2 == 0:
                            self.pump(gen)
            while zpipe:
                for stg in zpipe:
                    if stg:
                        stg.pop(0)()
                zpipe[:] = [p_ for p_ in zpipe if p_]
            self.S.phase_end()

    def phase_gate(self, gen):
        P, D, Q, B = self.P, self.D, self.Q, self.SB
        blocks = [(512 * i, 512) for i in range(4)] + [(2048, 16)]
        with ExitStack() as st2:
            Wg = self.rot(st2, 2, [128, 8, 384], BF16, "Wg")
            Wb = self.rot(st2, 2, [128, 3, 4, 128], BF16, "WbS")
            sg = self.rot(st2, 3, [128, 512], F32, "sg")
            tmp = self.rot(st2, 3, [128, 512], F32, "tmpg")
            mc = self.rot(st2, 2, [128, 512], F32, "mc")
            it = 0
            def gload(c):
                W, wtok = Wg.next()
                Wr, wrtok = Wb.next()
                for i, nm in enumerate(("w_br_a", "w_br_b", "w_br_c")):
                    self.load_w(W, wtok, "w_in", C_GA + 1024 * i + 128 * c, 128, 128 * i, part=i)
                    self.dma(Wr.h[:, i, :, :], D[nm].ap().rearrange("(kc p) n -> p kc n", p=128)[:, :, 128 * c:128 * c + 128],
                             (), [(wrtok, i)], q="pool")
                return W, wtok, Wr, wrtok
            nxt = gload(0)
            Gr = self.bank_rot(["PJ", "O3"])
            for c in range(8):
                W, wtok, Wr, wrtok = nxt
                if c + 1 < 8:
                    nxt = gload(c + 1)
                for bi, (col0, n) in enumerate(blocks):
                    if bi < 4:
                        m_, mtok = mc.next()
                    for i in range(3):
                        pg, pgt = Gr.next()
                        for kc in range(8):
                            self.mm(pg.h[:, 0:n], W.h[:, kc, 128 * i:128 * i + 128], P["hT"].h[:, kc, col0:col0 + n],
                                    kc == 0, kc == 7, [(wtok, i, kc // 2)] + [("hT", col0 + 128 * k) for k in range(max(1, n // 128))], [pgt])
                        if bi == 4:
                            self.act(B["sgS"].h[:, 3 * c + i, :], pg.h[:, 0:16], AF.Tanh, [pgt], [("sgS", c)], scale=0.5)
                            continue
                        s_, stok = sg.next()
                        self.act(s_.h[:, 0:n], pg.h[:, 0:n], AF.Tanh, [pgt], [stok], scale=0.5)
                        pb, pbt = Q["S"].next()
                        for kc in range(4):
                            self.mm(pb.h[:, 0:n], Wr.h[:, i, kc, :], P["yT"].h[:, 4 * i + kc, col0:col0 + n],
                                    kc == 0, kc == 3, [(wrtok, i), ("y", 4 * i + kc, bi)], [pbt])
                        if i == 0:
                            self.stt("dve", m_.h[:, 0:n], s_.h[:, 0:n], 1.0, pb.h[:, 0:n], ALU.add, ALU.mult, [pbt, stok], [mtok])
                        else:
                            t_, ttok = tmp.next()
                            self.stt("dve", t_.h[:, 0:n], s_.h[:, 0:n], 1.0, pb.h[:, 0:n], ALU.add, ALU.mult, [pbt, stok], [ttok])
                            if i == 1:
                                self.tt("pool", m_.h[:, 0:n], m_.h[:, 0:n], t_.h[:, 0:n], ALU.add, [mtok, ttok], [mtok])
                            else:
                                self.tt("pool", self.mT.h[:, c, col0:col0 + n], m_.h[:, 0:n], t_.h[:, 0:n], ALU.add,
                                        [mtok, ttok], [("mT", c, bi)])
                        it += 1
                        if it % 2 == 0:
                            self.pump(gen)
            self.pump(gen, 1000)
            self.sample_finalize()
            self.S.phase_end()

    def phase_tail_out(self):
        P, D, Q, B = self.P, self.D, self.Q, self.SB
        with ExitStack() as st:
            Wo = self.sb(st, [128, 8, 1024], BF16, "Wo")
            Wbr = [self.sb(st, [128, 4, 1024], BF16, f"WbrF{i}") for i in range(3)]
            prod = self.sb(st, [128, 24, 16], F32, "prodS")
            xt = self.rot(st, 3, [128, 1024], F32, "xto")
            yo = self.rot(st, 4, [128, 512], F32, "yo")
            wv = D["w_out"].ap().rearrange("(kc p) n -> p kc n", p=128)
            for g in range(4):
                self.dma(Wo.h[:, 2 * g:2 * g + 2, :], wv[:, 2 * g:2 * g + 2, :], (), [("Wo", g)], q="pool")
                self.amul(Wo.h[:, 2 * g:2 * g + 2, :], Wo.h[:, 2 * g:2 * g + 2, :], 0.5, [("Wo", g)], [("Wo", g)])
            for i, nm in enumerate(("w_br_a", "w_br_b", "w_br_c")):
                self.dma(Wbr[i].h[:, :, :], D[nm].ap().rearrange("(kc p) n -> p kc n", p=128), (), [("WbrF", i)], q="pool")

            def out_tile(ti, col0, n, src, dst, r0):
                x_, xtok = xt.next()
                self.dma(x_.h[0:n, :], D[src].ap()[r0:r0 + n, :], (), [xtok])
                bi = 4 if ti == 16 else ti // 4
                for half in range(2):
                    pj, pjt = PJr.next()
                    for kc in range(8):
                        self.mm(pj.h[0:n, :], self.mT.h[:, kc, col0:col0 + n], Wo.h[:, kc, 512 * half:512 * half + 512],
                                kc == 0, kc == 7, [("Wo", kc // 2), ("mT", kc, bi)], [pjt])
                    y_, ytok = yo.next()
                    self.tt("dve", y_.h[0:n, :], pj.h[0:n, :], x_.h[0:n, 512 * half:512 * half + 512], ALU.add, [pjt, xtok], [ytok])
                    self.dma(D[dst].ap()[r0:r0 + n, 512 * half:512 * half + 512], y_.h[0:n, :], [ytok], [], q="act")

            PJr = self.bank_rot(["PJ", "ACC", "O3"])
            for ti in range(16):
                out_tile(ti, 128 * ti, 128, "x", "y", 128 * ti)
            self.stt("dve", P["yT"].h[:, :, 2048:2064], P["yT"].h[:, :, 2048:2064], 0.5, B["szS"].h[:, :, :], ALU.mult, ALU.mult,
                     [("yT", "samp")] + [("szS", j) for j in range(12)], [("y", j, 4) for j in range(12)])
            pb, pbt = Q["S"].next()
            first = True
            for c in range(8):
                for i in range(3):
                    col = (3 * c + i) * 16
                    for kc in range(4):
                        self.mm(pb.h[:, col:col + 16], Wbr[i].h[:, kc, 128 * c:128 * c + 128], P["yT"].h[:, 4 * i + kc, 2048:2064],
                                first, False, [("WbrF", i), ("y", 4 * i + kc, 4)], [pbt], sgc=True)
                        first = False
            self.stt("dve", prod.h[:, :, :], B["sgS"].h[:, :, :], 1.0, pb.h[:, 0:384].rearrange("p (a t) -> p a t", t=16),
                     ALU.add, ALU.mult, [pbt] + [("sgS", c) for c in range(8)], ["prodS"])
            pv = prod.h[:, :, :].rearrange("p (c i) t -> p c i t", i=3)
            self.tt("dve", pv[:, :, 0, :], pv[:, :, 0, :], pv[:, :, 1, :], ALU.add, ["prodS"], ["prodS"])
            self.tt("dve", self.mT.h[:, :, 2048:2064], pv[:, :, 0, :], pv[:, :, 2, :], ALU.add, ["prodS"],
                    [("mT", c, 4) for c in range(8)])
            out_tile(16, 2048, 16, "xs", "ys", 0)
            self.S.phase_end()


def _bucket_np(dist):
    n = np.maximum(dist, 0)
    nf = np.maximum(n, 1).astype(np.float32)
    v = np.log(nf / np.float32(16)) / np.float32(math.log(2048 / 16)) * np.float32(16)
    large = 16 + v.astype(np.int32)
    return np.where(n < 16, n, np.minimum(large, 31))


def _static_tables():
    pats = [(1, 128), (4, 128), (16, 128), (1, 127)]
    ohp = np.zeros((32, 4, 384), np.float32)
    ohs = np.zeros((32, 4, 128), np.float32)
    for p, (d, mx) in enumerate(pats):
        for x in range(383):
            delta = x - 127
            if 0 <= delta <= mx:
                ohp[_bucket_np(np.array(delta * d))[()], p, x] = 1.0
        for rho in range(128):
            steps = 128 - rho
            if steps <= mx:
                ohs[_bucket_np(np.array(steps * d))[()], p, rho] = 1.0
    masks = np.zeros((128, 1024), np.float32)
    for p in range(128):
        masks[p, 8 * (p % 64):8 * (p % 64) + 8] = 1.0
        masks[p, 512 + 4 * p:512 + 4 * p + 4] = 1.0
    return ohp.reshape(32, 4 * 384), ohs.reshape(32, 4 * 128), masks


_NC_CACHE = {}


def kernel(x_prompt, x_sample, mem_prompt, cache_a_k, cache_a_v, cache_b_k, cache_b_v, cache_mem_k, cache_mem_v,
           rel_bias, ln_g, w_in, gq_a, gk_a, gq_b, gk_b, gq_c, gk_c, sinks_b, mem_ln_g, w_mem_kv,
           w_br_a, w_br_b, w_br_c, w_out):
    f = lambda a: np.ascontiguousarray(np.asarray(a, dtype=np.float32))
    if "nc" not in _NC_CACHE:
        _NC_CACHE["nc"] = Builder().build()
    nc = _NC_CACHE["nc"]
    ohp, ohs, masks = _static_tables()
    shared = {"masks": masks, "rel_bias": f(rel_bias), "ln_g": f(ln_g), "w_in": f(w_in)[0], "gq_a": f(gq_a), "gk_a": f(gk_a),
              "gq_b": f(gq_b), "gk_b": f(gk_b), "gq_c": f(gq_c), "gk_c": f(gk_c), "sinks": f(sinks_b),
              "mem_ln_g": f(mem_ln_g), "w_mem_kv": f(w_mem_kv)[0], "w_br_a": f(w_br_a)[0], "w_br_b": f(w_br_b)[0],
              "w_br_c": f(w_br_c)[0], "w_out": f(w_out)[0], "ohp": ohp, "ohs": ohs}
    xp, xs, mp = f(x_prompt), f(x_sample), f(mem_prompt)
    cak, cav, cbk, cbv, cmk, cmv = (f(a)[0] for a in (cache_a_k, cache_a_v, cache_b_k, cache_b_v, cache_mem_k, cache_mem_v))
    in_maps = []
    for c in range(8):
        sl = slice(16 * c, 16 * c + 16)
        m = dict(shared)
        m.update({"x": xp[c], "xs": xs[sl, 0], "mem": mp[c],
                  "cak": cak[sl].reshape(16 * 2048, 512), "cav": cav[sl].reshape(16 * 2048, 512),
                  "cbk": cbk[sl].reshape(16 * 128, 128), "cbv": cbv[sl].reshape(16 * 128, 128),
                  "cmk": cmk[sl].reshape(16 * 256, 512), "cmv": cmv[sl].reshape(16 * 256, 512)})
        in_maps.append(m)
    res = run_bass_kernel_spmd(nc, in_maps, core_ids=list(range(8)))
    R = res.results
    cat = lambda k: np.stack([np.asarray(R[c][k], dtype=np.float32) for c in range(8)], 0)
    y = cat("y")
    ys = cat("ys").reshape(128, 1, 1024)
    pak = cat("pak").reshape(1, 8, 2048, 8, 64)
    pav = cat("pav").reshape(1, 8, 2048, 8, 64)
    pbk = cat("pbk").reshape(1, 8, 128, 2, 64)
    pbv = cat("pbv").reshape(1, 8, 128, 2, 64)
    pmk = cat("pmk").reshape(1, 8, 256, 4, 128)
    pmv = cat("pmv").reshape(1, 8, 256, 4, 128)
    sak = cat("sak").reshape(1, 128, 1, 8, 64)
    sav = cat("sav").reshape(1, 128, 1, 8, 64)
    sbk = cat("sbk").reshape(1, 128, 1, 2, 64)
    sbv = cat("sbv").reshape(1, 128, 1, 2, 64)
    return (y, ys, pak, pav, pbk, pbv, pmk, pmv, sak, sav, sbk, sbv)
```

```python
import math
from contextlib import ExitStack

import numpy as np
import concourse.bass as bass
import concourse.mybir as mybir
from concourse.bass_utils import run_bass_kernel_spmd

F32 = mybir.dt.float32
BF16 = mybir.dt.bfloat16
AF = mybir.ActivationFunctionType
ALU = mybir.AluOpType
AX = mybir.AxisListType

NT = 2064
EPS = 1e-6
IN_W = 7424
C_QA, C_KA, C_VA, C_ZA = 0, 512, 1024, 1536
C_QB, C_KB, C_VB, C_ZB = 2048, 2560, 2688, 2816
C_QC, C_ZC = 3328, 3840
C_GA = 4352
A_D = (1, 4, 16)


class Sched:
    ENGS = ("pe", "act", "dve", "pool", "sp")

    def __init__(self, nc, stack, dma_slots=None):
        self.nc = nc
        self.ops = []
        self.last_writer = {}
        self.readers = {}
        self.dma_slots = dma_slots or {"sp": 8, "pool": 6, "act": 4}
        self.sems = {}
        for e in ("pe", "act", "dve", "pool"):
            self.sems[e] = stack.enter_context(nc.semaphore("s_" + e))
        for q, n in self.dma_slots.items():
            for i in range(n):
                self.sems[(q, i)] = stack.enter_context(nc.semaphore(f"d_{q}{i}"))
        self.dma_count = {q: 0 for q in self.dma_slots}
        self.slot_last = {}
        self.emitted = 0
        self.sig_count = {k: 0 for k in self.sems}
        self.clock = {e: {} for e in self.ENGS}
        self.since_barrier_dma = []
        self.last_op_eng = {}
        self.excl = set()
        self.last_access = {}

    def add(self, eng, fn, reads=(), writes=(), dma=False, extra_deps=()):
        idx = len(self.ops)
        deps = set(extra_deps)
        for r in set(reads) | set(writes):
            if r in self.excl:
                d = self.last_access.get(r)
                if d is not None:
                    od = self.ops[d]
                    if dma or od["dma"] or od["eng"] != eng:
                        deps.add(d)
                self.last_access[r] = idx
        for r in reads:
            w = self.last_writer.get(r)
            if w is not None:
                deps.add(w)
        for r in writes:
            for d in [self.last_writer.get(r)] + self.readers.get(r, []):
                if d is None:
                    continue
                od = self.ops[d]
                if (not dma) and (not od["dma"]) and od["eng"] == eng:
                    continue
                deps.add(d)
        op = dict(eng=eng, fn=fn, dma=dma, deps=deps, sig=None, need_sig=dma, idx=idx)
        if dma:
            n = self.dma_count[eng]
            self.dma_count[eng] = n + 1
            slot = (eng, n % self.dma_slots[eng])
            prev = self.slot_last.get(slot)
            if prev is not None:
                deps.add(prev)
            self.slot_last[slot] = idx
            op["slot"] = slot
            self.since_barrier_dma.append(idx)
        for d in deps:
            self.ops[d]["need_sig"] = True
        for r in writes:
            self.last_writer[r] = idx
            self.readers[r] = []
        for r in reads:
            if r not in writes:
                self.readers.setdefault(r, []).append(idx)
        self.ops.append(op)
        if fn is not None and not dma:
            self.last_op_eng[eng] = idx
        return idx

    def barrier(self):
        deps = set(self.since_barrier_dma) | set(self.last_op_eng.values())
        self.since_barrier_dma = []
        for e in self.ENGS:
            self.add(e, None, extra_deps=deps)

    def emit(self):
        nc = self.nc
        lo, hi = self.emitted, len(self.ops)
        self.emitted = hi
        for op in self.ops[lo:hi]:
            if op["fn"] is None:
                continue
            if op["dma"]:
                k = op["slot"]
                self.sig_count[k] += 16
                op["sig"] = (k, self.sig_count[k])
            elif op["need_sig"]:
                k = op["eng"]
                self.sig_count[k] += 1
                op["sig"] = (k, self.sig_count[k])
        by_eng = {e: [] for e in self.ENGS}
        for op in self.ops[lo:hi]:
            e = op["eng"]
            clk = self.clock[e]
            wm = {}
            for d in sorted(op["deps"]):
                od = self.ops[d]
                if od["sig"] is None:
                    continue
                k, v = od["sig"]
                if clk.get(k, 0) < v:
                    wm[k] = max(wm.get(k, 0), v)
                    for kk, vv in od["clk"].items():
                        if clk.get(kk, 0) < vv:
                            clk[kk] = vv
            op["waits"] = list(wm.items())
            oc = dict(clk)
            if op["sig"] is not None:
                k, v = op["sig"]
                oc[k] = v
            op["clk"] = oc
            by_eng[e].append(op)

        def run(engobj, lst):
            for op in lst:
                for k, v in op["waits"]:
                    engobj.wait_ge(self.sems[k], v)
                if op["fn"] is None:
                    continue
                ins = op["fn"](engobj)
                if op["sig"] is not None:
                    ins.then_inc(self.sems[op["sig"][0]], 16 if op["dma"] else 1)

        with nc.Block() as block:
            if by_eng["pe"]:
                @block.tensor
                def _(t):
                    run(t, by_eng["pe"])
            if by_eng["act"]:
                @block.scalar
                def _(t):
                    run(t, by_eng["act"])
            if by_eng["dve"]:
                @block.vector
                def _(t):
                    run(t, by_eng["dve"])
            if by_eng["pool"]:
                @block.gpsimd
                def _(t):
                    run(t, by_eng["pool"])
            if by_eng["sp"]:
                @block.sync
                def _(t):
                    run(t, by_eng["sp"])
        for op in self.ops[lo:hi]:
            op["fn"] = None if op["fn"] is None else True

    def phase_end(self):
        self.barrier()
        self.emit()


class T:
    def __init__(self, h, shape):
        self.h = h
        self.shape = shape
        self.F = int(np.prod(shape[1:]))

    def ap(self, p0, n, col, dims):
        return bass.AP(self.h, p0 * self.F + col, [[self.F, n]] + [list(d) for d in dims])


class Rot:
    def __init__(self, ts, name):
        self.ts = ts
        self.name = name
        self.i = 0

    def next(self):
        j = self.i % len(self.ts)
        self.i += 1
        return self.ts[j], (self.name, j)


class RotL:
    def __init__(self, pairs):
        self.pairs = pairs
        self.i = 0

    def next(self):
        p = self.pairs[self.i % len(self.pairs)]
        self.i += 1
        return p


class Builder:
    def bank_rot(self, names):
        prs = []
        for nm in names:
            r = self.Q[nm]
            if isinstance(r, Rot):
                prs += [(t, (r.name, i)) for i, t in enumerate(r.ts)]
            else:
                prs.append((r, nm))
        return RotL(prs)

    def __init__(self):
        self.nc = bass.Bass("TRN2", target_bir_lowering=False)
        self.uid = 0
        self.only = None
        self.gate_stack = None

    def sb(self, st, shape, dt, name=None):
        self.uid += 1
        return T(st.enter_context(self.nc.sbuf_tensor(f"{name or 't'}_{self.uid}", shape, dt)), shape)

    def ps(self, st, shape, dt, name=None):
        self.uid += 1
        return T(st.enter_context(self.nc.psum_tensor(f"{name or 'p'}_{self.uid}", shape, dt)), shape)

    def rot(self, st, n, shape, dt, name):
        return Rot([self.sb(st, shape, dt, name) for _ in range(n)], name + str(self.uid))

    def dma(self, out, in_, r=(), w=(), q="sp", slow=False):
        if slow:
            self.S.add(q, lambda e, o=out, i=in_: e.dma_start(out=o, in_=i, allow_slow_non_contiguous=True), r, w, dma=True)
        else:
            self.S.add(q, lambda e, o=out, i=in_: e.dma_start(out=o, in_=i), r, w, dma=True)

    def mm(self, out, lhsT, rhs, start, stop, r, w, sgc=False, tp=None):
        if tp is None:
            self.S.add("pe", lambda e, o=out, l=lhsT, rr=rhs, a=start, b=stop, s=sgc:
                       e.matmul(o, l, rr, start=a, stop=b, skip_group_check=s), r, w)
        else:
            self.S.add("pe", lambda e, o=out, l=lhsT, rr=rhs, a=start, b=stop, s=sgc, t=tp:
                       e.matmul(o, l, rr, start=a, stop=b, skip_group_check=s, tile_position=t), r, w)

    def tr(self, out, in_, ident, r, w):
        self.S.add("pe", lambda e, o=out, i=in_, d=ident: e.transpose(out=o, in_=i, identity=d), r, w)

    def act(self, out, in_, func, r, w, scale=1.0, bias=None):
        if bias is None:
            self.S.add("act", lambda e, o=out, i=in_, f=func, s=scale: e.activation(out=o, in_=i, func=f, scale=s), r, w)
        else:
            self.S.add("act", lambda e, o=out, i=in_, f=func, s=scale, b=bias:
                       e.activation(out=o, in_=i, func=f, scale=s, bias=b), r, w)

    def acopy(self, out, in_, r, w):
        self.S.add("act", lambda e, o=out, i=in_: e.copy(out=o, in_=i), r, w)

    def amul(self, out, in_, mul, r, w):
        self.S.add("act", lambda e, o=out, i=in_, m=mul: e.mul(out=o, in_=i, mul=m), r, w)

    def tt(self, eng, out, in0, in1, op, r, w):
        self.S.add(eng, lambda e, o=out, a=in0, b=in1, p=op: e.tensor_tensor(out=o, in0=a, in1=b, op=p), r, w)

    def stt(self, eng, out, in0, scalar, in1, op0, op1, r, w):
        self.S.add(eng, lambda e, o=out, a=in0, s=scalar, b=in1, p0=op0, p1=op1:
                   e.scalar_tensor_tensor(out=o, in0=a, scalar=s, in1=b, op0=p0, op1=p1), r, w)

    def tsmul(self, eng, out, in0, scalar, r, w):
        self.S.add(eng, lambda e, o=out, a=in0, s=scalar: e.tensor_scalar_mul(out=o, in0=a, scalar1=s), r, w)

    def tcopy(self, eng, out, in_, r, w):
        self.S.add(eng, lambda e, o=out, i=in_: e.tensor_copy(out=o, in_=i), r, w)

    def recip(self, out, in_, r, w):
        self.S.add("dve", lambda e, o=out, i=in_: e.reciprocal(out=o, in_=i), r, w)

    def rsum(self, out, in_, r, w):
        self.S.add("dve", lambda e, o=out, i=in_: e.reduce_sum(out=o, in_=i, axis=AX.X), r, w)

    def memset(self, eng, ap, val, w):
        self.S.add(eng, lambda e, a=ap, v=val: e.memset(a, v), (), w)

    def asel(self, out, pattern, base, cm, r, w):
        self.S.add("pool", lambda e, o=out, p=pattern, b=base, c=cm: e.affine_select(
            out=o, in_=o, pattern=p, compare_op=ALU.not_equal, fill=1.0, base=b, channel_multiplier=c), r, w)

    def build(self):
        nc = self.nc

        def din(name, shape):
            return nc.dram_tensor(name, shape, F32, kind="ExternalInput")

        def dout(name, shape):
            return nc.dram_tensor(name, shape, F32, kind="ExternalOutput")

        D = self.D = {}
        for name, shape in [("x", [2048, 1024]), ("xs", [16, 1024]), ("mem", [256, 1024]),
                            ("cak", [16 * 2048, 512]), ("cav", [16 * 2048, 512]),
                            ("cbk", [16 * 128, 128]), ("cbv", [16 * 128, 128]),
                            ("cmk", [16 * 256, 512]), ("cmv", [16 * 256, 512]),
                            ("rel_bias", [32, 16]), ("ln_g", [1, 1024]), ("w_in", [1024, IN_W]),
                            ("gq_a", [1, 64]), ("gk_a", [1, 64]), ("gq_b", [1, 64]), ("gk_b", [1, 64]),
                            ("gq_c", [1, 128]), ("gk_c", [1, 128]), ("sinks", [1, 8]),
                            ("mem_ln_g", [1, 1024]), ("w_mem_kv", [1024, 1024]),
                            ("w_br_a", [512, 1024]), ("w_br_b", [512, 1024]), ("w_br_c", [512, 1024]),
                            ("w_out", [1024, 1024]), ("ohp", [32, 4 * 384]), ("ohs", [32, 4 * 128]), ("masks", [128, 1024])]:
            D[name] = din(name, shape)
        for name, shape in [("y", [2048, 1024]), ("ys", [16, 1024]), ("pak", [2048, 512]), ("pav", [2048, 512]),
                            ("pbk", [128, 128]), ("pbv", [128, 128]), ("pmk", [256, 512]), ("pmv", [256, 512]),
                            ("sak", [16, 512]), ("sav", [16, 512]), ("sbk", [16, 128]), ("sbv", [16, 128])]:
            D[name] = dout(name, shape)
        D["gS"] = nc.dram_tensor("gS", [64, 384], F32)
        D["qS"] = nc.dram_tensor("qS", [48, 512], F32)
        D["rD"] = nc.dram_tensor("rD", [20, 2048], F32)

        with ExitStack() as st:
            self.S = Sched(nc, st)
            P = self.P = {}
            P["hT"] = self.sb(st, [128, 8, NT], BF16, "hT")
            P["yT"] = self.sb(st, [128, 12, NT], BF16, "yT")
            P["identb"] = self.sb(st, [128, 128], BF16, "identb")
            P["Jf"] = self.sb(st, [128, 128], F32, "Jf")
            P["onesb"] = self.sb(st, [128, 128], BF16, "onesb")
            P["eps"] = self.sb(st, [128, 1], F32, "eps")
            P["kscA"] = self.sb(st, [128, 1], F32, "kscA")
            P["kscB"] = self.sb(st, [128, 1], F32, "kscB")
            P["kscC"] = self.sb(st, [128, 1], F32, "kscC")
            P["gkA"] = self.sb(st, [128, 64], F32, "gkA")
            P["gkB"] = self.sb(st, [128, 64], F32, "gkB")
            P["gkC"] = self.sb(st, [128, 128], F32, "gkC")
            P["gqsA"] = self.sb(st, [16, 64], F32, "gqsA")
            P["gqsB"] = self.sb(st, [16, 64], F32, "gqsB")
            P["gqsC"] = self.sb(st, [16, 128], F32, "gqsC")
            P["E"] = self.sb(st, [32, 16], F32, "E")
            P["e0"] = self.sb(st, [16, 16], F32, "e0")
            P["esink"] = self.sb(st, [16, 8], F32, "esink")
            P["sinkL"] = self.sb(st, [1, 8 * 128], F32, "sinkL")
            P["EBs"] = self.sb(st, [128, 4, 16], F32, "EBs")
            P["OHB"] = self.sb(st, [128, 16 * 16], BF16, "OHB")
            P["qsA"] = self.sb(st, [16, 512], F32, "qsA")
            P["qsB"] = self.sb(st, [16, 512], F32, "qsB")
            P["qsC"] = self.sb(st, [16, 512], F32, "qsC")
            P["ksA"] = self.sb(st, [16, 512], F32, "ksA")
            P["vsA"] = self.sb(st, [16, 512], F32, "vsA")
            P["ksB"] = self.sb(st, [16, 128], F32, "ksB")
            P["vsB"] = self.sb(st, [16, 128], F32, "vsB")
            Q = self.Q = {}
            Q["PJ"] = Rot([self.ps(st, [128, 512], F32, "PJ") for _ in range(2)], "PJ")
            Q["TRb"] = self.ps(st, [128, 1024], BF16, "TRb")
            Q["S"] = Rot([self.ps(st, [128, 512], F32, "S") for _ in range(2)], "S")
            Q["ACC"] = Rot([self.ps(st, [128, 512], F32, "ACC") for _ in range(2)], "ACC")
            Q["O3"] = self.ps(st, [128, 512], F32, "O3")
            for nm in ("PJ", "S", "ACC"):
                for i in range(2):
                    self.S.excl.add((Q[nm].name, i))
            self.S.excl |= {"O3", "TRb"}

            phases = [("setup", self.phase_setup), ("norm", self.phase_norm), ("A0", lambda: self.phase_A(0)),
                      ("A1", lambda: self.phase_A(1)), ("B", self.phase_B), ("C", self.phase_C),
                      ("tail", self.phase_tail_all)]
            for nm, fn in phases:
                if self.only is not None and nm not in self.only:
                    continue
                fn()
        return nc

    def phase_setup(self):
        P, D, Q = self.P, self.D, self.Q
        with ExitStack() as st:
            tmpf = self.sb(st, [128, 128], F32, "tmpf")
            rb = self.sb(st, [32, 16], F32, "rb")
            ohp = self.sb(st, [32, 4 * 384], F32, "ohp")
            ohs = self.sb(st, [32, 4 * 128], F32, "ohs")
            gv = self.rot(st, 2, [16, 384], F32, "gv")
            self.memset("pool", tmpf.h[:], 0.0, ["tmpf"])
            self.asel(tmpf.h[:], [[-1, 128]], 0, 1, ["tmpf"], ["tmpf"])
            self.tcopy("pool", P["identb"].h[:], tmpf.h[:], ["tmpf"], ["identb"])
            self.memset("pool", P["Jf"].h[:], 0.0, ["Jf"])
            self.asel(P["Jf"].h[:], [[1, 128]], -127, 1, ["Jf"], ["Jf"])
            self.memset("pool", P["onesb"].h[:], 1.0, ["onesb"])
            self.memset("pool", P["eps"].h[:], EPS, ["eps"])
            ohbf = self.sb(st, [128, 256], F32, "ohbf")
            self.memset("pool", ohbf.h[:], 0.0, ["ohbf"])
            self.asel(ohbf.h[:].rearrange("p (b m) -> p b m", m=16), [[1, 16], [-1, 16]], 0, 0, ["ohbf"], ["ohbf"])
            self.tcopy("pool", P["OHB"].h[:], ohbf.h[:], ["ohbf"], ["OHB"])
            self.S.barrier()
            for nm, src, n in (("kscA", "gq_a", 64), ("kscB", "gq_b", 64)):
                for half in range(2):
                    self.dma(P[nm].h[64 * half:64 * half + 64, :], bass.AP(D[src], 0, [[1, 64], [1, 1]]), (), [nm])
                self.tsmul("dve", P[nm].h[:], P[nm].h[:], 0.125, [nm], [nm])
            self.dma(P["kscC"].h[:, :], bass.AP(D["gq_c"], 0, [[1, 128], [1, 1]]), (), ["kscC"])
            self.tsmul("dve", P["kscC"].h[:], P["kscC"].h[:], 128 ** -0.5, ["kscC"], ["kscC"])
            for nm, src, n, np_ in (("gkA", "gk_a", 64, 128), ("gkB", "gk_b", 64, 128), ("gkC", "gk_c", 128, 128),
                                    ("gqsA", "gq_a", 64, 16), ("gqsB", "gq_b", 64, 16), ("gqsC", "gq_c", 128, 16)):
                self.dma(P[nm].h[:], bass.AP(D[src], 0, [[0, np_], [1, n]]), (), [nm])
            self.tsmul("dve", P["gqsA"].h[:], P["gqsA"].h[:], 0.125, ["gqsA"], ["gqsA"])
            self.tsmul("dve", P["gqsB"].h[:], P["gqsB"].h[:], 0.125, ["gqsB"], ["gqsB"])
            self.tsmul("dve", P["gqsC"].h[:], P["gqsC"].h[:], 128 ** -0.5, ["gqsC"], ["gqsC"])
            self.dma(rb.h[:], D["rel_bias"].ap(), (), ["rb"])
            self.act(P["E"].h[:], rb.h[:], AF.Exp, ["rb"], ["E"])
            self.dma(P["e0"].h[:], bass.AP(D["rel_bias"], 0, [[0, 16], [1, 16]]), (), ["e0"])
            self.act(P["e0"].h[:], P["e0"].h[:], AF.Exp, ["e0"], ["e0"])
            self.dma(P["esink"].h[:], bass.AP(D["sinks"], 0, [[0, 16], [1, 8]]), (), ["esink"])
            self.act(P["esink"].h[:], P["esink"].h[:], AF.Exp, ["esink"], ["esink"])
            self.memset("dve", P["sinkL"].h[:], 0.0, ["sinkL"])
            for h in range(8):
                lo = 64 if h % 2 == 0 else 0
                self.tcopy("dve", P["sinkL"].ap(0, 1, h * 128 + lo, [[1, 64]]), P["esink"].ap(0, 1, h, [[0, 64]]),
                           ["esink", "sinkL"], ["sinkL"])
            self.dma(ohp.h[:], D["ohp"].ap(), (), ["ohp"])
            self.dma(ohs.h[:], D["ohs"].ap(), (), ["ohs"])
            for p in range(4):
                pj, pjt = Q["PJ"].next()
                self.mm(pj.h[0:16, 0:384], P["E"].h[:, :], ohp.h[:, p * 384:(p + 1) * 384], True, True, ["E", "ohp"], [pjt])
                g, gt = gv.next()
                self.acopy(g.h[:], pj.h[0:16, 0:384], [pjt], [gt])
                self.dma(D["gS"].ap()[p * 16:(p + 1) * 16, :], g.h[:], [gt], ["gS"])
                pj, pjt = Q["PJ"].next()
                self.mm(pj.h[:, 0:16], ohs.h[:, p * 128:(p + 1) * 128], P["E"].h[:, :], True, True, ["E", "ohs"], [pjt])
                self.acopy(P["EBs"].h[:, p, :], pj.h[:, 0:16], [pjt], ["EBs"])
            self.S.phase_end()

    def norm_tiles(self, st, jobs, deep=True):
        P, D, Q = self.P, self.D, self.Q
        xt = self.rot(st, 3 if deep else 2, [128, 1024], F32, "xt")
        sq = self.rot(st, 2 if deep else 1, [128, 1024], F32, "sq")
        xb = self.rot(st, 3 if deep else 2, [128, 1024], BF16, "xb")
        s1 = self.rot(st, 4, [128, 4], F32, "s1")

        def stages(job):
            (src, r0, dstT, dst, g, gtok, col0, n) = job
            x_, xtok = xt.next()
            q_, qtok = sq.next()
            b_, btok = xb.next()
            s_, stok = s1.next()

            def A():
                self.dma(x_.h[0:n, :], D[src].ap()[r0:r0 + n, :], (), [xtok])
                self.act(q_.h[0:n, :], x_.h[0:n, :], AF.Square, [xtok], [qtok])
                self.rsum(s_.h[0:n, 0:1], q_.h[0:n, :], [qtok], [stok])

            def B():
                self.act(s_.h[0:n, 1:2], s_.h[0:n, 0:1], AF.Sqrt, [stok], [stok], scale=1.0 / 1024, bias=P["eps"].h[0:n, :])
                self.recip(s_.h[0:n, 2:3], s_.h[0:n, 1:2], [stok], [stok])
                self.stt("dve", b_.h[0:n, :], x_.h[0:n, :], s_.h[0:n, 2:3], g.h[0:n, :], ALU.mult, ALU.mult,
                         [xtok, stok, gtok], [btok])

            def C():
                for kc in range(8):
                    self.tr(Q["TRb"].h[:, kc * 128:kc * 128 + n], b_.h[0:n, kc * 128:(kc + 1) * 128],
                            P["identb"].h[0:n, 0:n], [btok, "identb"], ["TRb"])
                self.acopy(dstT.h[:, :, col0:col0 + n],
                           Q["TRb"].h[:, :].rearrange("p (k t) -> p k t", t=128)[:, :, 0:n], ["TRb"], [(dst, col0)])
            return [A, B, C]

        pipe = []
        for i in range(len(jobs) + 2):
            if i < len(jobs):
                pipe.append(stages(jobs[i]))
            for stg in pipe:
                if stg:
                    stg.pop(0)()
            pipe = [p_ for p_ in pipe if p_]

    def phase_norm(self):
        P, D, Q = self.P, self.D, self.Q
        with ExitStack() as st:
            gln = self.sb(st, [128, 1024], F32, "gln")
            self.dma(gln.h[:], bass.AP(D["ln_g"], 0, [[0, 128], [1, 1024]]), (), ["gln"])
            jobs = [("x", 128 * i, P["hT"], "hT", gln, "gln", 128 * i, 128) for i in range(16)]
            jobs.append(("xs", 0, P["hT"], "hT", gln, "gln", 2048, 16))
            self.norm_tiles(st, jobs)
            self.S.phase_end()

    def load_w(self, W, wtok, src, c0, width, o0, part=0):
        if not hasattr(self, "wparts"):
            self.wparts = {}
        self.wparts.setdefault(wtok, set()).add(part)
        v = self.D[src].ap().rearrange("(kc p) n -> p kc n", p=128)
        for g in range(4):
            self.dma(W.h[:, 2 * g:2 * g + 2, o0:o0 + width], v[:, 2 * g:2 * g + 2, c0:c0 + width], (), [(wtok, part, g)], q="pool")

    def wrd(self, wtok, kc):
        return [(wtok, p, kc // 2) for p in sorted(self.wparts[wtok])]

    def proj_tm(self, pj, pjt, hTname, col0, n, W, wtok, width, tokens=None):
        hT = self.P[hTname]
        for kc in range(8):
            if tokens is None:
                l = hT.h[:, kc, col0:col0 + n]
                rd = [(hTname, col0)] + self.wrd(wtok, kc)
            else:
                start, step = tokens
                l = hT.ap(0, 128, kc * hT.shape[2] + start, [[step, n]])
                rd = [(hTname, 128 * i) for i in range(16)] + self.wrd(wtok, kc)
            self.mm(pj.h[0:n, 0:width], l, W.h[:, kc, 0:width], kc == 0, kc == 7, rd, [pjt])

    def rms_stats_a(self, pj, pjt, n, c0, nh, hd, sq, ss):
        q_, qtok = sq.next()
        s_, stok = ss.next()
        w = nh * hd
        self.act(q_.h[0:n, 0:w], pj.h[0:n, c0:c0 + w], AF.Square, [pjt], [qtok])
        self.rsum(s_.h[0:n, 0:nh], q_.h[0:n, 0:w].rearrange("p (h e) -> p h e", e=hd), [qtok], [stok])
        return s_, stok

    def rms_stats_b(self, s_, stok, n, nh, hd):
        self.act(s_.h[0:n, 8:8 + nh], s_.h[0:n, 0:nh], AF.Sqrt, [stok], [stok], scale=1.0 / hd, bias=self.P["eps"].h[0:n, :])
        self.recip(s_.h[0:n, 16:16 + nh], s_.h[0:n, 8:8 + nh], [stok], [stok])

    def rms_stats(self, pj, pjt, n, c0, nh, hd, sq, ss):
        q_, qtok = sq.next()
        s_, stok = ss.next()
        w = nh * hd
        self.act(q_.h[0:n, 0:w], pj.h[0:n, c0:c0 + w], AF.Square, [pjt], [qtok])
        self.rsum(s_.h[0:n, 0:nh], q_.h[0:n, 0:w].rearrange("p (h e) -> p h e", e=hd), [qtok], [stok])
        self.act(s_.h[0:n, 8:8 + nh], s_.h[0:n, 0:nh], AF.Sqrt, [stok], [stok], scale=1.0 / hd, bias=self.P["eps"].h[0:n, :])
        self.recip(s_.h[0:n, 16:16 + nh], s_.h[0:n, 8:8 + nh], [stok], [stok])
        return s_, stok

    @staticmethod
    def bc_heads(t, n, col, nh, hd):
        return t.ap(0, n, col, [[1, nh], [0, hd]])

    def run_groups(self, groups, Pbuf, Pmbuf, skew=2):
        P = self.P
        Sr = self.bank_rot(["S", "PJ"])
        pendq = []

        def do_pv(item):
            pg, Pm_, pmtok_ = item
            pmtoks = [pmtok_ + (ui,) for ui in range(len(pg["units"]))]
            if pg.get("pre_sink"):
                A_, atok, h = pg["pre_sink"]
                self.mm(A_.h[:, :], P["sinkL"].ap(0, 1, h * 128, [[1, 128]]), P["ones_f"].h[0:1, :], True, False,
                        ["sinkL", "ones_f"], [atok], sgc=True)
            for s_i, t in enumerate(pg["tiles"]):
                self.mm(t["out"], t["v"], Pm_.h[:, 128 * s_i:128 * (s_i + 1)], t["start"], False,
                        pmtoks + t["vrd"], [pg["acctok"]], sgc=True)
            if pg.get("post"):
                pg["post"]()

        for g in groups:
            Sb, stok = Sr.next()
            nt = len(g["tiles"])
            for s_i, t in enumerate(g["tiles"]):
                self.mm(Sb.h[:, 128 * s_i:128 * (s_i + 1)], t["k"], t["q"], True, True, t["rd"], [stok])
            Pt, ptok = Pbuf.next()
            Pm, pmtok = Pmbuf.next()
            self.act(Pt.h[:, 0:128 * nt], Sb.h[:, 0:128 * nt], AF.Exp, [stok], [ptok])
            for ui, (c0, w, eb, ebtok) in enumerate(g["units"]):
                self.mmcnt = getattr(self, "mmcnt", 0) + 1
                self.tt("pool" if self.mmcnt % 4 == 0 else "dve", Pm.h[:, c0:c0 + w], Pt.h[:, c0:c0 + w], eb, ALU.mult,
                        [ptok, ebtok], [pmtok + (ui,)])
            pendq.append((g, Pm, pmtok))
            if len(pendq) > skew:
                do_pv(pendq.pop(0))
        while pendq:
            do_pv(pendq.pop(0))

    def build_EB(self, st, EB, ebname, combos):
        P, D, Q = self.P, self.D, self.Q
        R = self.rot(st, 1, [128, 256], F32, "R")
        for (idx, row) in combos:
            r_, rtok = R.next()
            self.dma(r_.h[:], bass.AP(D["gS"], row * 384, [[1, 128], [1, 256]]), ["gS"], [rtok])
            pj, pjt = Q["PJ"].next()
            self.mm(pj.h[:, 0:256], P["Jf"].h[:, :], r_.h[:, :], True, True, ["Jf", rtok], [pjt])
            self.acopy(EB.h[:, idx, :], pj.h[:, 0:256], [pjt], [(ebname, idx)])

    def phase_A(self, hf):
        P, D, Q = self.P, self.D, self.Q
        with ExitStack() as st:
            Wqk = self.sb(st, [128, 8, 512], BF16, "Wqk")
            Wv = self.sb(st, [128, 8, 256], BF16, "Wv")
            qT = self.sb(st, [128, 2, NT], BF16, "qTA")
            kT = self.sb(st, [128, 2, NT], BF16, "kTA")
            Vst = self.sb(st, [128, 48 * 384], BF16, "Vst")
            EB = self.sb(st, [128, 12, 256], F32, "EBA")
            acc3 = self.rot(st, 1, [128, 2048], F32, "acc3")
            sq = self.rot(st, 2, [128, 512], F32, "sqA")
            ss = self.rot(st, 4, [128, 24], F32, "ssA")
            qb = self.rot(st, 4, [128, 256], BF16, "qbA")
            kn = self.rot(st, 2, [128, 256], F32, "knA")
            ko = self.rot(st, 3, [128, 256], F32, "koA")
            kb = self.rot(st, 3, [128, 256], BF16, "kbA")
            vo = self.rot(st, 2, [128, 256], F32, "voA")
            rsm = self.rot(st, 4, [128, 16], F32, "rsmA")
            dcp = self.rot(st, 1, [128, 512], F32, "dcpA")
            maskD = self.sb(st, [128, 512], F32, "maskD")
            self.dma(maskD.h[:, :], D["masks"].ap()[:, 0:512], (), ["maskD"])

            self.load_w(Wqk, "Wqk", "w_in", C_QA + 256 * hf, 256, 0)
            self.load_w(Wqk, "Wqk", "w_in", C_KA + 256 * hf, 256, 256, part=1)
            self.load_w(Wv, "Wv", "w_in", C_VA + 256 * hf, 256, 0)
            self.memset("pool", Vst.ap(0, 128, 64, [[192, 96], [1, 64]]), 1.0, ["Vones"])

            def vdst(arr, ti):
                return Vst.ap(0, 128, (arr * 16 + ti) * 384, [[192, 2], [128, 2], [1, 64]])

            def vsrc(pj):
                return pj.h[:, 0:256].rearrange("p (a b e) -> p a b e", b=2, e=64)
            self.build_EB(st, EB, "EBA", [(hl * 3 + p, p * 16 + (4 * hf + hl)) for hl in range(4) for p in range(3)])

            import os
            tiles = [(128 * i, 128) for i in range(16)] + [(2048, 16)]
            tiles = tiles[:int(os.environ.get("DBG_TILES", "17"))]
            PJr = self.bank_rot(["PJ", "S", "ACC"])

            def tile_stages(ti, col0, n):
                samp = (ti == 16)
                X = {}

                def SA():
                    pj, pjt = PJr.next()
                    self.proj_tm(pj, pjt, "hT", col0, n, Wqk, "Wqk", 512)
                    X["pj"] = (pj, pjt)
                    X["s"] = self.rms_stats_a(pj, pjt, n, 0, 8, 64, sq, ss)
                    pv, pvt = PJr.next()
                    self.proj_tm(pv, pvt, "hT", col0, n, Wv, "Wv", 256)
                    if samp:
                        self.acopy(P["vsA"].h[0:16, 256 * hf:256 * hf + 256], pv.h[0:16, 0:256], [pvt], [("vsA", hf)])
                        self.dma(D["sav"].ap()[0:16, 256 * hf:256 * hf + 256], P["vsA"].h[0:16, 256 * hf:256 * hf + 256],
                                 [("vsA", hf)], [])
                    else:
                        v_, vtok = vo.next()
                        self.acopy(v_.h[0:n, :], pv.h[0:n, 0:256], [pvt], [vtok])
                        self.dma(D["pav"].ap()[col0:col0 + n, 256 * hf:256 * hf + 256], v_.h[0:n, :], [vtok], [])
                        self.tcopy("dve", vdst(0, ti), vsrc(pv), [pvt], [("V", 0, ti)])

                def SB1():
                    pj, pjt = X["pj"]
                    s_, stok = X["s"]
                    self.rms_stats_b(s_, stok, n, 8, 64)
                    q_, qtok = qb.next()
                    X["q"] = (q_, qtok)
                    self.tt("dve", q_.h[0:n, :].rearrange("p (h e) -> p h e", e=64),
                            pj.h[0:n, 0:256].rearrange("p (h e) -> p h e", e=64),
                            self.bc_heads(s_, n, 16, 4, 64), ALU.mult, [pjt, stok], [qtok])
                    n_, ntok = kn.next()
                    self.tt("dve", n_.h[0:n, :].rearrange("p (h e) -> p h e", e=64),
                            pj.h[0:n, 256:512].rearrange("p (h e) -> p h e", e=64),
                            self.bc_heads(s_, n, 20, 4, 64), ALU.mult, [pjt, stok], [ntok])
                    if samp:
                        self.tt("dve", P["qsA"].h[0:16, 256 * hf:256 * hf + 256].rearrange("p (h e) -> p h e", e=64),
                                pj.h[0:16, 0:256].rearrange("p (h e) -> p h e", e=64),
                                self.bc_heads(s_, 16, 16, 4, 64), ALU.mult, [pjt, stok], [("qsA", hf)])
                        self.tt("pool", P["qsA"].h[0:16, 256 * hf:256 * hf + 256].rearrange("p (h e) -> p h e", e=64),
                                P["qsA"].h[0:16, 256 * hf:256 * hf + 256].rearrange("p (h e) -> p h e", e=64),
                                P["gqsA"].ap(0, 16, 0, [[0, 4], [1, 64]]), ALU.mult, [("qsA", hf), "gqsA"], [("qsA", hf)])
                        otok = ("ksA", hf)
                        oap = P["ksA"].h[0:16, 256 * hf:256 * hf + 256]
                    else:
                        o_, otok = ko.next()
                        oap = o_.h[0:n, :]
                    X["o"] = (oap, otok)
                    self.tt("pool", oap.rearrange("p (h e) -> p h e", e=64), n_.h[0:n, :].rearrange("p (h e) -> p h e", e=64),
                            P["gkA"].ap(0, n, 0, [[0, 4], [1, 64]]), ALU.mult, [ntok, "gkA"], [otok])
                    dst = D["sak"].ap()[0:16, 256 * hf:256 * hf + 256] if samp else D["pak"].ap()[col0:col0 + n, 256 * hf:256 * hf + 256]
                    self.dma(dst, oap, [otok], [])

                def SB2():
                    oap, otok = X["o"]
                    b_, btok = kb.next()
                    X["b"] = (b_, btok)
                    self.acopy(b_.h[0:n, :], oap, [otok], [btok])

                def SC():
                    q_, qtok = X["q"]
                    b_, btok = X["b"]
                    for j in range(2):
                        self.tr(Q["TRb"].h[:, j * 128:j * 128 + n], q_.h[0:n, j * 128:(j + 1) * 128], P["identb"].h[0:n, 0:n],
                                [qtok, "identb"], ["TRb"])
                    for j in range(2):
                        self.tr(Q["TRb"].h[:, (2 + j) * 128:(2 + j) * 128 + n], b_.h[0:n, j * 128:(j + 1) * 128],
                                P["identb"].h[0:n, 0:n], [btok, "identb"], ["TRb"])
                    trv = Q["TRb"].h[:, 0:512].rearrange("p (k t) -> p k t", t=128)
                    self.acopy(qT.h[:, :, col0:col0 + n], trv[:, 0:2, 0:n], ["TRb"], [("qTA", ti)])
                    self.tsmul("dve", kT.h[:, :, col0:col0 + n], trv[:, 2:4, 0:n], P["kscA"].h[:, 0:1], ["TRb", "kscA"], [("kTA", ti)])
                return [SA, SB1, SB2, SC]

            pipe = []

            def step(newtile=None):
                nonlocal pipe
                if newtile is not None:
                    pipe.append(tile_stages(*newtile))
                for stg in pipe:
                    if stg:
                        stg.pop(0)()
                pipe = [p_ for p_ in pipe if p_]
            for ti, (col0, n) in enumerate(tiles):
                step((ti, col0, n))
            defer = []
            for arr in (1, 2):
                for ti in range(int(os.environ.get("DBG_VARR", "16"))):
                    if pipe and ti % 2 == 1:
                        step()
                    if arr == 1:
                        tb, r = ti // 4, ti % 4
                        tokens = (512 * tb + r, 4)
                    else:
                        tokens = (ti, 16)
                    pj, pjt = PJr.next()
                    self.proj_tm(pj, pjt, "hT", 0, 128, Wv, "Wv", 256, tokens=tokens)
                    if ti % 2 == 0:
                        self.acopy(vdst(arr, ti), vsrc(pj), [pjt], [("V", arr, ti)])
                    else:
                        self.tcopy("dve", vdst(arr, ti), vsrc(pj), [pjt], [("V", arr, ti)])

            while pipe:
                step()
            self.S.barrier()
            Pb = RotL([(T(Wqk.h[:, 2 * i:2 * i + 2, :].rearrange("p a b -> p (a b)").bitcast(F32), [128, 512]), ("PbA", i))
                       for i in range(4)])
            Pm = RotL([(T(Wv.h[:, 2 * i:2 * i + 2, :].rearrange("p a b -> p (a b)"), [128, 512]), ("PmA", i)) for i in range(4)])
            allq = [("qTA", i) for i in range(16)]
            allk = [("kTA", i) for i in range(16)]

            def vaug(arr, ti, hl):
                base = (arr * 16 + ti) * 384 + (hl // 2) * 192 + (0 if hl % 2 == 0 else 64)
                return Vst.ap(0, 128, base, [[1, 128]])

            groups = []
            for hl in range(4):
                hp, pr = hl % 2, hl // 2
                p0 = 64 * hp
                a3, a3tok = acc3.next()
                for rg in range(4):
                    tl = []
                    for k in range(4):
                        r = 4 * rg + k
                        tl.append(dict(k=kT.ap(p0, 64, pr * NT + r, [[16, 128]]), q=qT.ap(p0, 64, pr * NT + r, [[16, 128]]),
                                       v=vaug(2, r, hl), out=Q["O3"].h[:, 128 * k:128 * (k + 1)], start=True,
                                       rd=allq + allk, vrd=[("V", 2, r), "Vones"]))
                    ebap = EB.ap(0, 128, (hl * 3 + 2) * 256, [[0, 4], [1, 128]])

                    def post3(rg=rg, a3=a3, a3tok=a3tok):
                        self.acopy(a3.ap(0, 128, 4 * rg, [[1, 4], [16, 128]]),
                                   Q["O3"].h[:, :].rearrange("p (k i) -> p k i", i=128), ["O3"], [a3tok])
                    groups.append(dict(tiles=tl, units=[(0, 512, ebap, ("EBA", hl * 3 + 2))], post=post3, acctok="O3",
                                       ebview=True))
                for tb in range(4):
                    A_, atok = Q["ACC"].next()
                    units = []
                    first = True
                    for n_ in range(4 * tb, 4 * tb + 4):
                        tl = [dict(k=kT.ap(p0, 64, pr * NT + 128 * n_, [[1, 128]]), q=qT.ap(p0, 64, pr * NT + 128 * n_, [[1, 128]]),
                                   v=vaug(0, n_, hl), out=A_.h[:, 128 * (n_ - 4 * tb):128 * (n_ - 4 * tb + 1)], start=first,
                                   rd=[("qTA", n_), ("kTA", n_)], vrd=[("V", 0, n_), "Vones"])]
                        first = False
                        if n_ > 0:
                            tl.append(dict(k=kT.ap(p0, 64, pr * NT + 128 * (n_ - 1), [[1, 128]]), q=tl[0]["q"],
                                           v=vaug(0, n_ - 1, hl), out=tl[0]["out"], start=False,
                                           rd=[("qTA", n_), ("kTA", n_ - 1)], vrd=[("V", 0, n_ - 1), "Vones"]))
                        units.append((tl, (hl * 3 + 0)))
                    for r in range(4):
                        qa = qT.ap(p0, 64, pr * NT + 512 * tb + r, [[4, 128]])
                        oa = A_.ap(0, 128, r, [[4, 128]])
                        tl = [dict(k=kT.ap(p0, 64, pr * NT + 512 * tb + r, [[4, 128]]), q=qa, v=vaug(1, 4 * tb + r, hl),
                                   out=oa, start=False, rd=allq + allk, vrd=[("V", 1, 4 * tb + r), "Vones"])]
                        if tb > 0:
                            tl.append(dict(k=kT.ap(p0, 64, pr * NT + 512 * (tb - 1) + r, [[4, 128]]), q=qa,
                                           v=vaug(1, 4 * (tb - 1) + r, hl), out=oa, start=False, rd=allq + allk,
                                           vrd=[("V", 1, 4 * (tb - 1) + r), "Vones"]))
                        units.append((tl, (hl * 3 + 1)))
                    cur_t, cur_u = [], []
                    packed = []
                    for (tl, ebi) in units:
                        if len(cur_t) + len(tl) > 4:
                            packed.append((cur_t, cur_u))
                            cur_t, cur_u = [], []
                        c0 = 128 * len(cur_t)
                        cur_u.append((c0, 128 * len(tl), EB.ap(0, 128, ebi * 256, [[1, 128 * len(tl)]]), ("EBA", ebi)))
                        cur_t = cur_t + tl
                    packed.append((cur_t, cur_u))

                    def postA(A_=A_, atok=atok, a3=a3, a3tok=a3tok, tb=tb, hp=hp, pr=pr):
                        nr, dr = (0, 64) if hp == 0 else (64, 0)
                        self.tt("dve", A_.h[:, :], A_.h[:, :], a3.h[:, 512 * tb:512 * tb + 512], ALU.add, [atok, a3tok], [atok])
                        self.acopy(P["yT"].ap(nr, 64, (2 * hf + pr) * NT + 512 * tb, [[1, 512]]), A_.h[nr:nr + 64, :], [atok],
                                   [("yT", 2 * hf + pr, tb, hp)])
                        rs_, rstok = rsm.next()
                        dc_, dctok = dcp.next()
                        self.acopy(dc_.h[dr:dr + 64, :], A_.h[dr:dr + 64, :], [atok], [dctok])
                        self.tt("pool", dc_.h[dr:dr + 64, :], dc_.h[dr:dr + 64, :], maskD.h[dr:dr + 64, :], ALU.mult, [dctok, "maskD"], [dctok])
                        self.S.add("dve", lambda e, o=rs_.h[dr:dr + 64, 8:16], i=dc_.h[dr:dr + 64, :].rearrange("p (c j) -> p j c", j=8):
                                   e.reduce_sum(out=o, in_=i, axis=AX.X), [dctok], [rstok])
                        self.recip(rs_.h[dr:dr + 64, 0:8], rs_.h[dr:dr + 64, 8:16], [rstok], [rstok])
                        hg = 4 * hf + 2 * pr + hp
                        self.dma(bass.AP(D["rD"], hg * 2048 + 512 * tb, [[8, 64], [1, 8]]), rs_.h[dr:dr + 64, 0:8], [rstok],
                                 [("rD", hg, tb)])
                    for gi, (tl, ul) in enumerate(packed):
                        groups.append(dict(tiles=tl, units=ul, post=postA if gi == len(packed) - 1 else None, acctok=atok))
            for g in groups:
                if g.get("ebview"):
                    ebi = g["units"][0][3][1]
                    g["units"] = [(128 * k, 128, EB.ap(0, 128, ebi * 256, [[1, 128]]), ("EBA", ebi)) for k in range(4)]
            import os
            lim = os.environ.get("DBG_GROUPS")
            if lim is not None:
                groups = groups[:int(lim)]
            self.run_groups(groups, Pb, Pm)
            self.S.phase_end()

    def phase_B(self):
        P, D, Q = self.P, self.D, self.Q
        with ExitStack() as st:
            Wq = self.sb(st, [128, 8, 512], BF16, "WqB")
            Wkv = self.sb(st, [128, 8, 256], BF16, "WkvB")
            qT = self.sb(st, [128, 4, NT], BF16, "qTB")
            kT = self.sb(st, [128, 2, NT], BF16, "kTB")
            VB = self.sb(st, [128, 16, 320], BF16, "VB")
            EB = self.sb(st, [128, 8, 256], F32, "EBB")
            sq = self.rot(st, 3, [128, 512], F32, "sqB")
            ss = self.rot(st, 6, [128, 24], F32, "ssB")
            qb = self.rot(st, 4, [128, 512], BF16, "qbB")
            kn = self.rot(st, 2, [128, 128], F32, "knB")
            ko = self.rot(st, 3, [128, 128], F32, "koB")
            kb = self.rot(st, 4, [128, 256], BF16, "kbB")
            vo = self.rot(st, 2, [128, 128], F32, "voB")
            rsm = self.rot(st, 4, [128, 16], F32, "rsmB")
            dcp = self.rot(st, 2, [128, 512], F32, "dcpB")
            maskD = self.sb(st, [128, 512], F32, "maskD")
            self.dma(maskD.h[:, :], D["masks"].ap()[:, 0:512], (), ["maskD"])
            P["ones_f"] = self.sb(st, [1, 512], F32, "ones_f")
            self.memset("pool", P["ones_f"].h[:], 1.0, ["ones_f"])
            self.load_w(Wq, "WqB", "w_in", C_QB, 512, 0)
            self.load_w(Wkv, "WkvB", "w_in", C_KB, 256, 0)
            for c in (0, 128, 256):
                self.memset("pool", VB.h[:, :, c:c + 64], 1.0, ["VBones"])
            self.build_EB(st, EB, "EBB", [(h, 3 * 16 + 8 + h) for h in range(8)])
            tiles = [(128 * i, 128) for i in range(16)] + [(2048, 16)]
            PJr = self.bank_rot(["PJ", "S", "ACC"])

            def tile_stages(ti, col0, n):
                samp = (ti == 16)
                X = {}

                def SA():
                    pj, pjt = PJr.next()
                    self.proj_tm(pj, pjt, "hT", col0, n, Wq, "WqB", 512)
                    X["pq"] = (pj, pjt)
                    X["sq"] = self.rms_stats_a(pj, pjt, n, 0, 8, 64, sq, ss)
                    pk, pkt = PJr.next()
                    self.proj_tm(pk, pkt, "hT", col0, n, Wkv, "WkvB", 256)
                    X["pk"] = (pk, pkt)
                    X["sk"] = self.rms_stats_a(pk, pkt, n, 0, 2, 64, sq, ss)
                    if samp:
                        self.acopy(P["vsB"].h[0:16, :], pk.h[0:16, 128:256], [pkt], ["vsB"])
                        self.dma(D["sbv"].ap(), P["vsB"].h[0:16, :], ["vsB"], [])
                    else:
                        if ti == 15:
                            v_, vtok = vo.next()
                            self.acopy(v_.h[:, :], pk.h[:, 128:256], [pkt], [vtok])
                            self.dma(D["pbv"].ap(), v_.h[:, :], [vtok], [])
                        self.tcopy("dve", VB.ap(0, 128, ti * 320 + 64, [[128, 2], [1, 64]]),
                                   pk.h[:, 128:256].rearrange("p (g e) -> p g e", e=64), [pkt], [("VB", ti)])

                def SB1():
                    pj, pjt = X["pq"]
                    s_, stok = X["sq"]
                    self.rms_stats_b(s_, stok, n, 8, 64)
                    q_, qtok = qb.next()
                    X["q"] = (q_, qtok)
                    self.tt("dve", q_.h[0:n, :].rearrange("p (h e) -> p h e", e=64),
                            pj.h[0:n, 0:512].rearrange("p (h e) -> p h e", e=64),
                            self.bc_heads(s_, n, 16, 8, 64), ALU.mult, [pjt, stok], [qtok])
                    if samp:
                        self.tt("dve", P["qsB"].h[0:16, :].rearrange("p (h e) -> p h e", e=64),
                                pj.h[0:16, 0:512].rearrange("p (h e) -> p h e", e=64),
                                self.bc_heads(s_, 16, 16, 8, 64), ALU.mult, [pjt, stok], ["qsB"])
                        self.tt("pool", P["qsB"].h[0:16, :].rearrange("p (h e) -> p h e", e=64),
                                P["qsB"].h[0:16, :].rearrange("p (h e) -> p h e", e=64),
                                P["gqsB"].ap(0, 16, 0, [[0, 8], [1, 64]]), ALU.mult, ["qsB", "gqsB"], ["qsB"])
                    pk, pkt = X["pk"]
                    s2, s2tok = X["sk"]
                    self.rms_stats_b(s2, s2tok, n, 2, 64)
                    n_, ntok = kn.next()
                    self.tt("dve", n_.h[0:n, :].rearrange("p (h e) -> p h e", e=64),
                            pk.h[0:n, 0:128].rearrange("p (h e) -> p h e", e=64),
                            self.bc_heads(s2, n, 16, 2, 64), ALU.mult, [pkt, s2tok], [ntok])
                    if samp:
                        o_, otok = P["ksB"], "ksB"
                    else:
                        o_, otok = ko.next()
                    X["o"] = (o_, otok)
                    self.tt("pool", o_.h[0:n, :].rearrange("p (h e) -> p h e", e=64), n_.h[0:n, :].rearrange("p (h e) -> p h e", e=64),
                            P["gkB"].ap(0, n, 0, [[0, 2], [1, 64]]), ALU.mult, [ntok, "gkB"], [otok])
                    if samp:
                        self.dma(D["sbk"].ap(), o_.h[0:16, :], [otok], [])
                    elif ti == 15:
                        self.dma(D["pbk"].ap(), o_.h[:, :], [otok], [])

                def SB2():
                    o_, otok = X["o"]
                    b_, btok = kb.next()
                    X["b"] = (b_, btok)
                    self.acopy(b_.h[0:n, :].rearrange("p (g r e) -> p g r e", r=2, e=64),
                               o_.ap(0, n, 0, [[64, 2], [0, 2], [1, 64]]), [otok], [btok])

                def SC():
                    q_, qtok = X["q"]
                    b_, btok = X["b"]
                    for j in range(4):
                        self.tr(Q["TRb"].h[:, j * 128:j * 128 + n], q_.h[0:n, j * 128:(j + 1) * 128], P["identb"].h[0:n, 0:n],
                                [qtok, "identb"], ["TRb"])
                    for j in range(2):
                        self.tr(Q["TRb"].h[:, (4 + j) * 128:(4 + j) * 128 + n], b_.h[0:n, j * 128:(j + 1) * 128],
                                P["identb"].h[0:n, 0:n], [btok, "identb"], ["TRb"])
                    trv = Q["TRb"].h[:, 0:512].rearrange("p (k t) -> p k t", t=128)
                    self.acopy(qT.h[:, :, col0:col0 + n], trv[:, 0:4, 0:n], ["TRb"], [("qTB", ti)])
                    trk = Q["TRb"].h[:, 512:768].rearrange("p (k t) -> p k t", t=128)
                    self.tsmul("dve", kT.h[:, :, col0:col0 + n], trk[:, 0:2, 0:n], P["kscB"].h[:, 0:1], ["TRb", "kscB"], [("kTB", ti)])
                return [SA, SB1, SB2, SC]

            pipe = []
            for it in range(len(tiles) + 3):
                if it < len(tiles):
                    pipe.append(tile_stages(it, *tiles[it]))
                for stg in pipe:
                    if stg:
                        stg.pop(0)()
                pipe = [p_ for p_ in pipe if p_]
            self.S.barrier()
            Pb = RotL([(T(Wq.h[:, 2 * i:2 * i + 2, :].rearrange("p a b -> p (a b)").bitcast(F32), [128, 512]), ("PbB", i))
                       for i in range(4)])
            Pm = RotL([(T(Wkv.h[:, 2 * i:2 * i + 2, :].rearrange("p a b -> p (a b)"), [128, 512]), ("PmB", i)) for i in range(4)])
            groups = []
            for h in range(8):
                g_, hp, pr = h // 4, h % 2, h // 2
                p0 = 64 * hp
                vcol = (64 + 128 * g_) if hp == 0 else (128 * g_)
                for tb in range(4):
                    A_, atok = Q["ACC"].next()
                    units = []
                    for n_ in range(4 * tb, 4 * tb + 4):
                        oa = A_.h[:, 128 * (n_ - 4 * tb):128 * (n_ - 4 * tb + 1)]
                        qa = qT.ap(p0, 64, pr * NT + 128 * n_, [[1, 128]])
                        tl = [dict(k=kT.ap(p0, 64, g_ * NT + 128 * n_, [[1, 128]]), q=qa,
                                   v=VB.ap(0, 128, n_ * 320 + vcol, [[1, 128]]), out=oa, start=False,
                                   rd=[("qTB", n_), ("kTB", n_)], vrd=[("VB", n_), "VBones"])]
                        if n_ > 0:
                            tl.append(dict(k=kT.ap(p0, 64, g_ * NT + 128 * (n_ - 1), [[1, 128]]), q=qa,
                                           v=VB.ap(0, 128, (n_ - 1) * 320 + vcol, [[1, 128]]), out=oa, start=False,
                                           rd=[("qTB", n_), ("kTB", n_ - 1)], vrd=[("VB", n_ - 1), "VBones"]))
                        units.append(tl)
                    packed = [(units[0] + units[1]), (units[2] + units[3])]

                    def postB(A_=A_, atok=atok, tb=tb, hp=hp, pr=pr):
                        nr, dr = (0, 64) if hp == 0 else (64, 0)
                        self.acopy(P["yT"].ap(nr, 64, (4 + pr) * NT + 512 * tb, [[1, 512]]), A_.h[nr:nr + 64, :], [atok],
                                   [("yT", 4 + pr, tb, hp)])
                        rs_, rstok = rsm.next()
                        dc_, dctok = dcp.next()
                        self.acopy(dc_.h[dr:dr + 64, :], A_.h[dr:dr + 64, :], [atok], [dctok])
                        self.tt("pool", dc_.h[dr:dr + 64, :], dc_.h[dr:dr + 64, :], maskD.h[dr:dr + 64, :], ALU.mult, [dctok, "maskD"], [dctok])
                        self.S.add("dve", lambda e, o=rs_.h[dr:dr + 64, 8:16], i=dc_.h[dr:dr + 64, :].rearrange("p (c j) -> p j c", j=8):
                                   e.reduce_sum(out=o, in_=i, axis=AX.X), [dctok], [rstok])
                        self.recip(rs_.h[dr:dr + 64, 0:8], rs_.h[dr:dr + 64, 8:16], [rstok], [rstok])
                        hg = 8 + 2 * pr + hp
                        self.dma(bass.AP(D["rD"], hg * 2048 + 512 * tb, [[8, 64], [1, 8]]), rs_.h[dr:dr + 64, 0:8], [rstok],
                                 [("rD", hg, tb)])
                    for gi, tl in enumerate(packed):
                        ul = []
                        c = 0
                        i = 0
                        while i < len(tl):
                            w = 2 if (i + 1 < len(tl) and tl[i + 1]["out"] is tl[i]["out"]) else 1
                            ul.append((128 * i, 128 * w, EB.ap(0, 128, h * 256, [[1, 128 * w]]), ("EBB", h)))
                            i += w
                        gd = dict(tiles=tl, units=ul, post=postB if gi == 1 else None, acctok=atok)
                        if gi == 0:
                            gd["pre_sink"] = (A_, atok, h)
                        groups.append(gd)
            self.run_groups(groups, Pb, Pm)
            self.S.phase_end()

    def phase_C(self):
        P, D, Q = self.P, self.D, self.Q
        with ExitStack() as st:
            Wq = self.sb(st, [128, 8, 512], BF16, "WqC")
            Wm = self.sb(st, [128, 8, 1024], BF16, "Wm")
            qT = self.sb(st, [128, 4, NT], BF16, "qTC")
            mkT = self.sb(st, [128, 4, 256], BF16, "mkT")
            mvb = self.sb(st, [128, 2, 512], BF16, "mvb")
            sq = self.rot(st, 2, [128, 512], F32, "sqC")
            ss = self.rot(st, 2, [128, 24], F32, "ssC")
            qb = self.rot(st, 4, [128, 512], BF16, "qbC")
            kn = self.rot(st, 2, [128, 512], F32, "knC")
            ko = self.rot(st, 2, [128, 512], F32, "koC")
            kb = self.rot(st, 2, [128, 512], BF16, "kbC")
            vo = self.rot(st, 2, [128, 512], F32, "voC")
            Pm = self.rot(st, 2, [128, 512], BF16, "PmC")
            rsm = self.rot(st, 4, [128, 16], F32, "rsmC")
            dcp = self.rot(st, 2, [128, 512], F32, "dcpC")
            maskD = self.sb(st, [128, 512], F32, "maskD")
            self.dma(maskD.h[:, :], D["masks"].ap()[:, 512:1024], (), ["maskD"])
            self.load_w(Wq, "WqC", "w_in", C_QC, 512, 0)
            self.load_w(Wm, "Wm", "w_mem_kv", 0, 1024, 0)
            P["memT"] = self.sb(st, [128, 8, 256], BF16, "memT")
            with ExitStack() as st3:
                gmem = self.sb(st3, [128, 1024], F32, "gmem")
                self.dma(gmem.h[:], bass.AP(D["mem_ln_g"], 0, [[0, 128], [1, 1024]]), (), ["gmem"])
                self.norm_tiles(st3, [("mem", 128 * i, P["memT"], "memT", gmem, "gmem", 128 * i, 128) for i in range(2)], deep=False)
                self.S.phase_end()
            for mt in range(2):
                pj, pjt = Q["PJ"].next()
                for kc in range(8):
                    self.mm(pj.h[:, :], P["memT"].h[:, kc, 128 * mt:128 * mt + 128], Wm.h[:, kc, 0:512], kc == 0, kc == 7,
                            [("memT", 128 * mt)] + self.wrd("Wm", kc), [pjt])
                s_, stok = self.rms_stats(pj, pjt, 128, 0, 4, 128, sq, ss)
                n_, ntok = kn.next()
                self.tt("dve", n_.h[:, :].rearrange("p (h e) -> p h e", e=128), pj.h[:, :].rearrange("p (h e) -> p h e", e=128),
                        self.bc_heads(s_, 128, 16, 4, 128), ALU.mult, [pjt, stok], [ntok])
                o_, otok = ko.next()
                self.tt("pool", o_.h[:, :].rearrange("p (h e) -> p h e", e=128), n_.h[:, :].rearrange("p (h e) -> p h e", e=128),
                        P["gkC"].ap(0, 128, 0, [[0, 4], [1, 128]]), ALU.mult, [ntok, "gkC"], [otok])
                self.dma(D["pmk"].ap()[128 * mt:128 * mt + 128, :], o_.h[:, :], [otok], [])
                b_, btok = kb.next()
                self.acopy(b_.h[:, :], o_.h[:, :], [otok], [btok])
                for j in range(4):
                    self.tr(Q["TRb"].h[:, j * 128:(j + 1) * 128], b_.h[:, j * 128:(j + 1) * 128], P["identb"].h[:, :],
                            [btok, "identb"], ["TRb"])
                trv = Q["TRb"].h[:, 0:512].rearrange("p (k t) -> p k t", t=128)
                self.tsmul("dve", mkT.h[:, :, 128 * mt:128 * mt + 128], trv, P["kscC"].h[:, 0:1], ["TRb", "kscC"], [("mkT", mt)])
                pj, pjt = Q["PJ"].next()
                for kc in range(8):
                    self.mm(pj.h[:, :], P["memT"].h[:, kc, 128 * mt:128 * mt + 128], Wm.h[:, kc, 512:1024], kc == 0, kc == 7,
                            [("memT", 128 * mt)] + self.wrd("Wm", kc), [pjt])
                v_, vtok = vo.next()
                self.acopy(v_.h[:, :], pj.h[:, :], [pjt], [vtok])
                self.dma(D["pmv"].ap()[128 * mt:128 * mt + 128, :], v_.h[:, :], [vtok], [])
                self.tcopy("dve", mvb.h[:, mt, :], pj.h[:, :], [pjt], [("mvb", mt)])
            tiles = [(128 * i, 128) for i in range(16)] + [(2048, 16)]
            deferC = []
            PJr = self.bank_rot(["PJ", "S", "ACC"])
            for ti, (col0, n) in enumerate(tiles):
                pj, pjt = PJr.next()
                self.proj_tm(pj, pjt, "hT", col0, n, Wq, "WqC", 512)
                s_, stok = self.rms_stats(pj, pjt, n, 0, 4, 128, sq, ss)
                if ti == 16:
                    self.tt("dve", P["qsC"].h[0:16, :].rearrange("p (h e) -> p h e", e=128),
                            pj.h[0:16, 0:512].rearrange("p (h e) -> p h e", e=128),
                            self.bc_heads(s_, 16, 16, 4, 128), ALU.mult, [pjt, stok], ["qsC"])
                    self.tt("pool", P["qsC"].h[0:16, :].rearrange("p (h e) -> p h e", e=128),
                            P["qsC"].h[0:16, :].rearrange("p (h e) -> p h e", e=128),
                            P["gqsC"].ap(0, 16, 0, [[0, 4], [1, 128]]), ALU.mult, ["qsC", "gqsC"], ["qsC"])
                    continue
                q_, qtok = qb.next()
                self.tt("dve", q_.h[0:n, :].rearrange("p (h e) -> p h e", e=128),
                        pj.h[0:n, 0:512].rearrange("p (h e) -> p h e", e=128),
                        self.bc_heads(s_, n, 16, 4, 128), ALU.mult, [pjt, stok], [qtok])
                def st2c(q_=q_, qtok=qtok, col0=col0, n=n, ti=ti):
                    for j in range(4):
                        self.tr(Q["TRb"].h[:, j * 128:j * 128 + n], q_.h[0:n, j * 128:(j + 1) * 128], P["identb"].h[0:n, 0:n],
                                [qtok, "identb"], ["TRb"])
                    trv = Q["TRb"].h[:, 0:512].rearrange("p (k t) -> p k t", t=128)
                    self.acopy(qT.h[:, :, col0:col0 + n], trv[:, 0:4, 0:n], ["TRb"], [("qTC", ti)])
                deferC.append(st2c)
                while len(deferC) > 2:
                    deferC.pop(0)()
            while deferC:
                deferC.pop(0)()
            self.dma(D["qS"].ap()[0:16, :], P["qsA"].h[:, :], [("qsA", 0), ("qsA", 1)], ["qS"])
            self.dma(D["qS"].ap()[16:32, :], P["qsB"].h[:, :], ["qsB"], ["qS"])
            self.dma(D["qS"].ap()[32:48, :], P["qsC"].h[:, :], ["qsC"], ["qS"])
            jobs = [(h, tb, mt) for h in range(4) for tb in range(4) for mt in range(2)]
            accs = {}
            pend = None
            for job in jobs + [None]:
                cur = None
                if job is not None:
                    h, tb, mt = job
                    Sb, stok = Q["S"].next()
                    self.mm(Sb.h[:, :], mkT.h[:, h, 128 * mt:128 * mt + 128], qT.h[:, h, 512 * tb:512 * tb + 512], True, True,
                            [("mkT", mt)] + [("qTC", 4 * tb + i) for i in range(4)], [stok])
                    Pm_, pmtok = Pm.next()
                    self.act(Pm_.h[:, :], Sb.h[:, :], AF.Exp, [stok], [pmtok])
                    cur = (job, Pm_, pmtok)
                if pend is not None:
                    (h, tb, mt), Pm_, pmtok = pend
                    if mt == 0:
                        accs[(h, tb)] = Q["ACC"].next()
                    A_, atok = accs[(h, tb)]
                    self.mm(A_.h[:, :], mvb.h[:, mt, 128 * h:128 * h + 128], Pm_.h[:, :], mt == 0, mt == 1,
                            [pmtok, ("mvb", mt)], [atok])
                    self.mm(Q["O3"].h[:, :], P["onesb"].h[:, :], Pm_.h[:, :], mt == 0, mt == 1, [pmtok, "onesb"], ["O3"])
                    if mt == 1:
                        self.acopy(P["yT"].h[:, 8 + h, 512 * tb:512 * tb + 512], A_.h[:, :], [atok], [("yT", 8 + h, tb, 0)])
                        rs_, rstok = rsm.next()
                        dc_, dctok = dcp.next()
                        self.acopy(dc_.h[:, :], Q["O3"].h[:, :], ["O3"], [dctok])
                        self.tt("pool", dc_.h[:, :], dc_.h[:, :], maskD.h[:, :], ALU.mult, [dctok, "maskD"], [dctok])
                        self.S.add("dve", lambda e, o=rs_.h[:, 8:12], i=dc_.h[:, :].rearrange("p (c j) -> p j c", j=4):
                                   e.reduce_sum(out=o, in_=i, axis=AX.X), [dctok], [rstok])
                        self.recip(rs_.h[:, 0:4], rs_.h[:, 8:12], [rstok], [rstok])
                        self.dma(bass.AP(D["rD"], (16 + h) * 2048 + 512 * tb, [[4, 128], [1, 4]]), rs_.h[:, 0:4], [rstok],
                                 [("rD", 16 + h, tb)])
                pend = cur
            self.S.phase_end()

    def sample_alloc(self, st, stM):
        B = self.SB = {}
        B["szS"] = self.sb(stM, [128, 12, 16], F32, "szS")
        B["sgS"] = self.sb(stM, [128, 24, 16], F32, "sgS")
        B["Kt"] = self.rot(st, 4, [128, 512], F32, "Kt")
        B["Vt"] = self.rot(st, 5, [128, 512], F32, "Vt")
        B["qbc"] = self.rot(st, 2, [128, 512], F32, "qbc")
        B["prod"] = self.rot(st, 2, [128, 512], F32, "prod")
        B["Wb"] = self.rot(st, 3, [128, 512], BF16, "Wb")
        B["sm"] = self.rot(st, 5, [128, 24], F32, "sm")
        B["pmb"] = self.rot(st, 5, [128, 8], BF16, "pmb")
        B["fin"] = self.sb(st, [16, 64], F32, "fin")
        B["fb"] = self.sb(st, [16, 64], F32, "fb")
        B["fc"] = self.sb(st, [16, 8], F32, "fc")
        B["t1"] = self.sb(st, [16, 512], F32, "t1")
        B["t2"] = self.sb(st, [16, 512], F32, "t2")
        B["ob"] = self.sb(st, [16, 1536], BF16, "ob")

    def sample_tiles(self):
        P, D, Q, B = self.P, self.D, self.Q, self.SB
        NUMB = Q["ACC"].ts[0]
        DENB = Q["ACC"].ts[1]
        p0 = {"A": 0, "B": 32, "C": 64}
        dcol = {"A": 0, "B": 8, "C": 16}
        cfg = {"A": dict(nh=8, hd=64, mi=0), "B": dict(nh=8, hd=64, mi=1), "C": dict(nh=4, hd=128, mi=2)}
        total = {"A": 48, "B": 16, "C": 32}
        count = {"A": 0, "B": 0, "C": 0}
        tiles = []
        for b in range(16):
            for mix in ("A", "B", "C"):
                for t in range({"A": 3, "B": 1, "C": 2}[mix]):
                    tiles.append((b, mix, t))
        loaded = {}

        def load(i):
            b, mix, t = tiles[i]
            k_, ktok = B["Kt"].next()
            v_, vtok = B["Vt"].next()
            if mix == "A":
                d = A_D[t]
                base = (b * 2048 + 2048 - 128 * d) * 512
                self.dma(k_.h[:, :], bass.AP(D["cak"], base, [[d * 512, 128], [1, 512]]), (), [ktok])
                self.dma(v_.h[:, :], bass.AP(D["cav"], base, [[d * 512, 128], [1, 512]]), (), [vtok], q="act")
            elif mix == "B":
                self.dma(k_.h[:, 0:128], D["cbk"].ap()[128 * b:128 * b + 128, :], (), [ktok])
                self.dma(v_.h[:, 0:128], D["cbv"].ap()[128 * b:128 * b + 128, :], (), [vtok], q="act")
            else:
                r0 = 256 * b + 128 * t
                self.dma(k_.h[:, :], D["cmk"].ap()[r0:r0 + 128, :], (), [ktok])
                self.dma(v_.h[:, :], D["cmv"].ap()[r0:r0 + 128, :], (), [vtok], q="act")
            loaded[i] = (k_, ktok, v_, vtok)

        PF = 2
        for i in range(min(PF, len(tiles))):
            load(i)
        qcur = {}
        state = {"dfirst": True}

        def make_stages(i, b, mix, t):
            c = cfg[mix]
            nh, hd = c["nh"], c["hd"]
            if t == 0:
                qb_, qbt = B["qbc"].next()
                self.dma(qb_.h[:, :], bass.AP(D["qS"], (c["mi"] * 16 + b) * 512, [[0, 128], [1, 512]]), ["qS"], [qbt])
                qcur[mix] = (qb_, qbt)
            qb_, qbt = qcur[mix]
            k_, ktok, v_, vtok = loaded.pop(i)
            if mix == "A":
                kin = k_.h[:, :].rearrange("p (h e) -> p h e", e=64)
                vin = v_.h[:, :].rearrange("p (h e) -> p h e", e=64)
                ebap = P["EBs"].h[:, t, 0:8]
            elif mix == "B":
                kin = k_.ap(0, 128, 0, [[64, 2], [0, 4], [1, 64]])
                vin = v_.ap(0, 128, 0, [[64, 2], [0, 4], [1, 64]])
                ebap = P["EBs"].h[:, 3, 8:16]
            else:
                kin = k_.h[:, :].rearrange("p (h e) -> p h e", e=128)
                vin = v_.h[:, :].rearrange("p (h e) -> p h e", e=128)
                ebap = None
            pr_, prtok = B["prod"].next()
            s_, stok = B["sm"].next()
            pm_, pmtok = B["pmb"].next()
            w_, wtok = B["Wb"].next()
            count[mix] += 1
            cnt = count[mix]

            def stA():
                if mix == "B":
                    pview = pr_.h[:, :].rearrange("p (g r e) -> p g r e", r=4, e=64)
                    qview = qb_.h[:, :].rearrange("p (g r e) -> p g r e", r=4, e=64)
                else:
                    pview = pr_.h[:, :].rearrange("p (h e) -> p h e", e=hd)
                    qview = qb_.h[:, :].rearrange("p (h e) -> p h e", e=hd)
                self.tt("dve", pview, kin, qview, ALU.mult, [ktok, qbt], [prtok])
                self.rsum(s_.h[:, 0:nh], pr_.h[:, :].rearrange("p (h e) -> p h e", e=hd), [prtok], [stok])

            def stB():
                if ebap is None:
                    self.act(pm_.h[:, 0:nh], s_.h[:, 0:nh], AF.Exp, [stok], [pmtok])
                else:
                    self.act(s_.h[:, 8:8 + nh], s_.h[:, 0:nh], AF.Exp, [stok], [stok])

            def stC():
                if ebap is not None:
                    self.tt("dve", pm_.h[:, 0:nh], s_.h[:, 8:8 + nh], ebap, ALU.mult, [stok, "EBs"], [pmtok])
                if mix == "B":
                    wview = w_.h[:, :].rearrange("p (g r e) -> p g r e", r=4, e=64)
                    pbc = pm_.ap(0, 128, 0, [[4, 2], [1, 4], [0, 64]])
                else:
                    wview = w_.h[:, :].rearrange("p (h e) -> p h e", e=hd)
                    pbc = pm_.ap(0, 128, 0, [[1, nh], [0, hd]])
                self.tt("pool", wview, vin, pbc, ALU.mult, [vtok, pmtok], [wtok])

            def stD():
                q0 = p0[mix]
                self.mm(NUMB.h[q0:q0 + 16, :], P["OHB"].h[:, 16 * b:16 * b + 16], w_.h[:, :], cnt == 1, cnt == total[mix],
                        ["OHB", wtok], ["num" + mix], sgc=True, tp=(0, q0))
                self.mm(DENB.h[0:16, dcol[mix]:dcol[mix] + nh], P["OHB"].h[:, 16 * b:16 * b + 16], pm_.h[:, 0:nh],
                        state["dfirst"], False, ["OHB", pmtok], ["den"], sgc=True)
                state["dfirst"] = False
            return [stA, stB, stC, stD]

        pipe = []
        for i in range(len(tiles) + 3):
            if i < len(tiles):
                if i + PF < len(tiles):
                    load(i + PF)
                pipe.append(make_stages(i, *tiles[i]))
            for stg in pipe:
                if stg:
                    stg.pop(0)()
            pipe = [p_ for p_ in pipe if p_]
            yield

    def pump(self, gen, k=1):
        if gen is None:
            return
        for _ in range(k):
            try:
                next(gen)
            except StopIteration:
                return

    def sample_finalize(self):
        P, D, Q, B = self.P, self.D, self.Q, self.SB
        NUMB = Q["ACC"].ts[0]
        DENB = Q["ACC"].ts[1]
        fin, fb, fc, t1, t2, ob = B["fin"], B["fb"], B["fc"], B["t1"], B["t2"], B["ob"]
        self.tt("dve", t1.h[:, :], P["qsA"].h[:, :], P["ksA"].h[:, :], ALU.mult,
                [("qsA", 0), ("qsA", 1), ("ksA", 0), ("ksA", 1)], ["t1"])
        self.rsum(fin.h[:, 0:8], t1.h[:, :].rearrange("p (h e) -> p h e", e=64), ["t1"], ["finA"])
        self.act(fin.h[:, 8:16], fin.h[:, 0:8], AF.Exp, ["finA"], ["finA"])
        self.stt("dve", fin.h[:, 16:24], fin.h[:, 8:16], 3.0, P["e0"].h[:, 0:8], ALU.mult, ALU.mult, ["finA", "e0"], ["finA"])
        self.tt("dve", t1.h[:, :].rearrange("p (h e) -> p h e", e=64), P["vsA"].h[:, :].rearrange("p (h e) -> p h e", e=64),
                fin.ap(0, 16, 16, [[1, 8], [0, 64]]), ALU.mult, ["finA", ("vsA", 0), ("vsA", 1), "t1"], ["t1"])
        self.tt("dve", t1.h[:, :], t1.h[:, :], NUMB.h[0:16, :], ALU.add, ["t1", "numA", "numB", "numC", "den"], ["t1"])
        self.tt("dve", fin.h[:, 24:32], fin.h[:, 16:24], DENB.h[0:16, 0:8], ALU.add, ["finA", "den"], ["finA"])
        self.recip(fin.h[:, 32:40], fin.h[:, 24:32], ["finA"], ["finA"])
        self.tt("dve", ob.h[:, 0:512].rearrange("p (h e) -> p h e", e=64), t1.h[:, :].rearrange("p (h e) -> p h e", e=64),
                fin.ap(0, 16, 32, [[1, 8], [0, 64]]), ALU.mult, ["t1", "finA"], ["obA"])
        self.tt("dve", t2.h[:, :].rearrange("p (g r e) -> p g r e", r=4, e=64),
                P["qsB"].h[:, :].rearrange("p (g r e) -> p g r e", r=4, e=64),
                P["ksB"].ap(0, 16, 0, [[64, 2], [0, 4], [1, 64]]), ALU.mult, ["qsB", "ksB"], ["t2"])
        self.rsum(fb.h[:, 0:8], t2.h[:, :].rearrange("p (h e) -> p h e", e=64), ["t2"], ["finB"])
        self.act(fb.h[:, 8:16], fb.h[:, 0:8], AF.Exp, ["finB"], ["finB"])
        self.tt("dve", fb.h[:, 16:24], fb.h[:, 8:16], P["e0"].h[:, 8:16], ALU.mult, ["finB", "e0"], ["finB"])
        self.tt("dve", t2.h[:, :].rearrange("p (g r e) -> p g r e", r=4, e=64),
                P["vsB"].ap(0, 16, 0, [[64, 2], [0, 4], [1, 64]]),
                fb.ap(0, 16, 16, [[4, 2], [1, 4], [0, 64]]), ALU.mult, ["finB", "vsB", "t2"], ["t2"])
        self.tt("dve", t2.h[:, :], t2.h[:, :], NUMB.h[32:48, :], ALU.add, ["t2", "numA", "numB", "numC", "den"], ["t2"])
        self.tt("dve", fb.h[:, 24:32], fb.h[:, 16:24], DENB.h[0:16, 8:16], ALU.add, ["finB", "den"], ["finB"])
        self.tt("dve", fb.h[:, 32:40], fb.h[:, 24:32], P["esink"].h[:, :], ALU.add, ["finB", "esink"], ["finB"])
        self.recip(fb.h[:, 40:48], fb.h[:, 32:40], ["finB"], ["finB"])
        self.tt("dve", ob.h[:, 512:1024].rearrange("p (h e) -> p h e", e=64), t2.h[:, :].rearrange("p (h e) -> p h e", e=64),
                fb.ap(0, 16, 40, [[1, 8], [0, 64]]), ALU.mult, ["t2", "finB"], ["obB"])
        self.recip(fc.h[:, 0:4], DENB.h[0:16, 16:20], ["den"], ["finC"])
        self.tt("dve", ob.h[:, 1024:1536].rearrange("p (h e) -> p h e", e=128), fc.ap(0, 16, 0, [[1, 4], [0, 128]]),
                NUMB.h[64:80, :].rearrange("p (h e) -> p h e", e=128), ALU.mult, ["numA", "numB", "numC", "den", "finC"], ["obC"])
        for j in range(12):
            self.tr(Q["TRb"].h[:, 16 * j:16 * j + 16], ob.h[0:16, 128 * j:128 * j + 128], P["identb"].h[0:16, 0:16],
                    ["obA", "obB", "obC", "identb"], ["TRb"])
        self.acopy(P["yT"].h[:, :, 2048:2064], Q["TRb"].h[:, 0:192].rearrange("p (j t) -> p j t", t=16), ["TRb"],
                   [("yT", "samp")])

    def yT_rd(self, j, blk):
        return [("yT", j, blk, 0), ("yT", j, blk, 1)]

    def phase_tail_all(self):
        with ExitStack() as stM:
            self.mT = self.sb(stM, [128, 8, NT], BF16, "mT")
            with ExitStack() as stS:
                self.sample_alloc(stS, stM)
                gen = self.sample_tiles()
                self.phase_z(gen)
                self.phase_gate(gen)
            self.phase_tail_out()

    def z_stages(self, W, wtok, mi, c, j, bi, col0, n, Zr, sz, uz, rb, tz):
        P, D, B = self.P, self.D, self.SB
        X = {}

        def SA():
            pj, pjt = Zr.next()
            for kc in range(8):
                self.mm(pj.h[:, 0:n], W.h[:, kc, 128 * c:128 * c + 128], P["hT"].h[:, kc, col0:col0 + n],
                        kc == 0, kc == 7, self.wrd(wtok, kc) + [("hT", col0 + 128 * i) for i in range(max(1, n // 128))], [pjt])
            s_, stok = sz.next()
            self.act(s_.h[:, 0:n], pj.h[:, 0:n], AF.Tanh, [pjt], [stok], scale=0.5)
            if bi == 4:
                self.stt("dve", B["szS"].h[:, j, :], s_.h[:, 0:16], 1.0, pj.h[:, 0:16], ALU.add, ALU.mult, [stok, pjt], [("szS", j)])
                return
            u_, utok = uz.next()
            X["u"] = (u_, utok)
            self.stt("dve", u_.h[:, 0:n], s_.h[:, 0:n], 1.0, pj.h[:, 0:n], ALU.add, ALU.mult, [stok, pjt], [utok])
            r_, rtok = rb.next()
            if mi < 2:
                for hp in range(2):
                    hg = 2 * j + hp
                    self.dma(r_.h[64 * hp:64 * hp + 64, :], bass.AP(D["rD"], hg * 2048 + col0, [[0, 64], [1, 512]]),
                             [("rD", hg, bi)], [rtok + (hp,)])
                X["r"] = (r_, [rtok + (0,), rtok + (1,)])
            else:
                hg = 16 + c
                self.dma(r_.h[:, :], bass.AP(D["rD"], hg * 2048 + col0, [[0, 128], [1, 512]]), [("rD", hg, bi)], [rtok + (0,)])
                X["r"] = (r_, [rtok + (0,)])

        def SB():
            r_, rtoks = X["r"]
            t_, ttok = tz.next()
            X["t"] = (t_, ttok)
            self.tt("pool", t_.h[:, :], P["yT"].h[:, j, col0:col0 + n], r_.h[:, :], ALU.mult, rtoks + self.yT_rd(j, bi), [ttok])

        def SC():
            u_, utok = X["u"]
            t_, ttok = X["t"]
            self.stt("dve", P["yT"].h[:, j, col0:col0 + n], t_.h[:, :], 0.5, u_.h[:, 0:n], ALU.mult, ALU.mult,
                     [utok, ttok] + self.yT_rd(j, bi), [("y", j, bi)])
        return [SA] if bi == 4 else [SA, SB, SC]

    def phase_z(self, gen):
        P, D, Q, B = self.P, self.D, self.Q, self.SB
        blocks = [(512 * i, 512) for i in range(4)] + [(2048, 16)]
        with ExitStack() as st:
            Wz = self.rot(st, 2, [128, 8, 512], BF16, "Wz")
            sz = self.rot(st, 2, [128, 512], F32, "sz")
            uz = self.rot(st, 3, [128, 512], F32, "uz")
            rb = self.rot(st, 3, [128, 512], F32, "rbz")
            tz = self.rot(st, 2, [128, 512], F32, "tz")
            it = 0
            zc = (C_ZA, C_ZB, C_ZC)
            Zr = self.bank_rot(["PJ", "S", "O3"])

            def zload(mi):
                W, wtok = Wz.next()
                self.load_w(W, wtok, "w_in", zc[mi], 512, 0)
                return W, wtok
            znxt = zload(0)
            zpipe = []
            for mi, c0 in enumerate(zc):
                W, wtok = znxt
                if mi + 1 < 3:
                    znxt = zload(mi + 1)
                for c in range(4):
                    j = 4 * mi + c
                    for bi, (col0, n) in enumerate(blocks):
                        zpipe.append(self.z_stages(W, wtok, mi, c, j, bi, col0, n, Zr, sz, uz, rb, tz))
                        for stg in zpipe:
                            if stg:
                                stg.pop(0)()
                        zpipe[:] = [p_ for p_ in zpipe if p_]
                        it += 1
                        if it % 2 == 0:
                            self.pump(gen)
            while zpipe:
                for stg in zpipe:
                    if stg:
                        stg.pop(0)()
                zpipe[:] = [p_ for p_ in zpipe if p_]
            self.S.phase_end()

    def phase_gate(self, gen):
        P, D, Q, B = self.P, self.D, self.Q, self.SB
        blocks = [(512 * i, 512) for i in range(4)] + [(2048, 16)]
        with ExitStack() as st2:
            Wg = self.rot(st2, 2, [128, 8, 384], BF16, "Wg")
            Wb = self.rot(st2, 2, [128, 3, 4, 128], BF16, "WbS")
            sg = self.rot(st2, 3, [128, 512], F32, "sg")
            tmp = self.rot(st2, 3, [128, 512], F32, "tmpg")
            mc = self.rot(st2, 2, [128, 512], F32, "mc")
            it = 0
            def gload(c):
                W, wtok = Wg.next()
                Wr, wrtok = Wb.next()
                for i, nm in enumerate(("w_br_a", "w_br_b", "w_br_c")):
                    self.load_w(W, wtok, "w_in", C_GA + 1024 * i + 128 * c, 128, 128 * i, part=i)
                    self.dma(Wr.h[:, i, :, :], D[nm].ap().rearrange("(kc p) n -> p kc n", p=128)[:, :, 128 * c:128 * c + 128],
                             (), [(wrtok, i)], q="pool")
                return W, wtok, Wr, wrtok
            nxt = gload(0)
            Gr = self.bank_rot(["PJ", "O3"])
            for c in range(8):
                W, wtok, Wr, wrtok = nxt
                if c + 1 < 8:
                    nxt = gload(c + 1)
                for bi, (col0, n) in enumerate(blocks):
                    if bi < 4:
                        m_, mtok = mc.next()
                    for i in range(3):
                        pg, pgt = Gr.next()
                        for kc in range(8):
                            self.mm(pg.h[:, 0:n], W.h[:, kc, 128 * i:128 * i + 128], P["hT"].h[:, kc, col0:col0 + n],
                                    kc == 0, kc == 7, [(wtok, i, kc // 2)] + [("hT", col0 + 128 * k) for k in range(max(1, n // 128))], [pgt])
                        if bi == 4:
                            self.act(B["sgS"].h[:, 3 * c + i, :], pg.h[:, 0:16], AF.Tanh, [pgt], [("sgS", c)], scale=0.5)
                            continue
                        s_, stok = sg.next()
                        self.act(s_.h[:, 0:n], pg.h[:, 0:n], AF.Tanh, [pgt], [stok], scale=0.5)
                        pb, pbt = Q["S"].next()
                        for kc in range(4):
                            self.mm(pb.h[:, 0:n], Wr.h[:, i, kc, :], P["yT"].h[:, 4 * i + kc, col0:col0 + n],
                                    kc == 0, kc == 3, [(wrtok, i), ("y", 4 * i + kc, bi)], [pbt])
                        if i == 0:
                            self.stt("dve", m_.h[:, 0:n], s_.h[:, 0:n], 1.0, pb.h[:, 0:n], ALU.add, ALU.mult, [pbt, stok], [mtok])
                        else:
                            t_, ttok = tmp.next()
                            self.stt("dve", t_.h[:, 0:n], s_.h[:, 0:n], 1.0, pb.h[:, 0:n], ALU.add, ALU.mult, [pbt, stok], [ttok])
                            if i == 1:
                                self.tt("pool", m_.h[:, 0:n], m_.h[:, 0:n], t_.h[:, 0:n], ALU.add, [mtok, ttok], [mtok])
                            else:
                                self.tt("pool", self.mT.h[:, c, col0:col0 + n], m_.h[:, 0:n], t_.h[:, 0:n], ALU.add,
                                        [mtok, ttok], [("mT", c, bi)])
                        it += 1
                        if it % 2 == 0:
                            self.pump(gen)
            self.pump(gen, 1000)
            self.sample_finalize()
            self.S.phase_end()

    def phase_tail_out(self):
        P, D, Q, B = self.P, self.D, self.Q, self.SB
        with ExitStack() as st:
            Wo = self.sb(st, [128, 8, 1024], BF16, "Wo")
            Wbr = [self.sb(st, [128, 4, 1024], BF16, f"WbrF{i}") for i in range(3)]
            prod = self.sb(st, [128, 24, 16], F32, "prodS")
            xt = self.rot(st, 3, [128, 1024], F32, "xto")
            yo = self.rot(st, 3, [128, 1024], F32, "yo")
            wv = D["w_out"].ap().rearrange("(kc p) n -> p kc n", p=128)
            for g in range(4):
                self.dma(Wo.h[:, 2 * g:2 * g + 2, :], wv[:, 2 * g:2 * g + 2, :], (), [("Wo", g)], q="pool")
                self.amul(Wo.h[:, 2 * g:2 * g + 2, :], Wo.h[:, 2 * g:2 * g + 2, :], 0.5, [("Wo", g)], [("Wo", g)])
            for i, nm in enumerate(("w_br_a", "w_br_b", "w_br_c")):
                self.dma(Wbr[i].h[:, :, :], D[nm].ap().rearrange("(kc p) n -> p kc n", p=128), (), [("WbrF", i)], q="pool")

            def out_tile(ti, col0, n, src, dst, r0):
                x_, xtok = xt.next()
                self.dma(x_.h[0:n, :], D[src].ap()[r0:r0 + n, :], (), [xtok])
                bi = 4 if ti == 16 else ti // 4
                y_, ytok = yo.next()
                for half in range(2):
                    pj, pjt = PJr.next()
                    for kc in range(8):
                        self.mm(pj.h[0:n, :], self.mT.h[:, kc, col0:col0 + n], Wo.h[:, kc, 512 * half:512 * half + 512],
                                kc == 0, kc == 7, [("Wo", kc // 2), ("mT", kc, bi)], [pjt])
                    self.tt("dve", y_.h[0:n, 512 * half:512 * half + 512], pj.h[0:n, :], x_.h[0:n, 512 * half:512 * half + 512],
                            ALU.add, [pjt, xtok], [ytok + (half,)])
                self.dma(D[dst].ap()[r0:r0 + n, :], y_.h[0:n, :], [ytok + (0,), ytok + (1,)], [ytok + (0,), ytok + (1,)], q="act")

            PJr = self.bank_rot(["PJ", "ACC", "O3"])
            for ti in range(16):
                out_tile(ti, 128 * ti, 128, "x", "y", 128 * ti)
            self.stt("dve", P["yT"].h[:, :, 2048:2064], P["yT"].h[:, :, 2048:2064], 0.5, B["szS"].h[:, :, :], ALU.mult, ALU.mult,
                     [("yT", "samp")] + [("szS", j) for j in range(12)], [("y", j, 4) for j in range(12)])
            pb, pbt = Q["S"].next()
            first = True
            for c in range(8):
                for i in range(3):
                    col = (3 * c + i) * 16
                    for kc in range(4):
                        self.mm(pb.h[:, col:col + 16], Wbr[i].h[:, kc, 128 * c:128 * c + 128], P["yT"].h[:, 4 * i + kc, 2048:2064],
                                first, False, [("WbrF", i), ("y", 4 * i + kc, 4)], [pbt], sgc=True)
                        first = False
            self.stt("dve", prod.h[:, :, :], B["sgS"].h[:, :, :], 1.0, pb.h[:, 0:384].rearrange("p (a t) -> p a t", t=16),
                     ALU.add, ALU.mult, [pbt] + [("sgS", c) for c in range(8)], ["prodS"])
            pv = prod.h[:, :, :].rearrange("p (c i) t -> p c i t", i=3)
            self.tt("dve", pv[:, :, 0, :], pv[:, :, 0, :], pv[:, :, 1, :], ALU.add, ["prodS"], ["prodS"])
            self.tt("dve", self.mT.h[:, :, 2048:2064], pv[:, :, 0, :], pv[:, :, 2, :], ALU.add, ["prodS"],
                    [("mT", c, 4) for c in range(8)])
            out_tile(16, 2048, 16, "xs", "ys", 0)
            self.S.phase_end()


def _bucket_np(dist):
    n = np.maximum(dist, 0)
    nf = np.maximum(n, 1).astype(np.float32)
    v = np.log(nf / np.float32(16)) / np.float32(math.log(2048 / 16)) * np.float32(16)
    large = 16 + v.astype(np.int32)
    return np.where(n < 16, n, np.minimum(large, 31))


def _static_tables():
    pats = [(1, 128), (4, 128), (16, 128), (1, 127)]
    ohp = np.zeros((32, 4, 384), np.float32)
    ohs = np.zeros((32, 4, 128), np.float32)
    for p, (d, mx) in enumerate(pats):
        for x in range(383):
            delta = x - 127
            if 0 <= delta <= mx:
                ohp[_bucket_np(np.array(delta * d))[()], p, x] = 1.0
        for rho in range(128):
            steps = 128 - rho
            if steps <= mx:
                ohs[_bucket_np(np.array(steps * d))[()], p, rho] = 1.0
    masks = np.zeros((128, 1024), np.float32)
    for p in range(128):
        masks[p, 8 * (p % 64):8 * (p % 64) + 8] = 1.0
        masks[p, 512 + 4 * p:512 + 4 * p + 4] = 1.0
    return ohp.reshape(32, 4 * 384), ohs.reshape(32, 4 * 128), masks


_NC_CACHE = {}


def kernel(x_prompt, x_sample, mem_prompt, cache_a_k, cache_a_v, cache_b_k, cache_b_v, cache_mem_k, cache_mem_v,
           rel_bias, ln_g, w_in, gq_a, gk_a, gq_b, gk_b, gq_c, gk_c, sinks_b, mem_ln_g, w_mem_kv,
           w_br_a, w_br_b, w_br_c, w_out):
    f = lambda a: np.ascontiguousarray(np.asarray(a, dtype=np.float32))
    if "nc" not in _NC_CACHE:
        _NC_CACHE["nc"] = Builder().build()
    nc = _NC_CACHE["nc"]
    ohp, ohs, masks = _static_tables()
    shared = {"masks": masks, "rel_bias": f(rel_bias), "ln_g": f(ln_g), "w_in": f(w_in)[0], "gq_a": f(gq_a), "gk_a": f(gk_a),
              "gq_b": f(gq_b), "gk_b": f(gk_b), "gq_c": f(gq_c), "gk_c": f(gk_c), "sinks": f(sinks_b),
              "mem_ln_g": f(mem_ln_g), "w_mem_kv": f(w_mem_kv)[0], "w_br_a": f(w_br_a)[0], "w_br_b": f(w_br_b)[0],
              "w_br_c": f(w_br_c)[0], "w_out": f(w_out)[0], "ohp": ohp, "ohs": ohs}
    xp, xs, mp = f(x_prompt), f(x_sample), f(mem_prompt)
    cak, cav, cbk, cbv, cmk, cmv = (f(a)[0] for a in (cache_a_k, cache_a_v, cache_b_k, cache_b_v, cache_mem_k, cache_mem_v))
    in_maps = []
    for c in range(8):
        sl = slice(16 * c, 16 * c + 16)
        m = dict(shared)
        m.update({"x": xp[c], "xs": xs[sl, 0], "mem": mp[c],
                  "cak": cak[sl].reshape(16 * 2048, 512), "cav": cav[sl].reshape(16 * 2048, 512),
                  "cbk": cbk[sl].reshape(16 * 128, 128), "cbv": cbv[sl].reshape(16 * 128, 128),
                  "cmk": cmk[sl].reshape(16 * 256, 512), "cmv": cmv[sl].reshape(16 * 256, 512)})
        in_maps.append(m)
    res = run_bass_kernel_spmd(nc, in_maps, core_ids=list(range(8)))
    R = res.results
    cat = lambda k: np.stack([np.asarray(R[c][k], dtype=np.float32) for c in range(8)], 0)
    y = cat("y")
    ys = cat("ys").reshape(128, 1, 1024)
    pak = cat("pak").reshape(1, 8, 2048, 8, 64)
    pav = cat("pav").reshape(1, 8, 2048, 8, 64)
    pbk = cat("pbk").reshape(1, 8, 128, 2, 64)
    pbv = cat("pbv").reshape(1, 8, 128, 2, 64)
    pmk = cat("pmk").reshape(1, 8, 256, 4, 128)
    pmv = cat("pmv").reshape(1, 8, 256, 4, 128)
    sak = cat("sak").reshape(1, 128, 1, 8, 64)
    sav = cat("sav").reshape(1, 128, 1, 8, 64)
    sbk = cat("sbk").reshape(1, 128, 1, 2, 64)
    sbv = cat("sbv").reshape(1, 128, 1, 2, 64)
    return (y, ys, pak, pav, pbk, pbv, pmk, pmv, sak, sav, sbk, sbv)
```
